# Optimizing a Trainium2 kernel written in Bass

```python
import jax
import jax.numpy as jnp
from jax import lax
import numpy as np

D_MODEL = 4096
BATCH = 2
SEQ = 8192
DEPTH = 1

HEAD_DIM = 128
D_GMLP = D_MODEL // 2
D_NA = D_MODEL - D_GMLP
N_GMLP_HEADS = D_GMLP // HEAD_DIM
N_NA_HEADS = D_NA // HEAD_DIM
D_IN = 2 * D_GMLP + 3 * D_NA
CHUNK = 128
GRID_W = 64
WIN_ROWS_MAX = 8
WIN_COLS = 16
D_FF = 11008
CONV_W = 3
D_PLE = 256
EPS = 1e-6

kernel_name = "hybrid_gmlp_natten_convffn_block"


def rms_norm(x, g):
    xf = x.astype(jnp.float32)
    y = xf * lax.rsqrt(jnp.mean(xf * xf, axis=-1, keepdims=True) + EPS)
    return (y * g.astype(jnp.float32)).astype(x.dtype)


def chunked_spatial_gating(u, v, g_v, w_s, b_s):
    B, S, H, hd = v.shape
    n_chunks = S // CHUNK
    vc = rms_norm(v, g_v).reshape(B, n_chunks, CHUNK, H, hd)
    mixed = jnp.einsum('hij,bnjhd->bnihd', w_s, vc) + b_s.T[None, None, :, :, None]
    return u * mixed.reshape(B, S, H, hd)


def neighbourhood_attention(q, k, v, rpb):
    B, S, H, hd = q.shape
    rows = S // GRID_W
    kr = min(WIN_ROWS_MAX, rows)
    qg = q.reshape(B, rows, GRID_W, H, hd)
    kg = k.reshape(B, rows, GRID_W, H, hd)
    vg = v.reshape(B, rows, GRID_W, H, hd)
    col = jnp.arange(GRID_W)
    col_start = jnp.clip(col - WIN_COLS // 2, 0, GRID_W - WIN_COLS)
    col_idx = col_start[:, None] + jnp.arange(WIN_COLS)[None, :]
    dc = col_idx - col[:, None] + (WIN_COLS - 1)
    rpb_col = rpb[:, :, dc]
    scale = HEAD_DIM ** -0.5

    def one_row(r):
        rs = jnp.clip(r - kr // 2, 0, rows - kr)
        q_r = lax.dynamic_index_in_dim(qg, r, axis=1, keepdims=False)
        k_blk = lax.dynamic_slice_in_dim(kg, rs, kr, axis=1)
        v_blk = lax.dynamic_slice_in_dim(vg, rs, kr, axis=1)
        k_nb = k_blk[:, :, col_idx]
        v_nb = v_blk[:, :, col_idx]
        dr = rs + jnp.arange(kr) - r + (WIN_ROWS_MAX - 1)
        bias = jnp.transpose(rpb_col[:, dr], (0, 2, 1, 3))
        s = jnp.einsum('bqhd,brqchd->bhqrc', q_r, k_nb,
                       preferred_element_type=jnp.float32) * scale
        s = s + bias[None].astype(jnp.float32)
        pr = jax.nn.softmax(s.reshape(B, H, GRID_W, kr * WIN_COLS), axis=-1)
        pr = pr.reshape(B, H, GRID_W, kr, WIN_COLS).astype(v.dtype)
        return jnp.einsum('bhqrc,brqchd->bqhd', pr, v_nb)

    out = lax.map(one_row, jnp.arange(rows))
    return jnp.moveaxis(out, 0, 1).reshape(B, S, H, hd)


def conv_ffn(xn, w_up, conv_w, conv_b, w_down):
    S = xn.shape[1]
    up = xn @ w_up
    half = CONV_W // 2
    padded = jnp.pad(up, ((0, 0), (half, half), (0, 0)))
    c = conv_b + sum(padded[:, j:j + S] * conv_w[j] for j in range(CONV_W))
    gate, val = jnp.split(c, 2, axis=-1)
    return (jax.nn.gelu(gate, approximate=False) * val) @ w_down


def setup_inputs(seed: int = 0) -> dict:
    key = jax.random.key(seed)
    ks = jax.random.split(key, 22)
    f32 = jnp.float32
    L = DEPTH

    def nrm(k, shape, scale):
        return jax.random.normal(k, shape, f32) * scale

    def gain(k, shape):
        return 1.0 + 0.05 * jax.random.normal(k, shape, f32)

    return {
        "x": nrm(ks[0], (BATCH, SEQ, D_MODEL), 1.0),
        "p": nrm(ks[1], (DEPTH, BATCH, SEQ, D_PLE), 1.0),
        "norm_mix_g": gain(ks[2], (L, D_MODEL)),
        "w_in": nrm(ks[3], (L, D_MODEL, D_IN), D_MODEL ** -0.5),
        "gmlp_v_g": gain(ks[4], (L, N_GMLP_HEADS, HEAD_DIM)),
        "gmlp_ws": nrm(ks[5], (L, N_GMLP_HEADS, CHUNK, CHUNK), CHUNK ** -0.5),
        "gmlp_bs": 1.0 + 0.1 * jax.random.normal(ks[6], (L, N_GMLP_HEADS, CHUNK), f32),
        "q_norm_g": gain(ks[7], (L, HEAD_DIM)),
        "k_norm_g": gain(ks[8], (L, HEAD_DIM)),
        "na_rpb": nrm(ks[9], (L, N_NA_HEADS, 2 * WIN_ROWS_MAX - 1, 2 * WIN_COLS - 1), 0.2),
        "out_norm_a_g": gain(ks[10], (L, N_GMLP_HEADS, HEAD_DIM)),
        "out_norm_b_g": gain(ks[11], (L, N_NA_HEADS, HEAD_DIM)),
        "w_out": nrm(ks[12], (L, D_MODEL, D_MODEL), D_MODEL ** -0.5),
        "norm_ffn_g": gain(ks[13], (L, D_MODEL)),
        "w_up": nrm(ks[14], (L, D_MODEL, 2 * D_FF), D_MODEL ** -0.5),
        "conv_w": nrm(ks[15], (L, CONV_W, 2 * D_FF), CONV_W ** -0.5),
        "conv_b": nrm(ks[16], (L, 2 * D_FF), 0.01),
        "w_down": nrm(ks[17], (L, D_FF, D_MODEL), D_FF ** -0.5),
        "norm_ple_g": gain(ks[18], (L, D_MODEL)),
        "w_ple_gate": nrm(ks[19], (L, D_MODEL, D_MODEL), D_MODEL ** -0.5),
        "w_ple_proj": nrm(ks[20], (L, D_PLE, D_MODEL), D_PLE ** -0.5),
        "ple_post_g": gain(ks[21], (L, D_MODEL)),
    }


def reference(x, p, norm_mix_g, w_in, gmlp_v_g, gmlp_ws, gmlp_bs, q_norm_g, k_norm_g,
              na_rpb, out_norm_a_g, out_norm_b_g, w_out, norm_ffn_g, w_up, conv_w,
              conv_b, w_down, norm_ple_g, w_ple_gate, w_ple_proj, ple_post_g):
    B, S, _ = x.shape
    h = x
    for i in range(DEPTH):
        hn = rms_norm(h, norm_mix_g[i])
        z = jnp.einsum('bsd,de->bse', hn, w_in[i])
        zg = jax.nn.gelu(z[..., :2 * D_GMLP], approximate=False)
        u = zg[..., :D_GMLP].reshape(B, S, N_GMLP_HEADS, HEAD_DIM)
        vs = zg[..., D_GMLP:].reshape(B, S, N_GMLP_HEADS, HEAD_DIM)
        q, k, v = jnp.split(z[..., 2 * D_GMLP:], 3, axis=-1)
        q = rms_norm(q.reshape(B, S, N_NA_HEADS, HEAD_DIM), q_norm_g[i])
        k = rms_norm(k.reshape(B, S, N_NA_HEADS, HEAD_DIM), k_norm_g[i])
        v = v.reshape(B, S, N_NA_HEADS, HEAD_DIM)
        a_out = chunked_spatial_gating(u, vs, gmlp_v_g[i], gmlp_ws[i], gmlp_bs[i])
        b_out = neighbourhood_attention(q, k, v, na_rpb[i])
        mix = jnp.concatenate([
            rms_norm(a_out, out_norm_a_g[i]).reshape(B, S, D_GMLP),
            rms_norm(b_out, out_norm_b_g[i]).reshape(B, S, D_NA)], axis=-1)
        h = h + jnp.einsum('bse,ed->bsd', mix, w_out[i])
        h = h + conv_ffn(rms_norm(h, norm_ffn_g[i]), w_up[i], conv_w[i], conv_b[i], w_down[i])
        gate = jax.nn.sigmoid(jnp.einsum('bsd,de->bse', rms_norm(h, norm_ple_g[i]), w_ple_gate[i]))
        e = rms_norm(jnp.einsum('bsk,kd->bsd', p[i], w_ple_proj[i]), ple_post_g[i])
        h = h + gate * e
    return h
```

```python
import numpy as np
import concourse.bass as bass
import concourse.mybir as mybir
from concourse.bass_utils import run_bass_kernel_spmd

F32 = mybir.dt.float32
BF16 = mybir.dt.bfloat16
AF = mybir.ActivationFunctionType
ALU = mybir.AluOpType
AX = mybir.AxisListType

D = 4096
KC = 32
NH = 16
HD = 128
DIN = 10240
DFF = 11008
NFC = 86
DPLE = 256
EPS = 1e-6
NEXT = 22
NMIX = 18
NTOK = 2048

ENGS = ("pe", "act", "dve", "pool", "sp")
EPOCH = 6000
DMA_RING = {"sp": 16, "pool": 8, "act": 4, "dve": 4, "pe": 4}


class _Op:
    __slots__ = ("eng", "fn", "deps", "dma", "sig", "sem", "cnt", "idx", "chan")


class Sched:
    def __init__(self):
        self.ops = []
        self.streams = {e: [] for e in ENGS}
        self.lastw = {}
        self.readers = {}
        self.chans = {}
        self.dma_last = {}
        self.bar = set()

    def add(self, eng, fn, reads=(), writes=(), dma=False, chan=None):
        op = _Op()
        op.eng = eng
        op.fn = fn
        op.dma = dma
        op.idx = len(self.ops)
        op.sig = False
        op.chan = chan
        deps = set()
        for k in tuple(reads) + tuple(writes):
            w = self.lastw.get(k)
            if w is not None:
                deps.add(w)
        for k in writes:
            r = self.readers.get(k)
            if r:
                deps.update(r.values())
        deps.discard(op.idx)
        deps |= self.bar
        op.deps = deps
        if dma:
            R = DMA_RING[eng]
            st = self.chans.setdefault(eng, [0])
            i = st[0]
            st[0] += 1
            op.sem = (eng, i % R)
            op.cnt = 16 * (i // R + 1)
            prev = self.dma_last.get(op.sem)
            if prev is not None:
                deps.add(prev)
            self.dma_last[op.sem] = op.idx
        for k in reads:
            r = self.readers.setdefault(k, {})
            rk = ("dma", op.idx) if dma else eng
            r[rk] = op.idx
        for k in writes:
            self.lastw[k] = op.idx
            self.readers[k] = {}
        self.ops.append(op)
        self.streams[eng].append(op)
        return op

    def barrier(self):
        b = set(self.dma_last.values())
        for e in ENGS:
            for o in reversed(self.streams[e]):
                if not o.dma:
                    b.add(o.idx)
                    break
        self.bar = b

    def emit(self, nc, stack):
        ops = self.ops
        for op in ops:
            for d in op.deps:
                dop = ops[d]
                if dop.dma:
                    continue
                if dop.eng == "pe" and op.eng == "pe" and not op.dma:
                    continue
                dop.sig = True
        eng_sems = {}
        for e in ENGS:
            n = sum(1 for o in self.streams[e] if o.sig and not o.dma)
            k = n // EPOCH + 1
            eng_sems[e] = [stack.enter_context(nc.semaphore(f"s_{e}_{i}")) for i in range(k)]
            c = 0
            for o in self.streams[e]:
                if o.sig and not o.dma:
                    o.sem = eng_sems[e][c // EPOCH]
                    o.cnt = c % EPOCH + 1
                    c += 1
        dsems = {}
        for o in ops:
            if o.dma:
                if o.sem not in dsems:
                    dsems[o.sem] = stack.enter_context(nc.semaphore(f"d_{o.sem[0]}_{o.sem[1]}"))
                o.sem = dsems[o.sem]
        block = stack.enter_context(nc.Block())

        def run_stream(e, engobj):
            waited = {}
            for o in self.streams[e]:
                for d in sorted(o.deps):
                    dop = ops[d]
                    if not dop.dma and dop.eng == "pe" and e == "pe" and not o.dma:
                        continue
                    key = id(dop.sem)
                    if waited.get(key, 0) >= dop.cnt:
                        continue
                    waited[key] = dop.cnt
                    engobj.wait_ge(dop.sem, dop.cnt)
                ins = o.fn(engobj)
                if o.dma:
                    ins.then_inc(o.sem, 16)
                elif o.sig:
                    ins.then_inc(o.sem, 1)
            last = {}
            for o in self.streams[e]:
                if o.dma:
                    last[id(o.sem)] = (o.sem, max(o.cnt, last.get(id(o.sem), (None, 0))[1]))
            for sem, cnt in last.values():
                engobj.wait_ge(sem, cnt)

        @block.tensor
        def _(eng):
            run_stream("pe", eng)

        @block.scalar
        def _(eng):
            run_stream("act", eng)

        @block.vector
        def _(eng):
            run_stream("dve", eng)

        @block.gpsimd
        def _(eng):
            run_stream("pool", eng)

        @block.sync
        def _(eng):
            run_stream("sp", eng)


class Builder:
    def __init__(self, nc, stack):
        self.nc = nc
        self.stack = stack
        self.s = Sched()
        self.uid = 0
        self.psum_rr = 0

    def sb(self, name, shape, dt):
        return self.stack.enter_context(self.nc.sbuf_tensor(name, list(shape), dt))

    def ps(self, name, shape, dt):
        return self.stack.enter_context(self.nc.psum_tensor(name, list(shape), dt))

    def dram(self, name, shape, dt, kind):
        return self.nc.dram_tensor(name, list(shape), dt, kind=kind)

    def dma(self, q, out, in_, r, w, chan):
        self.s.add(q, lambda e, o=out, i=in_: e.dma_start(out=o, in_=i), r, w, dma=True, chan=chan)

    def mm(self, out, lhsT, rhs, start, stop, r, w):
        self.s.add("pe", lambda e, o=out, l=lhsT, x=rhs, a=start, b=stop: e.matmul(o, l, x, start=a, stop=b), r, w)

    def tr(self, out, in_, ident, r, w):
        self.s.add("pe", lambda e, o=out, i=in_, d=ident: e.transpose(o, i, d), r, w)

    def act(self, out, in_, func, r, w, bias=None, scale=None, accum=None, eng="act"):
        def fn(e, o=out, i=in_, f=func, b=bias, sc=scale, ac=accum):
            kw = {}
            if b is not None:
                kw["bias"] = b
            if sc is not None:
                kw["scale"] = sc
            if ac is not None:
                kw["accum_out"] = ac
            return e.activation(o, i, f, **kw)
        self.s.add(eng, fn, r, w)

    def tt(self, eng, out, in0, in1, op, r, w):
        self.s.add(eng, lambda e, o=out, a=in0, b=in1, p=op: e.tensor_tensor(o, a, b, p), r, w)

    def ts(self, eng, out, in0, s1, s2, op0, op1, r, w, accum=None):
        def fn(e, o=out, a=in0, x=s1, y=s2, p=op0, q=op1, ac=accum):
            if q is None:
                return e.tensor_scalar(o, a, x, None, p)
            if ac is not None:
                return e.tensor_scalar(o, a, x, y, p, q, ac)
            return e.tensor_scalar(o, a, x, y, p, q)
        self.s.add(eng, fn, r, w)

    def stt(self, eng, out, in0, scalar, in1, op0, op1, r, w):
        self.s.add(eng, lambda e, o=out, a=in0, sc=scalar, b=in1, p=op0, q=op1:
                   e.scalar_tensor_tensor(o, a, sc, b, p, q), r, w)

    def red(self, eng, out, in_, op, r, w):
        self.s.add(eng, lambda e, o=out, i=in_, p=op: e.tensor_reduce(o, i, AX.X, p), r, w)

    def copy(self, eng, out, in_, r, w):
        if eng == "act":
            self.s.add(eng, lambda e, o=out, i=in_: e.copy(o, i), r, w)
        else:
            self.s.add(eng, lambda e, o=out, i=in_: e.tensor_copy(o, i), r, w)

    def memset(self, eng, ap, val, w):
        self.s.add(eng, lambda e, a=ap, v=val: e.memset(a, v), (), w)


def wsrc(W, r0, nkc, c0, ncol):
    return W[r0:r0 + 128 * nkc, c0:c0 + ncol].rearrange("(k p) c -> p k c", p=128)


SCALE = float(HD) ** -0.5
NPP = 18 + 172 * 4 + 32
PP_CPAR = 18
PP_MASK = 18 + 172 * 4
MASKVAL = -100.0


class Ctx:
    pass


def cls_of(j):
    return {1: 0, 2: 1, 15: 2, 16: 3}.get(j, 4)


def rstd_from_ms(B, ms_ap, tmp_ap, out_ap, keys):
    B.act(tmp_ap, ms_ap, AF.Sqrt, keys, keys)
    B.s.add("dve", lambda e, o=out_ap, i=tmp_ap: e.reciprocal(o, i), keys, keys)


def phase_norm_T(B, C, name, src_tile_ap, src_key, ntiles, dst_fn):
    xin, ss = C.n_xin, C.n_ss
    B.dma("sp", xin[0], src_tile_ap(0), [src_key(0)], [("n_xin", 0)], "ld" + name)
    for t in range(ntiles):
        b = t % 2
        xbf = C.n_xbf[b]
        kx = ("n_xin", b)
        if t + 1 < ntiles:
            B.dma("sp", xin[1 - b], src_tile_ap(t + 1), [src_key(t + 1)], [("n_xin", 1 - b)], "ld" + name)
        ksq = ("n_xbf", b)
        kss = ("n_ss", b)
        B.act(xbf, xin[b], AF.Square, [kx], [ksq, kss], accum=ss[b][:, 0:1])
        B.ts("dve", ss[b][:, 1:2], ss[b][:, 0:1], 1.0 / D, EPS, ALU.mult, ALU.add, [kss], [kss])
        rstd_from_ms(B, ss[b][:, 1:2], ss[b][:, 3:4], ss[b][:, 2:3], [kss])
        for hlf in range(2):
            sl = slice(hlf * 2048, (hlf + 1) * 2048)
            B.stt("dve", xbf[:, sl], xin[b][:, sl], ss[b][:, 2:3], C.gtab[:, sl], ALU.mult, ALU.mult,
                  [kx, kss, C.gtab_key], [ksq])
        for g in range(4):
            pi = (t * 4 + g) % 2
            pb = C.psb[pi]
            kp = ("psb", pi)
            for i in range(8):
                kc = g * 8 + i
                B.tr(pb[:, i * 128:(i + 1) * 128], xbf[:, kc * 128:(kc + 1) * 128], C.ident[:, :],
                     [ksq, "ident"], [kp])
            dst, kd = dst_fn(t, g)
            eng = "act" if g % 2 == 0 else "dve"
            B.copy(eng, dst, pb[:, :].rearrange("p (k t) -> p k t", t=128), [kp], [kd])
        if C.after_tile is not None:
            C.after_tile(t)


def fm_norm_epilogue(B, C, bank_ap, kbank, N, gcol, out_ap, kout, idx, gkey="pp"):
    b = idx % 2
    sqf, rs = C.sqb[b], C.rs[b]
    ks, kr = ("sqf", b), ("rs", b)
    B.act(sqf[:, 0:N], bank_ap, AF.Square, [kbank], [ks])
    ob = C.pf[5]
    B.mm(ob[:, 0:N], C.ones[:, :], sqf[:, 0:N], True, True, [ks, "ones"], [("pf", 5)])
    B.ts("dve", rs[:, 0:N], ob[:, 0:N], 1.0 / HD, EPS, ALU.mult, ALU.add, [("pf", 5)], [kr])
    B.act(rs[:, 0:N], rs[:, 0:N], AF.Sqrt, [kr], [kr])
    B.s.add("dve", lambda e, o=rs[:, 0:N], i=rs[:, 0:N]: e.reciprocal(o, i), [kr], [kr])
    B.stt("dve", out_ap, bank_ap, gcol, rs[:, 0:N], ALU.mult, ALU.mult, [kbank, kr, gkey], [kout])


def build(nc, stack, cfg):
    B = Builder(nc, stack)
    C = Ctx()
    dbg = cfg.get("debug", False)
    phases = cfg.get("phases", (0, 1, 2, 25, 3, 4))
    skind = "ExternalOutput" if dbg else "Internal"

    x_ext = B.dram("x_ext", [NEXT * 128, D], F32, "ExternalInput")
    p_own = B.dram("p_own", [NTOK, DPLE], F32, "ExternalInput")
    gt_d = {n: B.dram(n, [128, D], F32, "ExternalInput") for n in ("g_mix", "g_ffn", "g_ple", "g_post")}
    gh_d = {n: B.dram(n, [128, 2048], F32, "ExternalInput") for n in ("g_v", "g_a", "g_b")}
    pp_d = B.dram("pp", [128, NPP], F32, "ExternalInput")
    wsT_d = B.dram("wsT", [128, NH, 128], F32, "ExternalInput")
    btab_d = B.dram("btab", [5, NH, 128, 5, 128], F32, "ExternalInput")
    ident_d = B.dram("ident", [128, 128], F32, "ExternalInput")
    w_in = B.dram("w_in", [D, DIN], F32, "ExternalInput")
    w_out = B.dram("w_out", [D, D], F32, "ExternalInput")
    w_up = B.dram("w_up", [D, 2 * DFF], F32, "ExternalInput")
    w_down = B.dram("w_down", [DFF, D], F32, "ExternalInput")
    w_gate = B.dram("w_gate", [D, D], F32, "ExternalInput")
    w_p = B.dram("w_p", [DPLE, D], F32, "ExternalInput")
    hnT_s = B.dram("hnT_s", [NEXT, 128, KC, 128], BF16, skind)
    kT_s = B.dram("kT_s", [NH, 128, NEXT * 128], BF16, skind)
    v_s = B.dram("v_s", [NEXT, 128, NH, 129], BF16, skind)
    h1_s = B.dram("h1_s", [NMIX * 128, D], F32, skind)
    hn2T_s = B.dram("hn2T_s", [NMIX, 128, KC, 128], BF16, skind)
    h2_s = B.dram("h2_s", [NTOK, D], F32, skind)
    out_d = B.dram("out", [NTOK, D], F32, "ExternalOutput")

    biga = B.sb("biga", [128, 4, KC, 128], BF16)
    gt_h = B.sb("gt", [128, NFC * 512], BF16)
    gt_f = gt_h.bitcast(F32)
    wp = [B.sb(f"wp{i}", [128, 8, 512], BF16) for i in range(4)]
    C.gtab = B.sb("gtab", [128, D], F32)
    sm_h = B.sb("small", [128, 8192], F32)
    sm_b = sm_h.bitcast(BF16)
    C.ident = B.sb("ident_sb", [128, 128], BF16)
    C.ones = B.sb("ones_sb", [128, 128], BF16)
    pp = B.sb("pp_sb", [128, NPP], F32)
    C.pf = [B.ps(f"pf{i}", [128, 512], F32) for i in range(6)]
    C.psb = [B.ps(f"psb{i}", [128, 1024], BF16) for i in range(2)]
    C.after_tile = None
    C.gtab_key = "gtab"

    C.n_xin = [gt_f[:, 0:4096], gt_f[:, 4096:8192]]
    C.n_xbf = [gt_h[:, 16384:20480], gt_h[:, 20480:24576]]
    n_stg = [gt_h[:, 32768:36864].rearrange("p (k t) -> p k t", t=128),
             gt_h[:, 36864:40960].rearrange("p (k t) -> p k t", t=128)]
    C.n_ss = [sm_h[:, 8000:8004], sm_h[:, 8004:8008]]

    state = {"bank": 0, "wp": 0, "nwp": 4}
    _gb = C.gtab.bitcast(BF16)
    wpx = [_gb[:, 0:4096].rearrange("p (k c) -> p k c", c=512), _gb[:, 4096:8192].rearrange("p (k c) -> p k c", c=512)]

    def bank():
        i = state["bank"] % 5
        state["bank"] += 1
        return C.pf[i], ("pf", i)

    def wpnext():
        i = state["wp"] % state["nwp"]
        state["wp"] += 1
        if i >= 4:
            return wpx[i - 4], ("gtabq", i - 4)
        return wp[i], ("wp", i)

    B.dma("pool", C.ident[:, :], ident_d.ap(), [], ["ident"], "c")
    B.dma("sp", pp[:, :], pp_d.ap(), [], ["pp"], "c")
    B.memset("dve", C.ones[:, :], 1.0, ["ones"])

    if 0 in phases:
        gtab_keep = C.gtab
        C.gtab = gt_f[:, 12288:16384]
        C.gtab_key = "gtab0"
        B.dma("sp", C.gtab, gt_d["g_mix"].ap(), [], ["gtab0"], "c")
        nt0 = cfg.get("nt0", NEXT)

        def after0(t):
            B.dma("act", hnT_s[t], n_stg[t % 2], [("n_stg", t % 2)], [("hnT_s", t)], "st0")
        C.after_tile = after0
        phase_norm_T(B, C, "p0", lambda t: x_ext[t * 128:(t + 1) * 128, :], lambda t: ("x_ext", t), nt0,
                     lambda t, g: (n_stg[t % 2][:, g * 8:(g + 1) * 8, :], ("n_stg", t % 2)))
        C.after_tile = None
        C.gtab = gtab_keep
        C.gtab_key = "gtab"

    if 1 in phases:
        state["nwp"] = 6
        sqb4 = [sm_b[:, i * 512:(i + 1) * 512] for i in range(4)]
        rs4 = [sm_h[:, 1024 + i * 512:1024 + (i + 1) * 512] for i in range(4)]
        kraw = [sm_h[:, 3072 + i * 512:3072 + (i + 1) * 512] for i in range(4)]
        kst = [sm_b[:, 10240 + i * 512:10240 + (i + 1) * 512] for i in range(4)]
        vst = [sm_b[:, 12288 + i * 516:12288 + (i + 1) * 516].rearrange("p (h d) -> p h d", d=129) for i in range(2)]
        for i in range(2):
            B.memset("dve", vst[i][:, :, 128:129], 1.0, [("vst", i)])
        wins = cfg.get("kv_wins", [(0, 4), (4, 4), (8, 4), (12, 4), (16, 4), (20, 2)])
        ecnt = 0
        pend1 = []
        tk1 = {"n": 0}

        def tick1():
            tk1["n"] += 1
            due = sorted([p for p in pend1 if p[0] <= tk1["n"]], key=lambda p: p[0])
            for p in due:
                pend1.remove(p)
                p[1]()

        def k_epi(q, h, e0, W, N):
            ob, kob = C.pf[5], ("pf", 5)
            B.mm(ob[:, 0:N], C.ones[:, :], sqb4[q][:, 0:N], True, True, [("sqb4", q), "ones"], [kob])
            B.ts("dve", rs4[q][:, 0:N], ob[:, 0:N], 1.0 / HD, EPS, ALU.mult, ALU.add, [kob], [("rs4", q)])
            B.act(rs4[q][:, 0:N], rs4[q][:, 0:N], AF.Sqrt, [("rs4", q)], [("rs4", q)])
            B.s.add("dve", lambda e, o=rs4[q][:, 0:N], i=rs4[q][:, 0:N]: e.reciprocal(o, i), [("rs4", q)], [("rs4", q)])
            B.stt("dve", kst[q][:, 0:N], kraw[q][:, 0:N], pp[:, 1:2], rs4[q][:, 0:N], ALU.mult, ALU.mult,
                  [("kraw", q), ("rs4", q), "pp"], [("kst", q)])
            B.dma("sp", kT_s[h, :, e0 * 128:e0 * 128 + N], kst[q][:, 0:N], [("kst", q)],
                  [("kT_s", h, e0 + i) for i in range(W)], "st1")

        for (e0, W) in wins:
            N = W * 128
            for i in range(W):
                B.dma("sp", biga[:, i], hnT_s[e0 + i], [("hnT_s", e0 + i)], [("biga", i)], "ld1")
            hk = [("biga", i) for i in range(W)]
            for part in ("k", "v"):
                for cg in range(cfg.get("kv_cgs", 4)):
                    c0 = (6144 if part == "k" else 8192) + cg * 512
                    nb = 4 if part == "k" else W
                    bks = [bank() for _ in range(nb)]
                    for half in range(4):
                        pc, kpc = wpnext()
                        B.dma("pool", pc[:, :, :], wsrc(w_in, half * 1024, 8, c0, 512), [], [kpc], "w1")
                        for q in range(nb):
                            for kc in range(8):
                                st_, sp_ = (half == 0 and kc == 0), (half == 3 and kc == 7)
                                if part == "k":
                                    B.mm(bks[q][0][:, 0:N].rearrange("p (w t) -> p w t", t=128),
                                         pc[:, kc, q * 128:(q + 1) * 128], biga[:, 0:W, half * 8 + kc, :],
                                         st_, sp_, [kpc] + hk, [bks[q][1]])
                                else:
                                    B.mm(bks[q][0][:, :], biga[:, q, half * 8 + kc, :], pc[:, kc, :],
                                         st_, sp_, [kpc, hk[q]], [bks[q][1]])
                            if half == 3:
                                bk, kb = bks[q]
                                if part == "k":
                                    B.act(sqb4[q][:, 0:N], bk[:, 0:N], AF.Square, [kb], [("sqb4", q)])
                                    B.copy("act", kraw[q][:, 0:N], bk[:, 0:N], [kb], [("kraw", q)])
                                    pend1.append([tk1["n"] + 3, lambda q=q, h=cg * 4 + q, e0=e0, W=W, N=N: k_epi(q, h, e0, W, N)])
                                else:
                                    b = ecnt % 2
                                    ecnt += 1
                                    B.copy("act", vst[b][:, :, 0:128], bk[:, :].rearrange("p (h d) -> p h d", d=128),
                                           [kb], [("vst", b)])
                                    B.dma("act", v_s[e0 + q, :, cg * 4:(cg + 1) * 4, :], vst[b], [("vst", b)],
                                          [("v_s", e0 + q)], "st1")
                            tick1()
            for p in sorted(pend1, key=lambda p: p[0]):
                p[1]()
            pend1.clear()
        B.s.barrier()

    if 2 in phases:
        state["nwp"] = 4
        mixT_f = gt_h[:, 0:16384].rearrange("p (k t) -> p k t", t=512)
        qT_f = gt_h[:, 16384:24576].rearrange("p (h t) -> p h t", t=512)
        biga_flat = biga[:, :, :, :].rearrange("p a k t -> p (a k t)")
        kmac = [biga_flat[:, 0:4096].rearrange("p (h t) -> p h t", t=1024),
                gt_h[:, 24576:28672].rearrange("p (h t) -> p h t", t=1024)]
        vmac = [biga_flat[:, 4096:8224].rearrange("p (m f) -> p m f", f=516),
                gt_h[:, 28672:32800].rearrange("p (m f) -> p m f", f=516)]
        kvkeys = [[("biga", i) for i in range(4)], ["kvB"]]
        pTall = [gt_h[:, 32800 + i * 4096:32800 + (i + 1) * 4096].rearrange("p (m q) -> p m q", q=512)
                 for i in range(2)]
        mtok = [gt_h[:, 40992 + i * 512:40992 + (i + 1) * 512].rearrange("p (t d) -> p t d", d=128) for i in range(2)]
        ysm = gt_f[:, 21008:21520].rearrange("p (t d) -> p t d", d=128)
        nst = gt_f[:, 21520:21552]
        C.sqf = [sm_h[:, 0:512], sm_h[:, 512:1024]]
        C.sqb = [sm_b[:, 0:512], sm_b[:, 1024:1536]]
        C.rs = [sm_h[:, 1024:1536], sm_h[:, 1536:2048]]
        gl = [sm_h[:, 2048:2560], sm_h[:, 2560:3072]]
        sq = sm_h[:, 3072:3328]
        st4 = [sm_h[:, 3328:3336], sm_h[:, 3336:3344]]
        ao = [sm_h[:, 3344:3600].rearrange("p (h d) -> p h d", d=128),
              sm_h[:, 3600:3856].rearrange("p (h d) -> p h d", d=128)]
        gvt = sm_h[:, 3856:4112]
        gat = sm_h[:, 4112:4368]
        gbt = C.gtab[:, 0:2048]
        gtab_b = C.gtab.bitcast(BF16)
        bm_h = [gtab_b, sm_b]
        bm_off = [4096, 4368 * 2]
        bm_ps = [gtab_b[:, 0:1].ap[0][0], sm_b[:, 0:1].ap[0][0]]
        bo = 5648 * 2
        vnb = [sm_b[:, bo + i * 256:bo + (i + 1) * 256].rearrange("p (h d) -> p h d", d=128) for i in range(2)]
        mtk = [sm_b[:, bo + 512 + i * 256:bo + 512 + (i + 1) * 256].rearrange("p (h d) -> p h d", d=128)
               for i in range(2)]
        wsT = sm_b[:, 12320:14368].rearrange("p (h i) -> p h i", i=128)
        sqy = sm_h[:, 7184:7696]
        xblk = [C.sqf[0], C.sqf[1], gl[0], gl[1]]
        xkey = [("sqf", 0), ("sqf", 1), ("gl", 0), ("gl", 1)]
        hblk = C.rs
        gqs = sm_h[:, 7700:7701]
        B.ts("dve", gqs, pp[:, 0:1], SCALE, 0.0, ALU.mult, ALU.add, ["pp"], ["gqs"])

        B.dma("pool", wsT, wsT_d.ap(), [], ["wsT"], "c2")
        B.dma("sp", gbt, gh_d["g_b"].ap(), [], ["gtab"], "c2")
        mts = cfg.get("mix_mts", [0, 4, 8, 12, 16])
        cnt = {"e": 0, "g": 0}
        for j0 in mts:
            W = min(4, NMIX - j0)
            N = W * 128
            mixT = mixT_f[:, :, 0:N]
            qT = qT_f[:, :, 0:N]
            for i in range(W):
                B.dma("sp", biga[:, i], hnT_s[j0 + 2 + i], [("hnT_s", j0 + 2 + i)], [("biga", i)], "ld2")
            hk = [("biga", i) for i in range(W)]
            pend = []
            tk = {"n": 0}

            def tick():
                tk["n"] += 1
                due = sorted([p for p in pend if p[0] <= tk["n"]], key=lambda p: p[0])
                for p in due:
                    pend.remove(p)
                    p[1]()

            def flush():
                for p in sorted(pend, key=lambda p: p[0]):
                    p[1]()
                pend.clear()

            def gm_E1(hp, t, bk, kb):
                b = t % 2
                G, kG = gl[b], ("gl", b)
                s4, ks4 = st4[b], ("st4", b)
                B.act(G, bk[:, :], AF.Gelu, [kb], [kG])
                B.tt("dve", sq, G[:, 256:512], G[:, 256:512], ALU.mult, [kG], ["sq"])
                B.red("dve", s4[:, 0:2], sq.rearrange("p (h d) -> p h d", d=128), ALU.add, ["sq"], [ks4])
                B.ts("dve", s4[:, 2:4], s4[:, 0:2], 1.0 / HD, EPS, ALU.mult, ALU.add, [ks4], [ks4])
                rstd_from_ms(B, s4[:, 2:4], s4[:, 4:6], s4[:, 6:8], [ks4])
                for hh in range(2):
                    B.stt("dve", vnb[b][:, hh, :], G[:, 256 + hh * 128:384 + hh * 128], s4[:, 6 + hh:7 + hh],
                          gvt[:, hh * 128:(hh + 1) * 128], ALU.mult, ALU.mult, [kG, ks4, "gvt"], [("vnb", b)])

            def gm_E2(hp, t):
                b = t % 2
                G, kG = gl[b], ("gl", b)
                s4, ks4 = st4[b], ("st4", b)
                mb, kmb = C.pf[5], ("pf", 5)
                for hh in range(2):
                    B.mm(mb[:, hh * 128:(hh + 1) * 128], wsT[:, hp * 2 + hh, :], vnb[b][:, hh, :], True, True,
                         ["wsT", ("vnb", b)], [kmb])
                for hh in range(2):
                    B.stt("dve", ao[b][:, hh, :], mb[:, hh * 128:(hh + 1) * 128], pp[:, 2 + hp * 2 + hh:3 + hp * 2 + hh],
                          G[:, hh * 128:(hh + 1) * 128], ALU.add, ALU.mult, [kmb, kG, "pp"], [("ao", b)])
                B.tt("dve", sq.rearrange("p (h d) -> p h d", d=128), ao[b], ao[b], ALU.mult, [("ao", b)], ["sq"])
                B.red("dve", s4[:, 0:2], sq.rearrange("p (h d) -> p h d", d=128), ALU.add, ["sq"], [ks4])
                B.ts("dve", s4[:, 2:4], s4[:, 0:2], 1.0 / HD, EPS, ALU.mult, ALU.add, [ks4], [ks4])
                rstd_from_ms(B, s4[:, 2:4], s4[:, 4:6], s4[:, 6:8], [ks4])
                for hh in range(2):
                    B.stt("dve", mtk[b][:, hh, :], ao[b][:, hh, :], s4[:, 6 + hh:7 + hh],
                          gat[:, hh * 128:(hh + 1) * 128], ALU.mult, ALU.mult, [("ao", b), ks4, "gat"], [("mtk", b)])

            def gm_E3(hp, t):
                b = t % 2
                pi = t % 2
                for hh in range(2):
                    B.tr(C.psb[pi][:, hh * 128:(hh + 1) * 128], mtk[b][:, hh, :], C.ident[:, :],
                         [("mtk", b), "ident"], [("psb", pi)])
                B.copy("act", mixT[:, hp * 2:hp * 2 + 2, t * 128:(t + 1) * 128],
                       C.psb[pi][:, 0:256].rearrange("p (k t) -> p k t", t=128), [("psb", pi)], ["mixT"])

            def ld_gv(hp):
                B.dma("sp", gvt, gh_d["g_v"][:, hp * 256:(hp + 1) * 256], [], ["gvt"], "ld2")

            def ld_ga(hp):
                B.dma("sp", gat, gh_d["g_a"][:, hp * 256:(hp + 1) * 256], [], ["gat"], "ld2")

            nhp = cfg.get("n_hp", 8)
            for hp in range(nhp):
                if hp == 0:
                    ld_gv(0)
                    ld_ga(0)
                bks = [(C.pf[t], ("pf", t)) for t in range(W)]
                for half in range(4):
                    pc, kpc = wpnext()
                    B.dma("pool", pc[:, :, 0:256], wsrc(w_in, half * 1024, 8, hp * 256, 256), [], [kpc], "w2")
                    B.dma("pool", pc[:, :, 256:512], wsrc(w_in, half * 1024, 8, 2048 + hp * 256, 256), [], [kpc], "w2")
                    for t in range(W):
                        for kc in range(8):
                            B.mm(bks[t][0][:, :], biga[:, t, half * 8 + kc, :], pc[:, kc, :],
                                 half == 0 and kc == 0, half == 3 and kc == 7, [kpc, hk[t]], [bks[t][1]])
                        if half == 3 and t == 0:
                            T0 = tk["n"] + 1
                            sched = [(0, "E1", 0), (1, "E1", 1), (4, "E2", 0), (5, "E1", 2), (5, "E2", 1), (6, "E1", 3),
                                     (8, "E3", 0), (9, "E3", 1), (9, "E2", 2), (10, "E2", 3), (13, "E3", 2), (14, "E3", 3)]
                            for dt, kind, tt_ in sched:
                                if tt_ >= W:
                                    continue
                                if kind == "E1":
                                    pend.append([T0 + dt, lambda hp=hp, t=tt_, bk=bks[tt_][0], kb=bks[tt_][1]: gm_E1(hp, t, bk, kb)])
                                elif kind == "E2":
                                    pend.append([T0 + dt, lambda hp=hp, t=tt_: gm_E2(hp, t)])
                                else:
                                    pend.append([T0 + dt, lambda hp=hp, t=tt_: gm_E3(hp, t)])
                            if hp + 1 < nhp:
                                pend.append([T0 + 7, lambda hp=hp: ld_gv(hp + 1)])
                                pend.append([T0 + 11, lambda hp=hp: ld_ga(hp + 1)])
                        tick()
            flush()
            for cg in range(cfg.get("n_qcg", 4)):
                bks = [bank() for _ in range(4)]
                for half in range(4):
                    pc, kpc = wpnext()
                    B.dma("pool", pc[:, :, :], wsrc(w_in, half * 1024, 8, 4096 + cg * 512, 512), [], [kpc], "w2")
                    for q in range(4):
                        for kc in range(8):
                            B.mm(bks[q][0][:, 0:N].rearrange("p (w t) -> p w t", t=128),
                                 pc[:, kc, q * 128:(q + 1) * 128], biga[:, 0:W, half * 8 + kc, :],
                                 half == 0 and kc == 0, half == 3 and kc == 7, [kpc] + hk, [bks[q][1]])
                        tick()
                flush()
                for q in range(4):
                    cnt["e"] += 1
                    fm_norm_epilogue(B, C, bks[q][0][:, 0:N], bks[q][1], N, gqs, qT[:, cg * 4 + q, :],
                                     "qT", cnt["e"], gkey="gqs")
            NM = W + 4
            nast = {"s": 0}

            def load_kv(g):
                sl = g % 2
                B.dma("sp", kmac[sl][:, :, 0:NM * 128],
                      kT_s[g * 4:(g + 1) * 4, :, j0 * 128:(j0 + NM) * 128].rearrange("h p t -> p h t"),
                      [("kT_s", hx, j0 + i) for hx in range(g * 4, g * 4 + 4) for i in range(NM)], kvkeys[sl], "ld2")
                B.dma("sp", vmac[sl][:, 0:NM, :].rearrange("p m (h d) -> p m h d", d=129),
                      v_s[j0:j0 + NM, :, g * 4:(g + 1) * 4, :].rearrange("m p h d -> p m h d"),
                      [("v_s", j0 + i) for i in range(NM)], kvkeys[sl], "ld2")

            def bm_load(h):
                b = h % 2
                for t in range(W):
                    B.dma("pool", bm_h[b][:, bm_off[b] + t * 640:bm_off[b] + (t + 1) * 640].rearrange("p (c q) -> p c q", q=128),
                          btab_d[cls_of(j0 + t), h], [], [("bm", b)], "w2")

            def na_S(h):
                g, hh, sl, b = h // 4, h % 4, (h // 4) % 2, h % 2
                for m in range(NM):
                    t_lo, t_hi = max(0, m - 4), min(W - 1, m)
                    nt = t_hi - t_lo + 1
                    sbi = nast["s"] % 3
                    nast["s"] += 1
                    bk, kb = C.pf[sbi], ("pf", sbi)
                    B.mm(bk[:, 0:nt * 128], kmac[sl][:, hh, m * 128:(m + 1) * 128], qT[:, h, t_lo * 128:(t_hi + 1) * 128],
                         True, False, kvkeys[sl] + ["qT"], [kb])
                    bap = bass.AP(bm_h[b], bm_off[b] + 128 * m + 512 * t_lo, [[bm_ps[b], 128], [512, nt], [1, 128]])
                    B.mm(bk[:, 0:nt * 128].rearrange("p (t q) -> p t q", q=128), C.ident[:, :], bap, False, True,
                         [("bm", b), "ident"], [kb])
                    B.act(pTall[b][:, m, 0:nt * 128], bk[:, 0:nt * 128], AF.Exp, [kb], [("pTall", b)])

            def na_PV(h):
                g, hh, sl, b = h // 4, h % 4, (h // 4) % 2, h % 2
                pvb = []
                for t in range(W):
                    if t % 2 == 0:
                        pvb.append((C.pf[3 + t // 2], ("pf", 3 + t // 2)))
                    bk, kb = pvb[-1]
                    for i in range(5):
                        m = t + i
                        t_lo = max(0, m - 4)
                        B.mm(bk[:, (t % 2) * 129:(t % 2) * 129 + 129], pTall[b][:, m, (t - t_lo) * 128:(t - t_lo + 1) * 128],
                             vmac[sl][:, m, hh * 129:(hh + 1) * 129], i == 0, i == 4, [("pTall", b)] + kvkeys[sl], [kb])
                for t in range(W):
                    bk, kb = pvb[t // 2]
                    c0 = (t % 2) * 129
                    B.s.add("dve", lambda e, o=nst[:, t:t + 1], i=bk[:, c0 + 128:c0 + 129]: e.reciprocal(o, i), [kb], ["nst"])
                    B.ts("dve", ysm[:, t, :], bk[:, c0:c0 + 128], nst[:, t:t + 1], 0.0, ALU.mult, ALU.add, [kb, "nst"], ["ysm"])
                yv = ysm[:, 0:W, :]
                B.tt("dve", sqy[:, 0:N].rearrange("p (t d) -> p t d", d=128), yv, yv, ALU.mult, ["ysm"], ["sqy"])
                B.red("dve", nst[:, 8:8 + W], sqy[:, 0:N].rearrange("p (t d) -> p t d", d=128), ALU.add, ["sqy"], ["nst"])
                B.ts("dve", nst[:, 16:16 + W], nst[:, 8:8 + W], 1.0 / HD, EPS, ALU.mult, ALU.add, ["nst"], ["nst"])
                rstd_from_ms(B, nst[:, 16:16 + W], nst[:, 24:24 + W], nst[:, 16:16 + W], ["nst"])
                for t in range(W):
                    B.stt("dve", mtok[b][:, t, :], ysm[:, t, :], nst[:, 16 + t:17 + t], gbt[:, h * 128:(h + 1) * 128],
                          ALU.mult, ALU.mult, ["ysm", "nst", "gtab"], [("mtok", b)])

            def na_T(h):
                b = h % 2
                pi = h % 2
                for t in range(W):
                    B.tr(C.psb[pi][:, t * 128:(t + 1) * 128], mtok[b][:, t, :], C.ident[:, :], [("mtok", b), "ident"],
                         [("psb", pi)])
                B.copy("act", mixT[:, 16 + h, :], C.psb[pi][:, 0:N], [("psb", pi)], ["mixT"])

            nah = cfg.get("n_nah", NH)
            pre = []
            for half in range(4):
                pc, kpc = wpnext()
                B.dma("pool", pc[:, :, :], wsrc(w_out, half * 1024, 8, 0, 512), [], [kpc], "w2")
                pre.append((pc, kpc))
            load_kv(0)
            bm_load(0)
            for k in range(nah + 2):
                if k + 1 < nah:
                    bm_load(k + 1)
                if k < nah:
                    na_S(k)
                if 1 <= k <= nah:
                    na_PV(k - 1)
                if k >= 2:
                    na_T(k - 2)
                if k % 4 == 0 and k // 4 + 1 < (nah + 3) // 4:
                    load_kv(k // 4 + 1)
            for cb in range(cfg.get("n_cb", 8)):
                bks = [bank() for _ in range(W)]
                for t in range(W):
                    j = j0 + t
                    B.dma("sp", xblk[t], x_ext[(j + 2) * 128:(j + 3) * 128, cb * 512:(cb + 1) * 512], [],
                          [xkey[t]], "ld2")
                for half in range(4):
                    if cb == 0:
                        pc, kpc = pre[half]
                    else:
                        pc, kpc = wpnext()
                        B.dma("pool", pc[:, :, :], wsrc(w_out, half * 1024, 8, cb * 512, 512), [], [kpc], "w2")
                    for t in range(W):
                        for kc in range(8):
                            B.mm(bks[t][0][:, :], mixT[:, half * 8 + kc, t * 128:(t + 1) * 128], pc[:, kc, :],
                                 half == 0 and kc == 0, half == 3 and kc == 7, [kpc, "mixT"], [bks[t][1]])
                for t in range(W):
                    j = j0 + t
                    b = (cb * W + t) % 2
                    B.tt("dve", hblk[b], bks[t][0][:, :], xblk[t], ALU.add, [bks[t][1], xkey[t]], [("rs", b)])
                    B.dma("act", h1_s[j * 128:(j + 1) * 128, cb * 512:(cb + 1) * 512], hblk[b], [("rs", b)],
                          [("h1_s", j)], "st2")
        B.s.barrier()

    if 25 in phases:
        B.dma("sp", C.gtab[:, :], gt_d["g_ffn"].ap(), [], ["gtab"], "c25")

        def after25(t):
            B.dma("act", hn2T_s[t], n_stg[t % 2], [("n_stg", t % 2)], [("hn2T_s", t)], "st25")
        C.after_tile = after25
        phase_norm_T(B, C, "p25", lambda t: h1_s[t * 128:(t + 1) * 128, :], lambda t: ("h1_s", t),
                     cfg.get("nt25", NMIX), lambda t, g: (n_stg[t % 2][:, g * 8:(g + 1) * 8, :], ("n_stg", t % 2)))
        C.after_tile = None
        B.s.barrier()

    if 3 in phases:
        state["nwp"] = 6
        gT = gt_h[:, :].rearrange("p (k t) -> p k t", t=512)
        uext = [sm_h[:, i * 520:i * 520 + 514] for i in range(2)]
        cacc = [sm_h[:, 1040 + i * 512:1040 + (i + 1) * 512] for i in range(2)]
        glb = sm_h[:, 2064:2576]
        h1b = [sm_h[:, 4736 + i * 512:4736 + (i + 1) * 512] for i in range(4)]
        h2b = [sm_h[:, 2576 + i * 512:2576 + (i + 1) * 512] for i in range(4)]
        hcol = sm_b[:, 9400:9464].rearrange("p (k t) -> p k t", t=2)
        hst = [sm_h[:, 6784:6792], sm_h[:, 6792:6800]]
        for w in cfg.get("ffn_wins", range(4)):
            for i in range(4):
                B.dma("sp", biga[:, i], hn2T_s[4 * w + 1 + i], [("hn2T_s", 4 * w + 1 + i)], [("biga", i)], "ld3")
            hk = [("biga", i) for i in range(4)]
            for side, jn, col in ((0, 4 * w, 127), (1, 4 * w + 5, 0)):
                pc, kpc = wpnext()
                nbv = pc[:, :, :].rearrange("p k c -> p (k c)").rearrange("p (k t) -> p k t", t=128)
                B.dma("sp", nbv, hn2T_s[jn], [("hn2T_s", jn)], [kpc], "ld3")
                B.copy("dve", hcol[:, :, side], nbv[:, :, col], [kpc], ["hcol"])
            mcol8 = pp[:, PP_MASK + 8 * w:PP_MASK + 8 * w + 8]
            for fg in range(cfg.get("n_fg", 43)):
                bks = [bank() for _ in range(4)]
                hb, khb = C.pf[5], ("pf", 5)
                for half in range(4):
                    pc, kpc = wpnext()
                    B.dma("pool", pc[:, :, 0:256], wsrc(w_up, half * 1024, 8, fg * 256, 256), [], [kpc], "w3")
                    B.dma("pool", pc[:, :, 256:512], wsrc(w_up, half * 1024, 8, DFF + fg * 256, 256), [], [kpc], "w3")
                    for q in range(4):
                        for kc in range(8):
                            st_, sp_ = (half == 0 and kc == 0), (half == 3 and kc == 7)
                            B.mm(bks[q][0][:, :].rearrange("p (w t) -> p w t", t=128), pc[:, kc, q * 128:(q + 1) * 128],
                                 biga[:, :, half * 8 + kc, :], st_, sp_, [kpc] + hk, [bks[q][1]])
                            B.mm(hb[:, 2 * q:2 * q + 2], pc[:, kc, q * 128:(q + 1) * 128], hcol[:, half * 8 + kc, :],
                                 st_ and q == 0, sp_ and q == 3, [kpc, "hcol"], [khb])
                hs = hst[fg % 2]
                B.tt("dve", hs, hb[:, 0:8], mcol8, ALU.mult, [khb, "pp"], [("hst", fg % 2)])
                for c in range(2):
                    fc = 2 * fg + c
                    for part in range(2):
                        q = part * 2 + c
                        bk, kb = bks[q]
                        ch = fc + part * NFC
                        cp = pp[:, PP_CPAR + ch * 4:PP_CPAR + ch * 4 + 4]
                        ue, kue = uext[part], ("uext", part)
                        ca, kca = cacc[part], ("cacc", part)
                        B.copy("act", ue[:, 1:513], bk[:, :], [kb], [kue])
                        B.copy("act", ue[:, 0:514:513], hs[:, 2 * q:2 * q + 2], [("hst", fg % 2)], [kue])
                        B.act(ca, bk[:, :], AF.Identity, [kb, "pp"], [kca], bias=cp[:, 3:4], scale=cp[:, 1:2])
                        B.stt("dve", ca, ue[:, 0:512], cp[:, 0:1], ca, ALU.mult, ALU.add, [kue, kca, "pp"], [kca])
                        B.stt("dve", ca, ue[:, 2:514], cp[:, 2:3], ca, ALU.mult, ALU.add, [kue, kca, "pp"], [kca])
                    B.act(glb, cacc[0], AF.Gelu, [("cacc", 0)], ["glb"])
                    B.tt("dve", gT[:, fc, :], glb, cacc[1], ALU.mult, ["glb", ("cacc", 1)], [("gT", fc)])
            nfc = cfg.get("n_fg", 43) * 2
            gk = [("gT", fc) for fc in range(nfc)]
            for cb in range(cfg.get("n_cb3", 8)):
                bks = [bank() for _ in range(4)]
                for t in range(4):
                    jm = 4 * w + 1 + t
                    B.dma("sp", h1b[t], h1_s[jm * 128:(jm + 1) * 128, cb * 512:(cb + 1) * 512], [("h1_s", jm)],
                          [("h1b", t)], "ld3")
                npc = (nfc + 7) // 8
                for pk in range(npc):
                    nk = min(8, nfc - pk * 8)
                    pc, kpc = wpnext()
                    B.dma("pool", pc[:, 0:nk, :], wsrc(w_down, pk * 1024, nk, cb * 512, 512), [], [kpc], "w3")
                    for t in range(4):
                        for k in range(nk):
                            kg = pk * 8 + k
                            B.mm(bks[t][0][:, :], gT[:, kg, t * 128:(t + 1) * 128], pc[:, k, :], kg == 0, kg == nfc - 1,
                                 [kpc, ("gT", kg)], [bks[t][1]])
                for t in range(4):
                    B.tt("dve", h2b[t], bks[t][0][:, :], h1b[t], ALU.add, [bks[t][1], ("h1b", t)], [("h2b", t)])
                    jo = 4 * w + t
                    B.dma("act", h2_s[jo * 128:(jo + 1) * 128, cb * 512:(cb + 1) * 512], h2b[t], [("h2b", t)],
                          [("h2_s", jo)], "st3")
        B.s.barrier()

    if 4 in phases:
        state["nwp"] = 4
        wps = gt_h[:, 24576:32768].rearrange("p (k c) -> p k c", c=D)
        pin = [gt_f[:, 16384 + i * 256:16384 + (i + 1) * 256] for i in range(2)]
        pbf = gt_h[:, 33792:34048]
        pTt = gt_h[:, 34048:35072].rearrange("p (t k q) -> p t k q", t=4, k=2)
        junk = gt_h[:, 35072:35584]
        ssq = sm_h[:, 0:32].rearrange("p (t c) -> p t c", c=8)
        rse = sm_h[:, 32:48]
        sgb = [sm_h[:, 64 + i * 512:64 + (i + 1) * 512] for i in range(2)] + \
              [sm_h[:, 5184 + i * 512:5184 + (i + 1) * 512] for i in range(2)]
        tmb = [sm_h[:, 1088 + i * 512:1088 + (i + 1) * 512] for i in range(2)] + \
              [sm_h[:, 6208 + i * 512:6208 + (i + 1) * 512] for i in range(2)]
        gpb = [sm_h[:, 2112 + i * 512:2112 + (i + 1) * 512] for i in range(2)]
        h2b = [sm_h[:, 3136 + i * 512:3136 + (i + 1) * 512] for i in range(4)]
        B.dma("sp", C.gtab[:, :], gt_d["g_ple"].ap(), [], ["gtab"], "c4")
        B.dma("pool", wps, w_p.ap().rearrange("(k p) c -> p k c", p=128), [], ["wps"], "w4")
        for w in cfg.get("ple_wins", range(4)):
            phase_norm_T(B, C, "p4", lambda t: h2_s[(4 * w + t) * 128:(4 * w + t + 1) * 128, :],
                         lambda t: ("h2_s", 4 * w + t), 4,
                         lambda t, g: (biga[:, t, g * 8:(g + 1) * 8, :], ("biga", t)))
            for t in range(4):
                jo = 4 * w + t
                b = t % 2
                B.dma("sp", pin[b], p_own[jo * 128:(jo + 1) * 128, :], [], [("pin", b)], "ld4")
                B.copy("dve", pbf, pin[b], [("pin", b)], ["pbf"])
                for k in range(2):
                    B.tr(C.psb[0][:, k * 128:(k + 1) * 128], pbf[:, k * 128:(k + 1) * 128], C.ident[:, :],
                         ["pbf", "ident"], [("psb", 0)])
                B.copy("act", pTt[:, t], C.psb[0][:, 0:256].rearrange("p (k q) -> p k q", q=128), [("psb", 0)],
                       [("pTt", t)])
                for cb in range(8):
                    bk, kb = bank()
                    for k in range(2):
                        B.mm(bk[:, :], pTt[:, t, k, :], wps[:, k, cb * 512:(cb + 1) * 512], k == 0, k == 1,
                             [("pTt", t), "wps"], [kb])
                    B.act(junk, bk[:, :], AF.Square, [kb], ["junk", ("ssq", t)], accum=ssq[:, t, cb:cb + 1])
                r4 = rse[:, t * 4:(t + 1) * 4]
                B.red("dve", r4[:, 0:1], ssq[:, t, :], ALU.add, [("ssq", t)], [("rse", t)])
                B.ts("dve", r4[:, 1:2], r4[:, 0:1], 1.0 / D, EPS, ALU.mult, ALU.add, [("rse", t)], [("rse", t)])
                rstd_from_ms(B, r4[:, 1:2], r4[:, 3:4], r4[:, 2:3], [("rse", t)])
            hk = [("biga", i) for i in range(4)]
            for cb in range(8):
                b = cb % 2
                B.dma("sp", gpb[b], gt_d["g_post"][:, cb * 512:(cb + 1) * 512], [], [("gpb", b)], "ld4")
                bks = [(C.pf[t], ("pf", t)) for t in range(4)]
                for t in range(4):
                    jo = 4 * w + t
                    B.dma("sp", h2b[t], h2_s[jo * 128:(jo + 1) * 128, cb * 512:(cb + 1) * 512], [("h2_s", jo)],
                          [("h2b", t)], "ld4")
                for half in range(4):
                    pc, kpc = wpnext()
                    B.dma("pool", pc[:, :, :], wsrc(w_gate, half * 1024, 8, cb * 512, 512), [], [kpc], "w4")
                    for t in range(4):
                        for kc in range(8):
                            B.mm(bks[t][0][:, :], biga[:, t, half * 8 + kc, :], pc[:, kc, :],
                                 half == 0 and kc == 0, half == 3 and kc == 7, [kpc, hk[t]], [bks[t][1]])
                for t in range(4):
                    jo = 4 * w + t
                    bb = t
                    B.act(sgb[bb], bks[t][0][:, :], AF.Sigmoid, [bks[t][1]], [("sgb", bb)])
                    eb, keb = C.pf[4 + t % 2], ("pf", 4 + t % 2)
                    for k in range(2):
                        B.mm(eb[:, :], pTt[:, t, k, :], wps[:, k, cb * 512:(cb + 1) * 512], k == 0, k == 1,
                             [("pTt", t), "wps"], [keb])
                    r4 = rse[:, t * 4:(t + 1) * 4]
                    B.stt("dve", tmb[bb], eb[:, :], r4[:, 2:3], gpb[b], ALU.mult, ALU.mult,
                          [keb, ("rse", t), ("gpb", b)], [("tmb", bb)])
                    B.tt("dve", tmb[bb], tmb[bb], sgb[bb], ALU.mult, [("tmb", bb), ("sgb", bb)], [("tmb", bb)])
                    B.tt("dve", tmb[bb], tmb[bb], h2b[t], ALU.add, [("tmb", bb), ("h2b", t)], [("tmb", bb)])
                    B.dma("sp", out_d[jo * 128:(jo + 1) * 128, cb * 512:(cb + 1) * 512], tmb[bb], [("tmb", bb)],
                          [("out", jo, cb)], "st4")
    else:
        B.dma("sp", out_d[0:128, 0:512], sm_h[:, 0:512], [], [("out", 0)], "st4")
    B.s.emit(nc, stack)
    return B


def _btab(rpb, c):
    out = np.full((5, NH, 128, 5, 128), MASKVAL, np.float32)
    rep = {0: 1, 1: 2, 2: 15, 3: 16, 4: 8}
    qi = np.arange(128)
    ks = np.arange(576)
    for cls, j in rep.items():
        r0 = 32 * c + 2 * (j - 1)
        if cls == 4:
            r0 = 64
            real_lo = r0 - 4
            slot_rows = real_lo + np.arange(9)
            actual = slot_rows.copy()
            mirrored = np.zeros(9, bool)
        else:
            slot_rows = (r0 - 4) + np.arange(9)
            actual = slot_rows.copy()
            mirrored = np.zeros(9, bool)
            for s in range(9):
                g = slot_rows[s]
                if g < 0:
                    actual[s] = 8 + g if g >= -4 else -1000
                    mirrored[s] = True
                elif g >= 128:
                    actual[s] = g - 8 if g <= 129 else -1000
                    mirrored[s] = True
        real_set = set(int(a) for a, m in zip(actual, mirrored) if not m)
        qrow = np.clip(r0 + qi // 64, 0, 127)
        qcol = qi % 64
        rs = np.clip(qrow - 4, 0, 120)
        cs = np.clip(qcol - 8, 0, 48)
        s_of = ks // 64
        ck = ks % 64
        ar = actual[s_of]
        ok_slot = np.array([(not mirrored[s]) or (int(actual[s]) not in real_set) for s in range(9)])[s_of]
        valid = (ok_slot[:, None] & (ar[:, None] >= rs[None, :]) & (ar[:, None] < rs[None, :] + 8)
                 & (ck[:, None] >= cs[None, :]) & (ck[:, None] < cs[None, :] + 16))
        dr = np.clip(ar[:, None] - qrow[None, :] + 7, 0, 14)
        dc = np.clip(ck[:, None] - qcol[None, :] + 15, 0, 30)
        vals = rpb[:, dr, dc]
        tab = np.where(valid[None], vals, np.float32(MASKVAL)).astype(np.float32)
        full = np.full((NH, 640, 128), MASKVAL, np.float32)
        full[:, :576] = tab
        out[cls] = full.reshape(NH, 5, 128, 128).transpose(0, 2, 1, 3)
    return out


def prep_core(inp, cid):
    b, c = cid // 4, cid % 4
    x = inp["x"][b]
    xe = np.zeros((NEXT * 128, D), np.float32)
    for er in range(44):
        g = 32 * c - 6 + er
        if g < 0:
            g = 8 + g if g >= -4 else None
        elif g >= 128:
            g = g - 8 if g <= 129 else None
        if g is not None:
            xe[er * 64:(er + 1) * 64] = x[g * 64:(g + 1) * 64]
    t0 = 2048 * c
    bc = lambda v, n: np.ascontiguousarray(np.broadcast_to(np.asarray(v, np.float32).reshape(1, n), (128, n)))
    pp = np.zeros((128, NPP), np.float32)
    pp[:, 0] = inp["q_norm_g"][0]
    pp[:, 1] = inp["k_norm_g"][0]
    pp[:, 2:18] = inp["gmlp_bs"][0].T
    cw = inp["conv_w"][0].reshape(3, 172, 128)
    cb = inp["conv_b"][0].reshape(172, 128)
    cp = np.concatenate([cw.transpose(2, 1, 0), cb.T[:, :, None]], axis=2)
    pp[:, PP_CPAR:PP_CPAR + 688] = cp.reshape(128, 688)
    m = np.ones((4, 2), np.float32)
    m[0, 0] = 0.0 if c == 0 else 1.0
    m[3, 1] = 0.0 if c == 3 else 1.0
    pp[:, PP_MASK:PP_MASK + 32] = np.tile(m[:, None, :], (1, 4, 1)).reshape(1, 32)
    return {
        "x_ext": xe,
        "p_own": np.ascontiguousarray(inp["p"][0, b, t0:t0 + NTOK]),
        "g_mix": bc(inp["norm_mix_g"][0], D), "g_ffn": bc(inp["norm_ffn_g"][0], D),
        "g_ple": bc(inp["norm_ple_g"][0], D), "g_post": bc(inp["ple_post_g"][0], D),
        "g_v": bc(inp["gmlp_v_g"][0].reshape(-1), 2048), "g_a": bc(inp["out_norm_a_g"][0].reshape(-1), 2048),
        "g_b": bc(inp["out_norm_b_g"][0].reshape(-1), 2048),
        "pp": pp,
        "wsT": np.ascontiguousarray(inp["gmlp_ws"][0].transpose(2, 0, 1)),
        "btab": _btab(np.asarray(inp["na_rpb"][0], np.float32), c),
        "ident": np.eye(128, dtype=np.float32),
        "w_in": inp["w_in"][0], "w_out": inp["w_out"][0], "w_up": inp["w_up"][0], "w_down": inp["w_down"][0],
        "w_gate": inp["w_ple_gate"][0], "w_p": inp["w_ple_proj"][0],
    }


def kernel(**inputs):
    from contextlib import ExitStack
    inp = {k: np.asarray(v) for k, v in inputs.items()}
    nc = bass.Bass("TRN2", target_bir_lowering=False)
    with ExitStack() as stack:
        build(nc, stack, {})
    in_maps = [prep_core(inp, cid) for cid in range(8)]
    res = run_bass_kernel_spmd(nc, in_maps, core_ids=list(range(8)))
    out = np.zeros((2, 8192, D), np.float32)
    for cid in range(8):
        b, c = cid // 4, cid % 4
        out[b, 2048 * c:2048 * (c + 1)] = np.asarray(res.results[cid]["out"])
    return out
```

```python
import numpy as np
import concourse.bass as bass
import concourse.mybir as mybir
from concourse.bass_utils import run_bass_kernel_spmd

F32 = mybir.dt.float32
BF16 = mybir.dt.bfloat16
AF = mybir.ActivationFunctionType
ALU = mybir.AluOpType
AX = mybir.AxisListType

D = 4096
KC = 32
NH = 16
HD = 128
DIN = 10240
DFF = 11008
NFC = 86
DPLE = 256
EPS = 1e-6
NEXT = 22
NMIX = 18
NTOK = 2048

ENGS = ("pe", "act", "dve", "pool", "sp")
EPOCH = 6000
DMA_RING = {"sp": 16, "pool": 8, "act": 4, "dve": 4, "pe": 4}


class _Op:
    __slots__ = ("eng", "fn", "deps", "dma", "sig", "sem", "cnt", "idx", "chan")


class Sched:
    def __init__(self):
        self.ops = []
        self.streams = {e: [] for e in ENGS}
        self.lastw = {}
        self.readers = {}
        self.chans = {}
        self.dma_last = {}
        self.bar = set()

    def add(self, eng, fn, reads=(), writes=(), dma=False, chan=None):
        op = _Op()
        op.eng = eng
        op.fn = fn
        op.dma = dma
        op.idx = len(self.ops)
        op.sig = False
        op.chan = chan
        deps = set()
        for k in tuple(reads) + tuple(writes):
            w = self.lastw.get(k)
            if w is not None:
                deps.add(w)
        for k in writes:
            r = self.readers.get(k)
            if r:
                deps.update(r.values())
        deps.discard(op.idx)
        deps |= self.bar
        op.deps = deps
        if dma:
            R = DMA_RING[eng]
            st = self.chans.setdefault(eng, [0])
            i = st[0]
            st[0] += 1
            op.sem = (eng, i % R)
            op.cnt = 16 * (i // R + 1)
            prev = self.dma_last.get(op.sem)
            if prev is not None:
                deps.add(prev)
            self.dma_last[op.sem] = op.idx
        for k in reads:
            r = self.readers.setdefault(k, {})
            rk = ("dma", op.idx) if dma else eng
            r[rk] = op.idx
        for k in writes:
            self.lastw[k] = op.idx
            self.readers[k] = {}
        self.ops.append(op)
        self.streams[eng].append(op)
        return op

    def barrier(self):
        b = set(self.dma_last.values())
        for e in ENGS:
            for o in reversed(self.streams[e]):
                if not o.dma:
                    b.add(o.idx)
                    break
        self.bar = b

    def emit(self, nc, stack):
        ops = self.ops
        for op in ops:
            for d in op.deps:
                dop = ops[d]
                if dop.dma:
                    continue
                if dop.eng == "pe" and op.eng == "pe" and not op.dma:
                    continue
                dop.sig = True
        eng_sems = {}
        for e in ENGS:
            n = sum(1 for o in self.streams[e] if o.sig and not o.dma)
            k = n // EPOCH + 1
            eng_sems[e] = [stack.enter_context(nc.semaphore(f"s_{e}_{i}")) for i in range(k)]
            c = 0
            for o in self.streams[e]:
                if o.sig and not o.dma:
                    o.sem = eng_sems[e][c // EPOCH]
                    o.cnt = c % EPOCH + 1
                    c += 1
        dsems = {}
        for o in ops:
            if o.dma:
                if o.sem not in dsems:
                    dsems[o.sem] = stack.enter_context(nc.semaphore(f"d_{o.sem[0]}_{o.sem[1]}"))
                o.sem = dsems[o.sem]
        block = stack.enter_context(nc.Block())

        def run_stream(e, engobj):
            waited = {}
            for o in self.streams[e]:
                for d in sorted(o.deps):
                    dop = ops[d]
                    if not dop.dma and dop.eng == "pe" and e == "pe" and not o.dma:
                        continue
                    key = id(dop.sem)
                    if waited.get(key, 0) >= dop.cnt:
                        continue
                    waited[key] = dop.cnt
                    engobj.wait_ge(dop.sem, dop.cnt)
                ins = o.fn(engobj)
                if o.dma:
                    ins.then_inc(o.sem, 16)
                elif o.sig:
                    ins.then_inc(o.sem, 1)
            last = {}
            for o in self.streams[e]:
                if o.dma:
                    last[id(o.sem)] = (o.sem, max(o.cnt, last.get(id(o.sem), (None, 0))[1]))
            for sem, cnt in last.values():
                engobj.wait_ge(sem, cnt)

        @block.tensor
        def _(eng):
            run_stream("pe", eng)

        @block.scalar
        def _(eng):
            run_stream("act", eng)

        @block.vector
        def _(eng):
            run_stream("dve", eng)

        @block.gpsimd
        def _(eng):
            run_stream("pool", eng)

        @block.sync
        def _(eng):
            run_stream("sp", eng)


class Builder:
    def __init__(self, nc, stack):
        self.nc = nc
        self.stack = stack
        self.s = Sched()
        self.uid = 0
        self.psum_rr = 0

    def sb(self, name, shape, dt):
        return self.stack.enter_context(self.nc.sbuf_tensor(name, list(shape), dt))

    def ps(self, name, shape, dt):
        return self.stack.enter_context(self.nc.psum_tensor(name, list(shape), dt))

    def dram(self, name, shape, dt, kind):
        return self.nc.dram_tensor(name, list(shape), dt, kind=kind)

    def dma(self, q, out, in_, r, w, chan):
        self.s.add(q, lambda e, o=out, i=in_: e.dma_start(out=o, in_=i), r, w, dma=True, chan=chan)

    def mm(self, out, lhsT, rhs, start, stop, r, w):
        self.s.add("pe", lambda e, o=out, l=lhsT, x=rhs, a=start, b=stop: e.matmul(o, l, x, start=a, stop=b), r, w)

    def tr(self, out, in_, ident, r, w):
        self.s.add("pe", lambda e, o=out, i=in_, d=ident: e.transpose(o, i, d), r, w)

    def act(self, out, in_, func, r, w, bias=None, scale=None, accum=None, eng="act"):
        def fn(e, o=out, i=in_, f=func, b=bias, sc=scale, ac=accum):
            kw = {}
            if b is not None:
                kw["bias"] = b
            if sc is not None:
                kw["scale"] = sc
            if ac is not None:
                kw["accum_out"] = ac
            return e.activation(o, i, f, **kw)
        self.s.add(eng, fn, r, w)

    def tt(self, eng, out, in0, in1, op, r, w):
        self.s.add(eng, lambda e, o=out, a=in0, b=in1, p=op: e.tensor_tensor(o, a, b, p), r, w)

    def ts(self, eng, out, in0, s1, s2, op0, op1, r, w, accum=None):
        def fn(e, o=out, a=in0, x=s1, y=s2, p=op0, q=op1, ac=accum):
            if q is None:
                return e.tensor_scalar(o, a, x, None, p)
            if ac is not None:
                return e.tensor_scalar(o, a, x, y, p, q, ac)
            return e.tensor_scalar(o, a, x, y, p, q)
        self.s.add(eng, fn, r, w)

    def stt(self, eng, out, in0, scalar, in1, op0, op1, r, w):
        self.s.add(eng, lambda e, o=out, a=in0, sc=scalar, b=in1, p=op0, q=op1:
                   e.scalar_tensor_tensor(o, a, sc, b, p, q), r, w)

    def red(self, eng, out, in_, op, r, w):
        self.s.add(eng, lambda e, o=out, i=in_, p=op: e.tensor_reduce(o, i, AX.X, p), r, w)

    def copy(self, eng, out, in_, r, w):
        if eng == "act":
            self.s.add(eng, lambda e, o=out, i=in_: e.copy(o, i), r, w)
        else:
            self.s.add(eng, lambda e, o=out, i=in_: e.tensor_copy(o, i), r, w)

    def memset(self, eng, ap, val, w):
        self.s.add(eng, lambda e, a=ap, v=val: e.memset(a, v), (), w)


def wsrc(W, r0, nkc, c0, ncol):
    return W[r0:r0 + 128 * nkc, c0:c0 + ncol].rearrange("(k p) c -> p k c", p=128)


SCALE = float(HD) ** -0.5
NPP = 18 + 172 * 4 + 32
PP_CPAR = 18
PP_MASK = 18 + 172 * 4
MASKVAL = -100.0


class Ctx:
    pass


def cls_of(j):
    return {1: 0, 2: 1, 15: 2, 16: 3}.get(j, 4)


def rstd_from_ms(B, ms_ap, tmp_ap, out_ap, keys):
    B.act(tmp_ap, ms_ap, AF.Sqrt, keys, keys)
    B.s.add("dve", lambda e, o=out_ap, i=tmp_ap: e.reciprocal(o, i), keys, keys)


def phase_norm_T(B, C, name, src_tile_ap, src_key, ntiles, dst_fn):
    xin, ss = C.n_xin, C.n_ss
    B.dma("sp", xin[0], src_tile_ap(0), [src_key(0)], [("n_xin", 0)], "ld" + name)
    for t in range(ntiles):
        b = t % 2
        xbf = C.n_xbf[b]
        kx = ("n_xin", b)
        if t + 1 < ntiles:
            B.dma("sp", xin[1 - b], src_tile_ap(t + 1), [src_key(t + 1)], [("n_xin", 1 - b)], "ld" + name)
        ksq = ("n_xbf", b)
        kss = ("n_ss", b)
        B.act(xbf, xin[b], AF.Square, [kx], [ksq, kss], accum=ss[b][:, 0:1])
        B.ts("dve", ss[b][:, 1:2], ss[b][:, 0:1], 1.0 / D, EPS, ALU.mult, ALU.add, [kss], [kss])
        rstd_from_ms(B, ss[b][:, 1:2], ss[b][:, 3:4], ss[b][:, 2:3], [kss])
        for hlf in range(2):
            sl = slice(hlf * 2048, (hlf + 1) * 2048)
            B.stt("dve", xbf[:, sl], xin[b][:, sl], ss[b][:, 2:3], C.gtab[:, sl], ALU.mult, ALU.mult,
                  [kx, kss, C.gtab_key], [ksq])
        for g in range(4):
            pi = (t * 4 + g) % 2
            pb = C.psb[pi]
            kp = ("psb", pi)
            for i in range(8):
                kc = g * 8 + i
                B.tr(pb[:, i * 128:(i + 1) * 128], xbf[:, kc * 128:(kc + 1) * 128], C.ident[:, :],
                     [ksq, "ident"], [kp])
            dst, kd = dst_fn(t, g)
            eng = "act" if g % 2 == 0 else "dve"
            B.copy(eng, dst, pb[:, :].rearrange("p (k t) -> p k t", t=128), [kp], [kd])
        if C.after_tile is not None:
            C.after_tile(t)


def fm_norm_epilogue(B, C, bank_ap, kbank, N, gcol, out_ap, kout, idx, gkey="pp"):
    b = idx % 2
    sqf, rs = C.sqb[b], C.rs[b]
    ks, kr = ("sqf", b), ("rs", b)
    B.act(sqf[:, 0:N], bank_ap, AF.Square, [kbank], [ks])
    ob = C.pf[5]
    B.mm(ob[:, 0:N], C.ones[:, :], sqf[:, 0:N], True, True, [ks, "ones"], [("pf", 5)])
    B.ts("dve", rs[:, 0:N], ob[:, 0:N], 1.0 / HD, EPS, ALU.mult, ALU.add, [("pf", 5)], [kr])
    B.act(rs[:, 0:N], rs[:, 0:N], AF.Sqrt, [kr], [kr])
    B.s.add("dve", lambda e, o=rs[:, 0:N], i=rs[:, 0:N]: e.reciprocal(o, i), [kr], [kr])
    B.stt("dve", out_ap, bank_ap, gcol, rs[:, 0:N], ALU.mult, ALU.mult, [kbank, kr, gkey], [kout])


def build(nc, stack, cfg):
    B = Builder(nc, stack)
    C = Ctx()
    dbg = cfg.get("debug", False)
    phases = cfg.get("phases", (0, 1, 2, 25, 3, 4))
    skind = "ExternalOutput" if dbg else "Internal"

    x_ext = B.dram("x_ext", [NEXT * 128, D], F32, "ExternalInput")
    p_own = B.dram("p_own", [NTOK, DPLE], F32, "ExternalInput")
    gt_d = {n: B.dram(n, [128, D], F32, "ExternalInput") for n in ("g_mix", "g_ffn", "g_ple", "g_post")}
    gh_d = {n: B.dram(n, [128, 2048], F32, "ExternalInput") for n in ("g_v", "g_a", "g_b")}
    pp_d = B.dram("pp", [128, NPP], F32, "ExternalInput")
    wsT_d = B.dram("wsT", [128, NH, 128], F32, "ExternalInput")
    btab_d = B.dram("btab", [5, NH, 128, 5, 128], F32, "ExternalInput")
    ident_d = B.dram("ident", [128, 128], F32, "ExternalInput")
    w_in = B.dram("w_in", [D, DIN], F32, "ExternalInput")
    w_out = B.dram("w_out", [D, D], F32, "ExternalInput")
    w_up = B.dram("w_up", [D, 2 * DFF], F32, "ExternalInput")
    w_down = B.dram("w_down", [DFF, D], F32, "ExternalInput")
    w_gate = B.dram("w_gate", [D, D], F32, "ExternalInput")
    w_p = B.dram("w_p", [DPLE, D], F32, "ExternalInput")
    hnT_s = B.dram("hnT_s", [NEXT, 128, KC, 128], BF16, skind)
    kT_s = B.dram("kT_s", [NH, 128, NEXT * 128], BF16, skind)
    v_s = B.dram("v_s", [NEXT, 128, NH, 129], BF16, skind)
    h1_s = B.dram("h1_s", [NMIX * 128, D], F32, skind)
    hn2T_s = B.dram("hn2T_s", [NMIX, 128, KC, 128], BF16, skind)
    h2_s = B.dram("h2_s", [NTOK, D], F32, skind)
    out_d = B.dram("out", [NTOK, D], F32, "ExternalOutput")

    biga = B.sb("biga", [128, 4, KC, 128], BF16)
    gt_h = B.sb("gt", [128, NFC * 512], BF16)
    gt_f = gt_h.bitcast(F32)
    wp = [B.sb(f"wp{i}", [128, 8, 512], BF16) for i in range(4)]
    C.gtab = B.sb("gtab", [128, D], F32)
    sm_h = B.sb("small", [128, 8192], F32)
    sm_b = sm_h.bitcast(BF16)
    C.ident = B.sb("ident_sb", [128, 128], BF16)
    C.ones = B.sb("ones_sb", [128, 128], BF16)
    pp = B.sb("pp_sb", [128, NPP], F32)
    C.pf = [B.ps(f"pf{i}", [128, 512], F32) for i in range(6)]
    C.psb = [B.ps(f"psb{i}", [128, 1024], BF16) for i in range(2)]
    C.after_tile = None
    C.gtab_key = "gtab"

    C.n_xin = [gt_f[:, 0:4096], gt_f[:, 4096:8192]]
    C.n_xbf = [gt_h[:, 16384:20480], gt_h[:, 20480:24576]]
    n_stg = [gt_h[:, 32768:36864].rearrange("p (k t) -> p k t", t=128),
             gt_h[:, 36864:40960].rearrange("p (k t) -> p k t", t=128)]
    C.n_ss = [sm_h[:, 8000:8004], sm_h[:, 8004:8008]]

    state = {"bank": 0, "wp": 0, "nwp": 4}
    _gb = C.gtab.bitcast(BF16)
    wpx = [_gb[:, 0:4096].rearrange("p (k c) -> p k c", c=512), _gb[:, 4096:8192].rearrange("p (k c) -> p k c", c=512)]

    def bank():
        i = state["bank"] % 5
        state["bank"] += 1
        return C.pf[i], ("pf", i)

    def wpnext():
        i = state["wp"] % state["nwp"]
        state["wp"] += 1
        if i >= 4:
            return wpx[i - 4], ("gtabq", i - 4)
        return wp[i], ("wp", i)

    B.dma("pool", C.ident[:, :], ident_d.ap(), [], ["ident"], "c")
    B.dma("sp", pp[:, :], pp_d.ap(), [], ["pp"], "c")
    B.memset("dve", C.ones[:, :], 1.0, ["ones"])

    if 0 in phases:
        gtab_keep = C.gtab
        C.gtab = gt_f[:, 12288:16384]
        C.gtab_key = "gtab0"
        B.dma("sp", C.gtab, gt_d["g_mix"].ap(), [], ["gtab0"], "c")
        nt0 = cfg.get("nt0", NEXT)

        def after0(t):
            B.dma("act", hnT_s[t], n_stg[t % 2], [("n_stg", t % 2)], [("hnT_s", t)], "st0")
        C.after_tile = after0
        phase_norm_T(B, C, "p0", lambda t: x_ext[t * 128:(t + 1) * 128, :], lambda t: ("x_ext", t), nt0,
                     lambda t, g: (n_stg[t % 2][:, g * 8:(g + 1) * 8, :], ("n_stg", t % 2)))
        C.after_tile = None
        C.gtab = gtab_keep
        C.gtab_key = "gtab"

    if 1 in phases:
        state["nwp"] = 6
        sqb4 = [sm_b[:, i * 512:(i + 1) * 512] for i in range(4)]
        rs4 = [sm_h[:, 1024 + i * 512:1024 + (i + 1) * 512] for i in range(4)]
        kraw = [sm_h[:, 3072 + i * 512:3072 + (i + 1) * 512] for i in range(4)]
        kst = [sm_b[:, 10240 + i * 512:10240 + (i + 1) * 512] for i in range(4)]
        vst = [sm_b[:, 12288 + i * 516:12288 + (i + 1) * 516].rearrange("p (h d) -> p h d", d=129) for i in range(2)]
        for i in range(2):
            B.memset("dve", vst[i][:, :, 128:129], 1.0, [("vst", i)])
        wins = cfg.get("kv_wins", [(0, 4), (4, 4), (8, 4), (12, 4), (16, 4), (20, 2)])
        ecnt = 0
        pend1 = []
        tk1 = {"n": 0}

        def tick1():
            tk1["n"] += 1
            due = sorted([p for p in pend1 if p[0] <= tk1["n"]], key=lambda p: p[0])
            for p in due:
                pend1.remove(p)
                p[1]()

        def k_epi(q, h, e0, W, N):
            ob, kob = C.pf[5], ("pf", 5)
            B.mm(ob[:, 0:N], C.ones[:, :], sqb4[q][:, 0:N], True, True, [("sqb4", q), "ones"], [kob])
            B.ts("dve", rs4[q][:, 0:N], ob[:, 0:N], 1.0 / HD, EPS, ALU.mult, ALU.add, [kob], [("rs4", q)])
            B.act(rs4[q][:, 0:N], rs4[q][:, 0:N], AF.Sqrt, [("rs4", q)], [("rs4", q)])
            B.s.add("dve", lambda e, o=rs4[q][:, 0:N], i=rs4[q][:, 0:N]: e.reciprocal(o, i), [("rs4", q)], [("rs4", q)])
            B.stt("dve", kst[q][:, 0:N], kraw[q][:, 0:N], pp[:, 1:2], rs4[q][:, 0:N], ALU.mult, ALU.mult,
                  [("kraw", q), ("rs4", q), "pp"], [("kst", q)])
            B.dma("sp", kT_s[h, :, e0 * 128:e0 * 128 + N], kst[q][:, 0:N], [("kst", q)],
                  [("kT_s", h, e0 + i) for i in range(W)], "st1")

        for (e0, W) in wins:
            N = W * 128
            for i in range(W):
                B.dma("sp", biga[:, i], hnT_s[e0 + i], [("hnT_s", e0 + i)], [("biga", i)], "ld1")
            hk = [("biga", i) for i in range(W)]
            for part in ("k", "v"):
                for cg in range(cfg.get("kv_cgs", 4)):
                    c0 = (6144 if part == "k" else 8192) + cg * 512
                    nb = 4 if part == "k" else W
                    bks = [bank() for _ in range(nb)]
                    for half in range(4):
                        pc, kpc = wpnext()
                        B.dma("pool", pc[:, :, :], wsrc(w_in, half * 1024, 8, c0, 512), [], [kpc], "w1")
                        for q in range(nb):
                            for kc in range(8):
                                st_, sp_ = (half == 0 and kc == 0), (half == 3 and kc == 7)
                                if part == "k":
                                    B.mm(bks[q][0][:, 0:N].rearrange("p (w t) -> p w t", t=128),
                                         pc[:, kc, q * 128:(q + 1) * 128], biga[:, 0:W, half * 8 + kc, :],
                                         st_, sp_, [kpc] + hk, [bks[q][1]])
                                else:
                                    B.mm(bks[q][0][:, :], biga[:, q, half * 8 + kc, :], pc[:, kc, :],
                                         st_, sp_, [kpc, hk[q]], [bks[q][1]])
                            if half == 3:
                                bk, kb = bks[q]
                                if part == "k":
                                    B.act(sqb4[q][:, 0:N], bk[:, 0:N], AF.Square, [kb], [("sqb4", q)])
                                    B.copy("act", kraw[q][:, 0:N], bk[:, 0:N], [kb], [("kraw", q)])
                                    pend1.append([tk1["n"] + 3, lambda q=q, h=cg * 4 + q, e0=e0, W=W, N=N: k_epi(q, h, e0, W, N)])
                                else:
                                    b = ecnt % 2
                                    ecnt += 1
                                    B.copy("act", vst[b][:, :, 0:128], bk[:, :].rearrange("p (h d) -> p h d", d=128),
                                           [kb], [("vst", b)])
                                    B.dma("act", v_s[e0 + q, :, cg * 4:(cg + 1) * 4, :], vst[b], [("vst", b)],
                                          [("v_s", e0 + q)], "st1")
                            tick1()
            for p in sorted(pend1, key=lambda p: p[0]):
                p[1]()
            pend1.clear()
        B.s.barrier()

    if 2 in phases:
        state["nwp"] = 4
        mixT_f = gt_h[:, 0:16384].rearrange("p (k t) -> p k t", t=512)
        qT_f = gt_h[:, 16384:24576].rearrange("p (h t) -> p h t", t=512)
        biga_flat = biga[:, :, :, :].rearrange("p a k t -> p (a k t)")
        kmac = [biga_flat[:, 0:4096].rearrange("p (h t) -> p h t", t=1024),
                gt_h[:, 24576:28672].rearrange("p (h t) -> p h t", t=1024)]
        vmac = [biga_flat[:, 4096:8224].rearrange("p (m f) -> p m f", f=516),
                gt_h[:, 28672:32800].rearrange("p (m f) -> p m f", f=516)]
        kvkeys = [[("biga", i) for i in range(4)], ["kvB"]]
        pTall = [gt_h[:, 32800 + i * 4096:32800 + (i + 1) * 4096].rearrange("p (m q) -> p m q", q=512)
                 for i in range(2)]
        mtok = [gt_h[:, 40992 + i * 512:40992 + (i + 1) * 512].rearrange("p (t d) -> p t d", d=128) for i in range(2)]
        ysm = gt_f[:, 21008:21520].rearrange("p (t d) -> p t d", d=128)
        nst = gt_f[:, 21520:21552]
        C.sqf = [sm_h[:, 0:512], sm_h[:, 512:1024]]
        C.sqb = [sm_b[:, 0:512], sm_b[:, 1024:1536]]
        C.rs = [sm_h[:, 1024:1536], sm_h[:, 1536:2048]]
        gl = [sm_h[:, 2048:2560], sm_h[:, 2560:3072]]
        sq = sm_h[:, 3072:3328]
        st4 = [sm_h[:, 3328:3336], sm_h[:, 3336:3344]]
        ao = [sm_h[:, 3344:3600].rearrange("p (h d) -> p h d", d=128),
              sm_h[:, 3600:3856].rearrange("p (h d) -> p h d", d=128)]
        gvt = sm_h[:, 3856:4112]
        gat = sm_h[:, 4112:4368]
        gbt = C.gtab[:, 0:2048]
        gtab_b = C.gtab.bitcast(BF16)
        bm_h = [gtab_b, sm_b]
        bm_off = [4096, 4368 * 2]
        bm_ps = [gtab_b[:, 0:1].ap[0][0], sm_b[:, 0:1].ap[0][0]]
        bo = 5648 * 2
        vnb = [sm_b[:, bo + i * 256:bo + (i + 1) * 256].rearrange("p (h d) -> p h d", d=128) for i in range(2)]
        mtk = [sm_b[:, bo + 512 + i * 256:bo + 512 + (i + 1) * 256].rearrange("p (h d) -> p h d", d=128)
               for i in range(2)]
        wsT = sm_b[:, 12320:14368].rearrange("p (h i) -> p h i", i=128)
        sqy = sm_h[:, 7184:7696]
        xblk = [C.sqf[0], C.sqf[1], gl[0], gl[1]]
        xkey = [("sqf", 0), ("sqf", 1), ("gl", 0), ("gl", 1)]
        hblk = C.rs
        gqs = sm_h[:, 7700:7701]
        B.ts("dve", gqs, pp[:, 0:1], SCALE, 0.0, ALU.mult, ALU.add, ["pp"], ["gqs"])

        B.dma("pool", wsT, wsT_d.ap(), [], ["wsT"], "c2")
        B.dma("sp", gbt, gh_d["g_b"].ap(), [], ["gtab"], "c2")
        mts = cfg.get("mix_mts", [0, 4, 8, 12, 16])
        cnt = {"e": 0, "g": 0}
        for j0 in mts:
            W = min(4, NMIX - j0)
            N = W * 128
            mixT = mixT_f[:, :, 0:N]
            qT = qT_f[:, :, 0:N]
            for i in range(W):
                B.dma("sp", biga[:, i], hnT_s[j0 + 2 + i], [("hnT_s", j0 + 2 + i)], [("biga", i)], "ld2")
            hk = [("biga", i) for i in range(W)]
            pend = []
            tk = {"n": 0}

            def tick():
                tk["n"] += 1
                due = sorted([p for p in pend if p[0] <= tk["n"]], key=lambda p: p[0])
                for p in due:
                    pend.remove(p)
                    p[1]()

            def flush():
                for p in sorted(pend, key=lambda p: p[0]):
                    p[1]()
                pend.clear()

            def gm_E1(hp, t, bk, kb):
                b = t % 2
                G, kG = gl[b], ("gl", b)
                s4, ks4 = st4[b], ("st4", b)
                B.act(G, bk[:, :], AF.Gelu, [kb], [kG])
                B.tt("dve", sq, G[:, 256:512], G[:, 256:512], ALU.mult, [kG], ["sq"])
                B.red("dve", s4[:, 0:2], sq.rearrange("p (h d) -> p h d", d=128), ALU.add, ["sq"], [ks4])
                B.ts("dve", s4[:, 2:4], s4[:, 0:2], 1.0 / HD, EPS, ALU.mult, ALU.add, [ks4], [ks4])
                rstd_from_ms(B, s4[:, 2:4], s4[:, 4:6], s4[:, 6:8], [ks4])
                for hh in range(2):
                    B.stt("dve", vnb[b][:, hh, :], G[:, 256 + hh * 128:384 + hh * 128], s4[:, 6 + hh:7 + hh],
                          gvt[:, hh * 128:(hh + 1) * 128], ALU.mult, ALU.mult, [kG, ks4, "gvt"], [("vnb", b)])

            def gm_E2(hp, t):
                b = t % 2
                G, kG = gl[b], ("gl", b)
                s4, ks4 = st4[b], ("st4", b)
                mb, kmb = C.pf[5], ("pf", 5)
                for hh in range(2):
                    B.mm(mb[:, hh * 128:(hh + 1) * 128], wsT[:, hp * 2 + hh, :], vnb[b][:, hh, :], True, True,
                         ["wsT", ("vnb", b)], [kmb])
                for hh in range(2):
                    B.stt("dve", ao[b][:, hh, :], mb[:, hh * 128:(hh + 1) * 128], pp[:, 2 + hp * 2 + hh:3 + hp * 2 + hh],
                          G[:, hh * 128:(hh + 1) * 128], ALU.add, ALU.mult, [kmb, kG, "pp"], [("ao", b)])
                B.tt("dve", sq.rearrange("p (h d) -> p h d", d=128), ao[b], ao[b], ALU.mult, [("ao", b)], ["sq"])
                B.red("dve", s4[:, 0:2], sq.rearrange("p (h d) -> p h d", d=128), ALU.add, ["sq"], [ks4])
                B.ts("dve", s4[:, 2:4], s4[:, 0:2], 1.0 / HD, EPS, ALU.mult, ALU.add, [ks4], [ks4])
                rstd_from_ms(B, s4[:, 2:4], s4[:, 4:6], s4[:, 6:8], [ks4])
                for hh in range(2):
                    B.stt("dve", mtk[b][:, hh, :], ao[b][:, hh, :], s4[:, 6 + hh:7 + hh],
                          gat[:, hh * 128:(hh + 1) * 128], ALU.mult, ALU.mult, [("ao", b), ks4, "gat"], [("mtk", b)])

            def gm_E3(hp, t):
                b = t % 2
                pi = t % 2
                for hh in range(2):
                    B.tr(C.psb[pi][:, hh * 128:(hh + 1) * 128], mtk[b][:, hh, :], C.ident[:, :],
                         [("mtk", b), "ident"], [("psb", pi)])
                B.copy("act", mixT[:, hp * 2:hp * 2 + 2, t * 128:(t + 1) * 128],
                       C.psb[pi][:, 0:256].rearrange("p (k t) -> p k t", t=128), [("psb", pi)], ["mixT"])

            def ld_gv(hp):
                B.dma("sp", gvt, gh_d["g_v"][:, hp * 256:(hp + 1) * 256], [], ["gvt"], "ld2")

            def ld_ga(hp):
                B.dma("sp", gat, gh_d["g_a"][:, hp * 256:(hp + 1) * 256], [], ["gat"], "ld2")

            nhp = cfg.get("n_hp", 8)
            for hp in range(nhp):
                if hp == 0:
                    ld_gv(0)
                    ld_ga(0)
                bks = [(C.pf[t], ("pf", t)) for t in range(W)]
                for half in range(4):
                    pc, kpc = wpnext()
                    B.dma("pool", pc[:, :, 0:256], wsrc(w_in, half * 1024, 8, hp * 256, 256), [], [kpc], "w2")
                    B.dma("pool", pc[:, :, 256:512], wsrc(w_in, half * 1024, 8, 2048 + hp * 256, 256), [], [kpc], "w2")
                    for t in range(W):
                        for kc in range(8):
                            B.mm(bks[t][0][:, :], biga[:, t, half * 8 + kc, :], pc[:, kc, :],
                                 half == 0 and kc == 0, half == 3 and kc == 7, [kpc, hk[t]], [bks[t][1]])
                        if half == 3 and t == 0:
                            T0 = tk["n"] + 1
                            sched = [(0, "E1", 0), (1, "E1", 1), (4, "E2", 0), (5, "E1", 2), (5, "E2", 1), (6, "E1", 3),
                                     (8, "E3", 0), (9, "E3", 1), (9, "E2", 2), (10, "E2", 3), (13, "E3", 2), (14, "E3", 3)]
                            for dt, kind, tt_ in sched:
                                if tt_ >= W:
                                    continue
                                if kind == "E1":
                                    pend.append([T0 + dt, lambda hp=hp, t=tt_, bk=bks[tt_][0], kb=bks[tt_][1]: gm_E1(hp, t, bk, kb)])
                                elif kind == "E2":
                                    pend.append([T0 + dt, lambda hp=hp, t=tt_: gm_E2(hp, t)])
                                else:
                                    pend.append([T0 + dt, lambda hp=hp, t=tt_: gm_E3(hp, t)])
                            if hp + 1 < nhp:
                                pend.append([T0 + 7, lambda hp=hp: ld_gv(hp + 1)])
                                pend.append([T0 + 11, lambda hp=hp: ld_ga(hp + 1)])
                        tick()
            flush()
            def bm_load(h):
                b = h % 2
                for t in range(W):
                    B.dma("pool", bm_h[b][:, bm_off[b] + t * 640:bm_off[b] + (t + 1) * 640].rearrange("p (c q) -> p c q", q=128),
                          btab_d[cls_of(j0 + t), h], [], [("bm", b)], "w2")

            bm_load(0)
            bm_load(1)
            alias = [("gl", 0), ("gl", 1), "sq", ("st4", 0), ("st4", 1), ("ao", 0), ("ao", 1), "gvt", "gat"]
            qraw = [sm_h[:, 2048 + i * 512:2048 + (i + 1) * 512] for i in range(4)]
            sqb4 = [sm_b[:, i * 512:(i + 1) * 512] for i in range(4)]
            sqk = [("sqf", 0), ("sqf", 0), ("sqf", 1), ("sqf", 1)]

            def q_epi(q, h):
                ob, kob = C.pf[5], ("pf", 5)
                rsq, krs = C.rs[q % 2], ("rs", q % 2)
                B.mm(ob[:, 0:N], C.ones[:, :], sqb4[q][:, 0:N], True, True, [sqk[q], "ones"], [kob])
                B.ts("dve", rsq[:, 0:N], ob[:, 0:N], 1.0 / HD, EPS, ALU.mult, ALU.add, [kob], [krs])
                B.act(rsq[:, 0:N], rsq[:, 0:N], AF.Sqrt, [krs], [krs])
                B.s.add("dve", lambda e, o=rsq[:, 0:N], i=rsq[:, 0:N]: e.reciprocal(o, i), [krs], [krs])
                B.stt("dve", qT[:, h, :], qraw[q][:, 0:N], gqs, rsq[:, 0:N], ALU.mult, ALU.mult,
                      alias + [krs, "gqs"], ["qT"])

            for cg in range(cfg.get("n_qcg", 4)):
                bks = [bank() for _ in range(4)]
                for half in range(4):
                    pc, kpc = wpnext()
                    B.dma("pool", pc[:, :, :], wsrc(w_in, half * 1024, 8, 4096 + cg * 512, 512), [], [kpc], "w2")
                    for q in range(4):
                        for kc in range(8):
                            B.mm(bks[q][0][:, 0:N].rearrange("p (w t) -> p w t", t=128),
                                 pc[:, kc, q * 128:(q + 1) * 128], biga[:, 0:W, half * 8 + kc, :],
                                 half == 0 and kc == 0, half == 3 and kc == 7, [kpc] + hk, [bks[q][1]])
                        if half == 3:
                            bk, kb = bks[q]
                            B.act(sqb4[q][:, 0:N], bk[:, 0:N], AF.Square, [kb], [sqk[q]])
                            B.copy("act", qraw[q][:, 0:N], bk[:, 0:N], [kb], alias)
                            pend.append([tk["n"] + 3, lambda q=q, h=cg * 4 + q: q_epi(q, h)])
                        tick()
            flush()
            NM = W + 4
            nast = {"s": 0}

            def load_kv(g):
                sl = g % 2
                B.dma("sp", kmac[sl][:, :, 0:NM * 128],
                      kT_s[g * 4:(g + 1) * 4, :, j0 * 128:(j0 + NM) * 128].rearrange("h p t -> p h t"),
                      [("kT_s", hx, j0 + i) for hx in range(g * 4, g * 4 + 4) for i in range(NM)], kvkeys[sl], "ld2")
                B.dma("sp", vmac[sl][:, 0:NM, :].rearrange("p m (h d) -> p m h d", d=129),
                      v_s[j0:j0 + NM, :, g * 4:(g + 1) * 4, :].rearrange("m p h d -> p m h d"),
                      [("v_s", j0 + i) for i in range(NM)], kvkeys[sl], "ld2")

            def na_S(h):
                g, hh, sl, b = h // 4, h % 4, (h // 4) % 2, h % 2
                for m in range(NM):
                    t_lo, t_hi = max(0, m - 4), min(W - 1, m)
                    nt = t_hi - t_lo + 1
                    sbi = nast["s"] % 3
                    nast["s"] += 1
                    bk, kb = C.pf[sbi], ("pf", sbi)
                    B.mm(bk[:, 0:nt * 128], kmac[sl][:, hh, m * 128:(m + 1) * 128], qT[:, h, t_lo * 128:(t_hi + 1) * 128],
                         True, False, kvkeys[sl] + ["qT"], [kb])
                    bap = bass.AP(bm_h[b], bm_off[b] + 128 * m + 512 * t_lo, [[bm_ps[b], 128], [512, nt], [1, 128]])
                    B.mm(bk[:, 0:nt * 128].rearrange("p (t q) -> p t q", q=128), C.ident[:, :], bap, False, True,
                         [("bm", b), "ident"], [kb])
                    B.act(pTall[b][:, m, 0:nt * 128], bk[:, 0:nt * 128], AF.Exp, [kb], [("pTall", b)])

            def na_PV(h):
                g, hh, sl, b = h // 4, h % 4, (h // 4) % 2, h % 2
                pvb = []
                for t in range(W):
                    if t % 2 == 0:
                        pvb.append((C.pf[3 + t // 2], ("pf", 3 + t // 2)))
                    bk, kb = pvb[-1]
                    for i in range(5):
                        m = t + i
                        t_lo = max(0, m - 4)
                        B.mm(bk[:, (t % 2) * 129:(t % 2) * 129 + 129], pTall[b][:, m, (t - t_lo) * 128:(t - t_lo + 1) * 128],
                             vmac[sl][:, m, hh * 129:(hh + 1) * 129], i == 0, i == 4, [("pTall", b)] + kvkeys[sl], [kb])
                for t in range(W):
                    bk, kb = pvb[t // 2]
                    c0 = (t % 2) * 129
                    B.s.add("dve", lambda e, o=nst[:, t:t + 1], i=bk[:, c0 + 128:c0 + 129]: e.reciprocal(o, i), [kb], ["nst"])
                    B.ts("dve", ysm[:, t, :], bk[:, c0:c0 + 128], nst[:, t:t + 1], 0.0, ALU.mult, ALU.add, [kb, "nst"], ["ysm"])
                yv = ysm[:, 0:W, :]
                B.tt("dve", sqy[:, 0:N].rearrange("p (t d) -> p t d", d=128), yv, yv, ALU.mult, ["ysm"], ["sqy"])
                B.red("dve", nst[:, 8:8 + W], sqy[:, 0:N].rearrange("p (t d) -> p t d", d=128), ALU.add, ["sqy"], ["nst"])
                B.ts("dve", nst[:, 16:16 + W], nst[:, 8:8 + W], 1.0 / HD, EPS, ALU.mult, ALU.add, ["nst"], ["nst"])
                rstd_from_ms(B, nst[:, 16:16 + W], nst[:, 24:24 + W], nst[:, 16:16 + W], ["nst"])
                for t in range(W):
                    B.stt("dve", mtok[b][:, t, :], ysm[:, t, :], nst[:, 16 + t:17 + t], gbt[:, h * 128:(h + 1) * 128],
                          ALU.mult, ALU.mult, ["ysm", "nst", "gtab"], [("mtok", b)])

            def na_T(h):
                b = h % 2
                pi = h % 2
                for t in range(W):
                    B.tr(C.psb[pi][:, t * 128:(t + 1) * 128], mtok[b][:, t, :], C.ident[:, :], [("mtok", b), "ident"],
                         [("psb", pi)])
                B.copy("act", mixT[:, 16 + h, :], C.psb[pi][:, 0:N], [("psb", pi)], ["mixT"])

            nah = cfg.get("n_nah", NH)
            pre = []
            for half in range(4):
                pc, kpc = wpnext()
                B.dma("pool", pc[:, :, :], wsrc(w_out, half * 1024, 8, 0, 512), [], [kpc], "w2")
                pre.append((pc, kpc))
            load_kv(0)
            for k in range(nah + 2):
                if 1 <= k and k + 1 < nah:
                    bm_load(k + 1)
                if k < nah:
                    na_S(k)
                if 1 <= k <= nah:
                    na_PV(k - 1)
                if k >= 2:
                    na_T(k - 2)
                if k % 4 == 0 and k // 4 + 1 < (nah + 3) // 4:
                    load_kv(k // 4 + 1)
            for cb in range(cfg.get("n_cb", 8)):
                bks = [bank() for _ in range(W)]
                for t in range(W):
                    j = j0 + t
                    B.dma("sp", xblk[t], x_ext[(j + 2) * 128:(j + 3) * 128, cb * 512:(cb + 1) * 512], [],
                          [xkey[t]], "ld2")
                for half in range(4):
                    if cb == 0:
                        pc, kpc = pre[half]
                    else:
                        pc, kpc = wpnext()
                        B.dma("pool", pc[:, :, :], wsrc(w_out, half * 1024, 8, cb * 512, 512), [], [kpc], "w2")
                    for t in range(W):
                        for kc in range(8):
                            B.mm(bks[t][0][:, :], mixT[:, half * 8 + kc, t * 128:(t + 1) * 128], pc[:, kc, :],
                                 half == 0 and kc == 0, half == 3 and kc == 7, [kpc, "mixT"], [bks[t][1]])
                for t in range(W):
                    j = j0 + t
                    b = (cb * W + t) % 2
                    B.tt("dve", hblk[b], bks[t][0][:, :], xblk[t], ALU.add, [bks[t][1], xkey[t]], [("rs", b)])
                    B.dma("act", h1_s[j * 128:(j + 1) * 128, cb * 512:(cb + 1) * 512], hblk[b], [("rs", b)],
                          [("h1_s", j)], "st2")
        B.s.barrier()

    if 25 in phases:
        B.dma("sp", C.gtab[:, :], gt_d["g_ffn"].ap(), [], ["gtab"], "c25")

        def after25(t):
            B.dma("act", hn2T_s[t], n_stg[t % 2], [("n_stg", t % 2)], [("hn2T_s", t)], "st25")
        C.after_tile = after25
        phase_norm_T(B, C, "p25", lambda t: h1_s[t * 128:(t + 1) * 128, :], lambda t: ("h1_s", t),
                     cfg.get("nt25", NMIX), lambda t, g: (n_stg[t % 2][:, g * 8:(g + 1) * 8, :], ("n_stg", t % 2)))
        C.after_tile = None
        B.s.barrier()

    if 3 in phases:
        state["nwp"] = 6
        gT = gt_h[:, :].rearrange("p (k t) -> p k t", t=512)
        uext = [sm_h[:, i * 520:i * 520 + 514] for i in range(2)]
        cacc = [sm_h[:, 1040 + i * 512:1040 + (i + 1) * 512] for i in range(2)]
        glb = sm_h[:, 2064:2576]
        h1b = [sm_h[:, 4736 + i * 512:4736 + (i + 1) * 512] for i in range(4)]
        h2b = [sm_h[:, 2576 + i * 512:2576 + (i + 1) * 512] for i in range(4)]
        hcol = sm_b[:, 9400:9464].rearrange("p (k t) -> p k t", t=2)
        hst = [sm_h[:, 6784:6792], sm_h[:, 6792:6800]]
        for w in cfg.get("ffn_wins", range(4)):
            for i in range(4):
                B.dma("sp", biga[:, i], hn2T_s[4 * w + 1 + i], [("hn2T_s", 4 * w + 1 + i)], [("biga", i)], "ld3")
            hk = [("biga", i) for i in range(4)]
            for side, jn, col in ((0, 4 * w, 127), (1, 4 * w + 5, 0)):
                pc, kpc = wpnext()
                nbv = pc[:, :, :].rearrange("p k c -> p (k c)").rearrange("p (k t) -> p k t", t=128)
                B.dma("sp", nbv, hn2T_s[jn], [("hn2T_s", jn)], [kpc], "ld3")
                B.copy("dve", hcol[:, :, side], nbv[:, :, col], [kpc], ["hcol"])
            mcol8 = pp[:, PP_MASK + 8 * w:PP_MASK + 8 * w + 8]
            for fg in range(cfg.get("n_fg", 43)):
                bks = [bank() for _ in range(4)]
                hb, khb = C.pf[5], ("pf", 5)
                for half in range(4):
                    pc, kpc = wpnext()
                    B.dma("pool", pc[:, :, 0:256], wsrc(w_up, half * 1024, 8, fg * 256, 256), [], [kpc], "w3")
                    B.dma("pool", pc[:, :, 256:512], wsrc(w_up, half * 1024, 8, DFF + fg * 256, 256), [], [kpc], "w3")
                    for q in range(4):
                        for kc in range(8):
                            st_, sp_ = (half == 0 and kc == 0), (half == 3 and kc == 7)
                            B.mm(bks[q][0][:, :].rearrange("p (w t) -> p w t", t=128), pc[:, kc, q * 128:(q + 1) * 128],
                                 biga[:, :, half * 8 + kc, :], st_, sp_, [kpc] + hk, [bks[q][1]])
                            B.mm(hb[:, 2 * q:2 * q + 2], pc[:, kc, q * 128:(q + 1) * 128], hcol[:, half * 8 + kc, :],
                                 st_ and q == 0, sp_ and q == 3, [kpc, "hcol"], [khb])
                hs = hst[fg % 2]
                B.tt("dve", hs, hb[:, 0:8], mcol8, ALU.mult, [khb, "pp"], [("hst", fg % 2)])
                for c in range(2):
                    fc = 2 * fg + c
                    for part in range(2):
                        q = part * 2 + c
                        bk, kb = bks[q]
                        ch = fc + part * NFC
                        cp = pp[:, PP_CPAR + ch * 4:PP_CPAR + ch * 4 + 4]
                        ue, kue = uext[part], ("uext", part)
                        ca, kca = cacc[part], ("cacc", part)
                        B.copy("act", ue[:, 1:513], bk[:, :], [kb], [kue])
                        B.copy("act", ue[:, 0:514:513], hs[:, 2 * q:2 * q + 2], [("hst", fg % 2)], [kue])
                        B.act(ca, bk[:, :], AF.Identity, [kb, "pp"], [kca], bias=cp[:, 3:4], scale=cp[:, 1:2])
                        B.stt("dve", ca, ue[:, 0:512], cp[:, 0:1], ca, ALU.mult, ALU.add, [kue, kca, "pp"], [kca])
                        B.stt("dve", ca, ue[:, 2:514], cp[:, 2:3], ca, ALU.mult, ALU.add, [kue, kca, "pp"], [kca])
                    B.act(glb, cacc[0], AF.Gelu, [("cacc", 0)], ["glb"])
                    B.tt("dve", gT[:, fc, :], glb, cacc[1], ALU.mult, ["glb", ("cacc", 1)], [("gT", fc)])
            nfc = cfg.get("n_fg", 43) * 2
            gk = [("gT", fc) for fc in range(nfc)]
            for cb in range(cfg.get("n_cb3", 8)):
                bks = [bank() for _ in range(4)]
                for t in range(4):
                    jm = 4 * w + 1 + t
                    B.dma("sp", h1b[t], h1_s[jm * 128:(jm + 1) * 128, cb * 512:(cb + 1) * 512], [("h1_s", jm)],
                          [("h1b", t)], "ld3")
                npc = (nfc + 7) // 8
                for pk in range(npc):
                    nk = min(8, nfc - pk * 8)
                    pc, kpc = wpnext()
                    B.dma("pool", pc[:, 0:nk, :], wsrc(w_down, pk * 1024, nk, cb * 512, 512), [], [kpc], "w3")
                    for t in range(4):
                        for k in range(nk):
                            kg = pk * 8 + k
                            B.mm(bks[t][0][:, :], gT[:, kg, t * 128:(t + 1) * 128], pc[:, k, :], kg == 0, kg == nfc - 1,
                                 [kpc, ("gT", kg)], [bks[t][1]])
                for t in range(4):
                    B.tt("dve", h2b[t], bks[t][0][:, :], h1b[t], ALU.add, [bks[t][1], ("h1b", t)], [("h2b", t)])
                    jo = 4 * w + t
                    B.dma("act", h2_s[jo * 128:(jo + 1) * 128, cb * 512:(cb + 1) * 512], h2b[t], [("h2b", t)],
                          [("h2_s", jo)], "st3")
        B.s.barrier()

    if 4 in phases:
        state["nwp"] = 4
        wps = gt_h[:, 24576:32768].rearrange("p (k c) -> p k c", c=D)
        pin = [gt_f[:, 16384 + i * 256:16384 + (i + 1) * 256] for i in range(2)]
        pbf = gt_h[:, 33792:34048]
        pTt = gt_h[:, 34048:35072].rearrange("p (t k q) -> p t k q", t=4, k=2)
        junk = gt_h[:, 35072:35584]
        ssq = sm_h[:, 0:32].rearrange("p (t c) -> p t c", c=8)
        rse = sm_h[:, 32:48]
        sgb = [sm_h[:, 64 + i * 512:64 + (i + 1) * 512] for i in range(2)] + \
              [sm_h[:, 5184 + i * 512:5184 + (i + 1) * 512] for i in range(2)]
        tmb = [sm_h[:, 1088 + i * 512:1088 + (i + 1) * 512] for i in range(2)] + \
              [sm_h[:, 6208 + i * 512:6208 + (i + 1) * 512] for i in range(2)]
        gpb = [sm_h[:, 2112 + i * 512:2112 + (i + 1) * 512] for i in range(2)]
        h2b = [sm_h[:, 3136 + i * 512:3136 + (i + 1) * 512] for i in range(4)]
        B.dma("sp", C.gtab[:, :], gt_d["g_ple"].ap(), [], ["gtab"], "c4")
        B.dma("pool", wps, w_p.ap().rearrange("(k p) c -> p k c", p=128), [], ["wps"], "w4")
        for w in cfg.get("ple_wins", range(4)):
            phase_norm_T(B, C, "p4", lambda t: h2_s[(4 * w + t) * 128:(4 * w + t + 1) * 128, :],
                         lambda t: ("h2_s", 4 * w + t), 4,
                         lambda t, g: (biga[:, t, g * 8:(g + 1) * 8, :], ("biga", t)))
            for t in range(4):
                jo = 4 * w + t
                b = t % 2
                B.dma("sp", pin[b], p_own[jo * 128:(jo + 1) * 128, :], [], [("pin", b)], "ld4")
                B.copy("dve", pbf, pin[b], [("pin", b)], ["pbf"])
                for k in range(2):
                    B.tr(C.psb[0][:, k * 128:(k + 1) * 128], pbf[:, k * 128:(k + 1) * 128], C.ident[:, :],
                         ["pbf", "ident"], [("psb", 0)])
                B.copy("act", pTt[:, t], C.psb[0][:, 0:256].rearrange("p (k q) -> p k q", q=128), [("psb", 0)],
                       [("pTt", t)])
                for cb in range(8):
                    bk, kb = bank()
                    for k in range(2):
                        B.mm(bk[:, :], pTt[:, t, k, :], wps[:, k, cb * 512:(cb + 1) * 512], k == 0, k == 1,
                             [("pTt", t), "wps"], [kb])
                    B.act(junk, bk[:, :], AF.Square, [kb], ["junk", ("ssq", t)], accum=ssq[:, t, cb:cb + 1])
                r4 = rse[:, t * 4:(t + 1) * 4]
                B.red("dve", r4[:, 0:1], ssq[:, t, :], ALU.add, [("ssq", t)], [("rse", t)])
                B.ts("dve", r4[:, 1:2], r4[:, 0:1], 1.0 / D, EPS, ALU.mult, ALU.add, [("rse", t)], [("rse", t)])
                rstd_from_ms(B, r4[:, 1:2], r4[:, 3:4], r4[:, 2:3], [("rse", t)])
            hk = [("biga", i) for i in range(4)]
            for cb in range(8):
                b = cb % 2
                B.dma("sp", gpb[b], gt_d["g_post"][:, cb * 512:(cb + 1) * 512], [], [("gpb", b)], "ld4")
                bks = [(C.pf[t], ("pf", t)) for t in range(4)]
                for t in range(4):
                    jo = 4 * w + t
                    B.dma("sp", h2b[t], h2_s[jo * 128:(jo + 1) * 128, cb * 512:(cb + 1) * 512], [("h2_s", jo)],
                          [("h2b", t)], "ld4")
                for half in range(4):
                    pc, kpc = wpnext()
                    B.dma("pool", pc[:, :, :], wsrc(w_gate, half * 1024, 8, cb * 512, 512), [], [kpc], "w4")
                    for t in range(4):
                        for kc in range(8):
                            B.mm(bks[t][0][:, :], biga[:, t, half * 8 + kc, :], pc[:, kc, :],
                                 half == 0 and kc == 0, half == 3 and kc == 7, [kpc, hk[t]], [bks[t][1]])
                for t in range(4):
                    jo = 4 * w + t
                    bb = t
                    B.act(sgb[bb], bks[t][0][:, :], AF.Sigmoid, [bks[t][1]], [("sgb", bb)])
                    eb, keb = C.pf[4 + t % 2], ("pf", 4 + t % 2)
                    for k in range(2):
                        B.mm(eb[:, :], pTt[:, t, k, :], wps[:, k, cb * 512:(cb + 1) * 512], k == 0, k == 1,
                             [("pTt", t), "wps"], [keb])
                    r4 = rse[:, t * 4:(t + 1) * 4]
                    B.stt("dve", tmb[bb], eb[:, :], r4[:, 2:3], gpb[b], ALU.mult, ALU.mult,
                          [keb, ("rse", t), ("gpb", b)], [("tmb", bb)])
                    B.tt("dve", tmb[bb], tmb[bb], sgb[bb], ALU.mult, [("tmb", bb), ("sgb", bb)], [("tmb", bb)])
                    B.tt("dve", tmb[bb], tmb[bb], h2b[t], ALU.add, [("tmb", bb), ("h2b", t)], [("tmb", bb)])
                    B.dma("sp", out_d[jo * 128:(jo + 1) * 128, cb * 512:(cb + 1) * 512], tmb[bb], [("tmb", bb)],
                          [("out", jo, cb)], "st4")
    else:
        B.dma("sp", out_d[0:128, 0:512], sm_h[:, 0:512], [], [("out", 0)], "st4")
    B.s.emit(nc, stack)
    return B


def _btab(rpb, c):
    out = np.full((5, NH, 128, 5, 128), MASKVAL, np.float32)
    rep = {0: 1, 1: 2, 2: 15, 3: 16, 4: 8}
    qi = np.arange(128)
    ks = np.arange(576)
    for cls, j in rep.items():
        r0 = 32 * c + 2 * (j - 1)
        if cls == 4:
            r0 = 64
            real_lo = r0 - 4
            slot_rows = real_lo + np.arange(9)
            actual = slot_rows.copy()
            mirrored = np.zeros(9, bool)
        else:
            slot_rows = (r0 - 4) + np.arange(9)
            actual = slot_rows.copy()
            mirrored = np.zeros(9, bool)
            for s in range(9):
                g = slot_rows[s]
                if g < 0:
                    actual[s] = 8 + g if g >= -4 else -1000
                    mirrored[s] = True
                elif g >= 128:
                    actual[s] = g - 8 if g <= 129 else -1000
                    mirrored[s] = True
        real_set = set(int(a) for a, m in zip(actual, mirrored) if not m)
        qrow = np.clip(r0 + qi // 64, 0, 127)
        qcol = qi % 64
        rs = np.clip(qrow - 4, 0, 120)
        cs = np.clip(qcol - 8, 0, 48)
        s_of = ks // 64
        ck = ks % 64
        ar = actual[s_of]
        ok_slot = np.array([(not mirrored[s]) or (int(actual[s]) not in real_set) for s in range(9)])[s_of]
        valid = (ok_slot[:, None] & (ar[:, None] >= rs[None, :]) & (ar[:, None] < rs[None, :] + 8)
                 & (ck[:, None] >= cs[None, :]) & (ck[:, None] < cs[None, :] + 16))
        dr = np.clip(ar[:, None] - qrow[None, :] + 7, 0, 14)
        dc = np.clip(ck[:, None] - qcol[None, :] + 15, 0, 30)
        vals = rpb[:, dr, dc]
        tab = np.where(valid[None], vals, np.float32(MASKVAL)).astype(np.float32)
        full = np.full((NH, 640, 128), MASKVAL, np.float32)
        full[:, :576] = tab
        out[cls] = full.reshape(NH, 5, 128, 128).transpose(0, 2, 1, 3)
    return out


def prep_core(inp, cid):
    b, c = cid // 4, cid % 4
    x = inp["x"][b]
    xe = np.zeros((NEXT * 128, D), np.float32)
    for er in range(44):
        g = 32 * c - 6 + er
        if g < 0:
            g = 8 + g if g >= -4 else None
        elif g >= 128:
            g = g - 8 if g <= 129 else None
        if g is not None:
            xe[er * 64:(er + 1) * 64] = x[g * 64:(g + 1) * 64]
    t0 = 2048 * c
    bc = lambda v, n: np.ascontiguousarray(np.broadcast_to(np.asarray(v, np.float32).reshape(1, n), (128, n)))
    pp = np.zeros((128, NPP), np.float32)
    pp[:, 0] = inp["q_norm_g"][0]
    pp[:, 1] = inp["k_norm_g"][0]
    pp[:, 2:18] = inp["gmlp_bs"][0].T
    cw = inp["conv_w"][0].reshape(3, 172, 128)
    cb = inp["conv_b"][0].reshape(172, 128)
    cp = np.concatenate([cw.transpose(2, 1, 0), cb.T[:, :, None]], axis=2)
    pp[:, PP_CPAR:PP_CPAR + 688] = cp.reshape(128, 688)
    m = np.ones((4, 2), np.float32)
    m[0, 0] = 0.0 if c == 0 else 1.0
    m[3, 1] = 0.0 if c == 3 else 1.0
    pp[:, PP_MASK:PP_MASK + 32] = np.tile(m[:, None, :], (1, 4, 1)).reshape(1, 32)
    return {
        "x_ext": xe,
        "p_own": np.ascontiguousarray(inp["p"][0, b, t0:t0 + NTOK]),
        "g_mix": bc(inp["norm_mix_g"][0], D), "g_ffn": bc(inp["norm_ffn_g"][0], D),
        "g_ple": bc(inp["norm_ple_g"][0], D), "g_post": bc(inp["ple_post_g"][0], D),
        "g_v": bc(inp["gmlp_v_g"][0].reshape(-1), 2048), "g_a": bc(inp["out_norm_a_g"][0].reshape(-1), 2048),
        "g_b": bc(inp["out_norm_b_g"][0].reshape(-1), 2048),
        "pp": pp,
        "wsT": np.ascontiguousarray(inp["gmlp_ws"][0].transpose(2, 0, 1)),
        "btab": _btab(np.asarray(inp["na_rpb"][0], np.float32), c),
        "ident": np.eye(128, dtype=np.float32),
        "w_in": inp["w_in"][0], "w_out": inp["w_out"][0], "w_up": inp["w_up"][0], "w_down": inp["w_down"][0],
        "w_gate": inp["w_ple_gate"][0], "w_p": inp["w_ple_proj"][0],
    }


def kernel(**inputs):
    from contextlib import ExitStack
    inp = {k: np.asarray(v) for k, v in inputs.items()}
    nc = bass.Bass("TRN2", target_bir_lowering=False)
    with ExitStack() as stack:
        build(nc, stack, {})
    in_maps = [prep_core(inp, cid) for cid in range(8)]
    res = run_bass_kernel_spmd(nc, in_maps, core_ids=list(range(8)))
    out = np.zeros((2, 8192, D), np.float32)
    for cid in range(8):
        b, c = cid // 4, cid % 4
        out[b, 2048 * c:2048 * (c + 1)] = np.asarray(res.results[cid]["out"])
    return out
```

```python
import numpy as np
import concourse.bass as bass
import concourse.mybir as mybir
from concourse.bass_utils import run_bass_kernel_spmd

F32 = mybir.dt.float32
BF16 = mybir.dt.bfloat16
AF = mybir.ActivationFunctionType
ALU = mybir.AluOpType
AX = mybir.AxisListType

D = 4096
KC = 32
NH = 16
HD = 128
DIN = 10240
DFF = 11008
NFC = 86
DPLE = 256
EPS = 1e-6
NEXT = 22
NMIX = 18
NTOK = 2048

ENGS = ("pe", "act", "dve", "pool", "sp")
EPOCH = 6000
DMA_RING = {"sp": 16, "pool": 8, "act": 4, "dve": 4, "pe": 4}


class _Op:
    __slots__ = ("eng", "fn", "deps", "dma", "sig", "sem", "cnt", "idx", "chan")


class Sched:
    def __init__(self):
        self.ops = []
        self.streams = {e: [] for e in ENGS}
        self.lastw = {}
        self.readers = {}
        self.chans = {}
        self.dma_last = {}
        self.bar = set()

    def add(self, eng, fn, reads=(), writes=(), dma=False, chan=None):
        op = _Op()
        op.eng = eng
        op.fn = fn
        op.dma = dma
        op.idx = len(self.ops)
        op.sig = False
        op.chan = chan
        deps = set()
        for k in tuple(reads) + tuple(writes):
            w = self.lastw.get(k)
            if w is not None:
                deps.add(w)
        for k in writes:
            r = self.readers.get(k)
            if r:
                deps.update(r.values())
        deps.discard(op.idx)
        deps |= self.bar
        op.deps = deps
        if dma:
            R = DMA_RING[eng]
            st = self.chans.setdefault(eng, [0])
            i = st[0]
            st[0] += 1
            op.sem = (eng, i % R)
            op.cnt = 16 * (i // R + 1)
            prev = self.dma_last.get(op.sem)
            if prev is not None:
                deps.add(prev)
            self.dma_last[op.sem] = op.idx
        for k in reads:
            r = self.readers.setdefault(k, {})
            rk = ("dma", op.idx) if dma else eng
            r[rk] = op.idx
        for k in writes:
            self.lastw[k] = op.idx
            self.readers[k] = {}
        self.ops.append(op)
        self.streams[eng].append(op)
        return op

    def barrier(self):
        b = set(self.dma_last.values())
        for e in ENGS:
            for o in reversed(self.streams[e]):
                if not o.dma:
                    b.add(o.idx)
                    break
        self.bar = b

    def emit(self, nc, stack):
        ops = self.ops
        for op in ops:
            for d in op.deps:
                dop = ops[d]
                if dop.dma:
                    continue
                if dop.eng == "pe" and op.eng == "pe" and not op.dma:
                    continue
                dop.sig = True
        eng_sems = {}
        for e in ENGS:
            n = sum(1 for o in self.streams[e] if o.sig and not o.dma)
            k = n // EPOCH + 1
            eng_sems[e] = [stack.enter_context(nc.semaphore(f"s_{e}_{i}")) for i in range(k)]
            c = 0
            for o in self.streams[e]:
                if o.sig and not o.dma:
                    o.sem = eng_sems[e][c // EPOCH]
                    o.cnt = c % EPOCH + 1
                    c += 1
        dsems = {}
        for o in ops:
            if o.dma:
                if o.sem not in dsems:
                    dsems[o.sem] = stack.enter_context(nc.semaphore(f"d_{o.sem[0]}_{o.sem[1]}"))
                o.sem = dsems[o.sem]
        block = stack.enter_context(nc.Block())

        def run_stream(e, engobj):
            waited = {}
            for o in self.streams[e]:
                for d in sorted(o.deps):
                    dop = ops[d]
                    if not dop.dma and dop.eng == "pe" and e == "pe" and not o.dma:
                        continue
                    key = id(dop.sem)
                    if waited.get(key, 0) >= dop.cnt:
                        continue
                    waited[key] = dop.cnt
                    engobj.wait_ge(dop.sem, dop.cnt)
                ins = o.fn(engobj)
                if o.dma:
                    ins.then_inc(o.sem, 16)
                elif o.sig:
                    ins.then_inc(o.sem, 1)
            last = {}
            for o in self.streams[e]:
                if o.dma:
                    last[id(o.sem)] = (o.sem, max(o.cnt, last.get(id(o.sem), (None, 0))[1]))
            for sem, cnt in last.values():
                engobj.wait_ge(sem, cnt)

        @block.tensor
        def _(eng):
            run_stream("pe", eng)

        @block.scalar
        def _(eng):
            run_stream("act", eng)

        @block.vector
        def _(eng):
            run_stream("dve", eng)

        @block.gpsimd
        def _(eng):
            run_stream("pool", eng)

        @block.sync
        def _(eng):
            run_stream("sp", eng)


class Builder:
    def __init__(self, nc, stack):
        self.nc = nc
        self.stack = stack
        self.s = Sched()
        self.uid = 0
        self.psum_rr = 0

    def sb(self, name, shape, dt):
        return self.stack.enter_context(self.nc.sbuf_tensor(name, list(shape), dt))

    def ps(self, name, shape, dt):
        return self.stack.enter_context(self.nc.psum_tensor(name, list(shape), dt))

    def dram(self, name, shape, dt, kind):
        return self.nc.dram_tensor(name, list(shape), dt, kind=kind)

    def dma(self, q, out, in_, r, w, chan):
        self.s.add(q, lambda e, o=out, i=in_: e.dma_start(out=o, in_=i), r, w, dma=True, chan=chan)

    def mm(self, out, lhsT, rhs, start, stop, r, w):
        self.s.add("pe", lambda e, o=out, l=lhsT, x=rhs, a=start, b=stop: e.matmul(o, l, x, start=a, stop=b), r, w)

    def tr(self, out, in_, ident, r, w):
        self.s.add("pe", lambda e, o=out, i=in_, d=ident: e.transpose(o, i, d), r, w)

    def act(self, out, in_, func, r, w, bias=None, scale=None, accum=None, eng="act"):
        def fn(e, o=out, i=in_, f=func, b=bias, sc=scale, ac=accum):
            kw = {}
            if b is not None:
                kw["bias"] = b
            if sc is not None:
                kw["scale"] = sc
            if ac is not None:
                kw["accum_out"] = ac
            return e.activation(o, i, f, **kw)
        self.s.add(eng, fn, r, w)

    def tt(self, eng, out, in0, in1, op, r, w):
        self.s.add(eng, lambda e, o=out, a=in0, b=in1, p=op: e.tensor_tensor(o, a, b, p), r, w)

    def ts(self, eng, out, in0, s1, s2, op0, op1, r, w, accum=None):
        def fn(e, o=out, a=in0, x=s1, y=s2, p=op0, q=op1, ac=accum):
            if q is None:
                return e.tensor_scalar(o, a, x, None, p)
            if ac is not None:
                return e.tensor_scalar(o, a, x, y, p, q, ac)
            return e.tensor_scalar(o, a, x, y, p, q)
        self.s.add(eng, fn, r, w)

    def stt(self, eng, out, in0, scalar, in1, op0, op1, r, w):
        self.s.add(eng, lambda e, o=out, a=in0, sc=scalar, b=in1, p=op0, q=op1:
                   e.scalar_tensor_tensor(o, a, sc, b, p, q), r, w)

    def red(self, eng, out, in_, op, r, w):
        self.s.add(eng, lambda e, o=out, i=in_, p=op: e.tensor_reduce(o, i, AX.X, p), r, w)

    def copy(self, eng, out, in_, r, w):
        if eng == "act":
            self.s.add(eng, lambda e, o=out, i=in_: e.copy(o, i), r, w)
        else:
            self.s.add(eng, lambda e, o=out, i=in_: e.tensor_copy(o, i), r, w)

    def memset(self, eng, ap, val, w):
        self.s.add(eng, lambda e, a=ap, v=val: e.memset(a, v), (), w)


def wsrc(W, r0, nkc, c0, ncol):
    return W[r0:r0 + 128 * nkc, c0:c0 + ncol].rearrange("(k p) c -> p k c", p=128)


SCALE = float(HD) ** -0.5
NPP = 18 + 172 * 4 + 32
PP_CPAR = 18
PP_MASK = 18 + 172 * 4
MASKVAL = -100.0


class Ctx:
    pass


def cls_of(j):
    return {1: 0, 2: 1, 15: 2, 16: 3}.get(j, 4)


def rstd_from_ms(B, ms_ap, tmp_ap, out_ap, keys):
    B.act(tmp_ap, ms_ap, AF.Sqrt, keys, keys)
    B.s.add("dve", lambda e, o=out_ap, i=tmp_ap: e.reciprocal(o, i), keys, keys)


def phase_norm_T(B, C, name, src_tile_ap, src_key, ntiles, dst_fn):
    xin, ss = C.n_xin, C.n_ss
    B.dma("sp", xin[0], src_tile_ap(0), [src_key(0)], [("n_xin", 0)], "ld" + name)
    for t in range(ntiles):
        b = t % 2
        xbf = C.n_xbf[b]
        kx = ("n_xin", b)
        if t + 1 < ntiles:
            B.dma("sp", xin[1 - b], src_tile_ap(t + 1), [src_key(t + 1)], [("n_xin", 1 - b)], "ld" + name)
        ksq = ("n_xbf", b)
        kss = ("n_ss", b)
        B.act(xbf, xin[b], AF.Square, [kx], [ksq, kss], accum=ss[b][:, 0:1])
        B.ts("dve", ss[b][:, 1:2], ss[b][:, 0:1], 1.0 / D, EPS, ALU.mult, ALU.add, [kss], [kss])
        rstd_from_ms(B, ss[b][:, 1:2], ss[b][:, 3:4], ss[b][:, 2:3], [kss])
        for hlf in range(2):
            sl = slice(hlf * 2048, (hlf + 1) * 2048)
            B.stt("dve", xbf[:, sl], xin[b][:, sl], ss[b][:, 2:3], C.gtab[:, sl], ALU.mult, ALU.mult,
                  [kx, kss, C.gtab_key], [ksq])
        for g in range(4):
            pi = (t * 4 + g) % 2
            pb = C.psb[pi]
            kp = ("psb", pi)
            for i in range(8):
                kc = g * 8 + i
                B.tr(pb[:, i * 128:(i + 1) * 128], xbf[:, kc * 128:(kc + 1) * 128], C.ident[:, :],
                     [ksq, "ident"], [kp])
            dst, kd = dst_fn(t, g)
            eng = "act" if g % 2 == 0 else "dve"
            B.copy(eng, dst, pb[:, :].rearrange("p (k t) -> p k t", t=128), [kp], [kd])
        if C.after_tile is not None:
            C.after_tile(t)


def fm_norm_epilogue(B, C, bank_ap, kbank, N, gcol, out_ap, kout, idx, gkey="pp"):
    b = idx % 2
    sqf, rs = C.sqb[b], C.rs[b]
    ks, kr = ("sqf", b), ("rs", b)
    B.act(sqf[:, 0:N], bank_ap, AF.Square, [kbank], [ks])
    ob = C.pf[5]
    B.mm(ob[:, 0:N], C.ones[:, :], sqf[:, 0:N], True, True, [ks, "ones"], [("pf", 5)])
    B.ts("dve", rs[:, 0:N], ob[:, 0:N], 1.0 / HD, EPS, ALU.mult, ALU.add, [("pf", 5)], [kr])
    B.act(rs[:, 0:N], rs[:, 0:N], AF.Sqrt, [kr], [kr])
    B.s.add("dve", lambda e, o=rs[:, 0:N], i=rs[:, 0:N]: e.reciprocal(o, i), [kr], [kr])
    B.stt("dve", out_ap, bank_ap, gcol, rs[:, 0:N], ALU.mult, ALU.mult, [kbank, kr, gkey], [kout])


def build(nc, stack, cfg):
    B = Builder(nc, stack)
    C = Ctx()
    dbg = cfg.get("debug", False)
    phases = cfg.get("phases", (0, 1, 2, 25, 3, 4))
    skind = "ExternalOutput" if dbg else "Internal"

    x_ext = B.dram("x_ext", [NEXT * 128, D], F32, "ExternalInput")
    p_own = B.dram("p_own", [NTOK, DPLE], F32, "ExternalInput")
    gt_d = {n: B.dram(n, [128, D], F32, "ExternalInput") for n in ("g_mix", "g_ffn", "g_ple", "g_post")}
    gh_d = {n: B.dram(n, [128, 2048], F32, "ExternalInput") for n in ("g_v", "g_a", "g_b")}
    pp_d = B.dram("pp", [128, NPP], F32, "ExternalInput")
    wsT_d = B.dram("wsT", [128, NH, 128], F32, "ExternalInput")
    btab_d = B.dram("btab", [5, NH, 128, 5, 128], F32, "ExternalInput")
    ident_d = B.dram("ident", [128, 128], F32, "ExternalInput")
    w_in = B.dram("w_in", [D, DIN], F32, "ExternalInput")
    w_out = B.dram("w_out", [D, D], F32, "ExternalInput")
    w_up = B.dram("w_up", [D, 2 * DFF], F32, "ExternalInput")
    w_down = B.dram("w_down", [DFF, D], F32, "ExternalInput")
    w_gate = B.dram("w_gate", [D, D], F32, "ExternalInput")
    w_p = B.dram("w_p", [DPLE, D], F32, "ExternalInput")
    hnT_s = B.dram("hnT_s", [NEXT, 128, KC, 128], BF16, skind)
    kT_s = B.dram("kT_s", [NH, 128, NEXT * 128], BF16, skind)
    v_s = B.dram("v_s", [NEXT, 128, NH, 129], BF16, skind)
    h1_s = B.dram("h1_s", [NMIX * 128, D], F32, skind)
    hn2T_s = B.dram("hn2T_s", [NMIX, 128, KC, 128], BF16, skind)
    h2_s = B.dram("h2_s", [NTOK, D], F32, skind)
    out_d = B.dram("out", [NTOK, D], F32, "ExternalOutput")

    biga = B.sb("biga", [128, 4, KC, 128], BF16)
    gt_h = B.sb("gt", [128, NFC * 512], BF16)
    gt_f = gt_h.bitcast(F32)
    wp = [B.sb(f"wp{i}", [128, 8, 512], BF16) for i in range(4)]
    C.gtab = B.sb("gtab", [128, D], F32)
    sm_h = B.sb("small", [128, 8192], F32)
    sm_b = sm_h.bitcast(BF16)
    C.ident = B.sb("ident_sb", [128, 128], BF16)
    C.ones = B.sb("ones_sb", [128, 128], BF16)
    pp = B.sb("pp_sb", [128, NPP], F32)
    C.pf = [B.ps(f"pf{i}", [128, 512], F32) for i in range(6)]
    C.psb = [B.ps(f"psb{i}", [128, 1024], BF16) for i in range(2)]
    C.after_tile = None
    C.gtab_key = "gtab"

    C.n_xin = [gt_f[:, 0:4096], gt_f[:, 4096:8192]]
    C.n_xbf = [gt_h[:, 16384:20480], gt_h[:, 20480:24576]]
    n_stg = [gt_h[:, 32768:36864].rearrange("p (k t) -> p k t", t=128),
             gt_h[:, 36864:40960].rearrange("p (k t) -> p k t", t=128)]
    C.n_ss = [sm_h[:, 8000:8004], sm_h[:, 8004:8008]]

    state = {"bank": 0, "wp": 0, "nwp": 4}
    _gb = C.gtab.bitcast(BF16)
    wpx = [_gb[:, 0:4096].rearrange("p (k c) -> p k c", c=512), _gb[:, 4096:8192].rearrange("p (k c) -> p k c", c=512)]

    def bank():
        i = state["bank"] % 5
        state["bank"] += 1
        return C.pf[i], ("pf", i)

    def wpnext():
        i = state["wp"] % state["nwp"]
        state["wp"] += 1
        if i >= 4:
            return wpx[i - 4], ("gtabq", i - 4)
        return wp[i], ("wp", i)

    B.dma("pool", C.ident[:, :], ident_d.ap(), [], ["ident"], "c")
    B.dma("sp", pp[:, :], pp_d.ap(), [], ["pp"], "c")
    B.memset("dve", C.ones[:, :], 1.0, ["ones"])

    if 0 in phases:
        gtab_keep = C.gtab
        C.gtab = gt_f[:, 12288:16384]
        C.gtab_key = "gtab0"
        B.dma("sp", C.gtab, gt_d["g_mix"].ap(), [], ["gtab0"], "c")
        nt0 = cfg.get("nt0", NEXT)

        def after0(t):
            B.dma("act", hnT_s[t], n_stg[t % 2], [("n_stg", t % 2)], [("hnT_s", t)], "st0")
        C.after_tile = after0
        phase_norm_T(B, C, "p0", lambda t: x_ext[t * 128:(t + 1) * 128, :], lambda t: ("x_ext", t), nt0,
                     lambda t, g: (n_stg[t % 2][:, g * 8:(g + 1) * 8, :], ("n_stg", t % 2)))
        C.after_tile = None
        C.gtab = gtab_keep
        C.gtab_key = "gtab"

    if 1 in phases:
        state["nwp"] = 6
        sqb4 = [sm_b[:, i * 512:(i + 1) * 512] for i in range(4)]
        rs4 = [sm_h[:, 1024 + i * 512:1024 + (i + 1) * 512] for i in range(4)]
        kraw = [sm_h[:, 3072 + i * 512:3072 + (i + 1) * 512] for i in range(4)]
        kst = [sm_b[:, 10240 + i * 512:10240 + (i + 1) * 512] for i in range(4)]
        vst = [sm_b[:, 12288 + i * 516:12288 + (i + 1) * 516].rearrange("p (h d) -> p h d", d=129) for i in range(2)]
        for i in range(2):
            B.memset("dve", vst[i][:, :, 128:129], 1.0, [("vst", i)])
        wins = cfg.get("kv_wins", [(0, 4), (4, 4), (8, 4), (12, 4), (16, 4), (20, 2)])
        ecnt = 0
        pend1 = []
        tk1 = {"n": 0}

        def tick1():
            tk1["n"] += 1
            due = sorted([p for p in pend1 if p[0] <= tk1["n"]], key=lambda p: p[0])
            for p in due:
                pend1.remove(p)
                p[1]()

        def k_epi(q, h, e0, W, N):
            ob, kob = C.pf[5], ("pf", 5)
            B.mm(ob[:, 0:N], C.ones[:, :], sqb4[q][:, 0:N], True, True, [("sqb4", q), "ones"], [kob])
            B.ts("dve", rs4[q][:, 0:N], ob[:, 0:N], 1.0 / HD, EPS, ALU.mult, ALU.add, [kob], [("rs4", q)])
            B.act(rs4[q][:, 0:N], rs4[q][:, 0:N], AF.Sqrt, [("rs4", q)], [("rs4", q)])
            B.s.add("dve", lambda e, o=rs4[q][:, 0:N], i=rs4[q][:, 0:N]: e.reciprocal(o, i), [("rs4", q)], [("rs4", q)])
            B.stt("dve", kst[q][:, 0:N], kraw[q][:, 0:N], pp[:, 1:2], rs4[q][:, 0:N], ALU.mult, ALU.mult,
                  [("kraw", q), ("rs4", q), "pp"], [("kst", q)])
            B.dma("sp", kT_s[h, :, e0 * 128:e0 * 128 + N], kst[q][:, 0:N], [("kst", q)],
                  [("kT_s", h, e0 + i) for i in range(W)], "st1")

        for (e0, W) in wins:
            N = W * 128
            for i in range(W):
                B.dma("sp", biga[:, i], hnT_s[e0 + i], [("hnT_s", e0 + i)], [("biga", i)], "ld1")
            hk = [("biga", i) for i in range(W)]
            for part in ("k", "v"):
                for cg in range(cfg.get("kv_cgs", 4)):
                    c0 = (6144 if part == "k" else 8192) + cg * 512
                    nb = 4 if part == "k" else W
                    bks = [bank() for _ in range(nb)]
                    for half in range(4):
                        pc, kpc = wpnext()
                        B.dma("pool", pc[:, :, :], wsrc(w_in, half * 1024, 8, c0, 512), [], [kpc, (kpc, "b")], "w1")
                        for q in range(nb):
                            for kc in range(8):
                                st_, sp_ = (half == 0 and kc == 0), (half == 3 and kc == 7)
                                if part == "k":
                                    B.mm(bks[q][0][:, 0:N].rearrange("p (w t) -> p w t", t=128),
                                         pc[:, kc, q * 128:(q + 1) * 128], biga[:, 0:W, half * 8 + kc, :],
                                         st_, sp_, [kpc, (kpc, "b")] + hk, [bks[q][1]])
                                else:
                                    B.mm(bks[q][0][:, :], biga[:, q, half * 8 + kc, :], pc[:, kc, :],
                                         st_, sp_, [kpc, (kpc, "b"), hk[q]], [bks[q][1]])
                            if half == 3:
                                bk, kb = bks[q]
                                if part == "k":
                                    B.act(sqb4[q][:, 0:N], bk[:, 0:N], AF.Square, [kb], [("sqb4", q)])
                                    B.copy("act", kraw[q][:, 0:N], bk[:, 0:N], [kb], [("kraw", q)])
                                    pend1.append([tk1["n"] + 3, lambda q=q, h=cg * 4 + q, e0=e0, W=W, N=N: k_epi(q, h, e0, W, N)])
                                else:
                                    b = ecnt % 2
                                    ecnt += 1
                                    B.copy("act", vst[b][:, :, 0:128], bk[:, :].rearrange("p (h d) -> p h d", d=128),
                                           [kb], [("vst", b)])
                                    B.dma("act", v_s[e0 + q, :, cg * 4:(cg + 1) * 4, :], vst[b], [("vst", b)],
                                          [("v_s", e0 + q)], "st1")
                            tick1()
            for p in sorted(pend1, key=lambda p: p[0]):
                p[1]()
            pend1.clear()
        B.s.barrier()

    if 2 in phases:
        state["nwp"] = 4
        mixT_f = gt_h[:, 0:16384].rearrange("p (k t) -> p k t", t=512)
        qT_f = gt_h[:, 16384:24576].rearrange("p (h t) -> p h t", t=512)
        biga_flat = biga[:, :, :, :].rearrange("p a k t -> p (a k t)")
        kmac = [biga_flat[:, 0:4096].rearrange("p (h t) -> p h t", t=1024),
                gt_h[:, 24576:28672].rearrange("p (h t) -> p h t", t=1024)]
        vmac = [biga_flat[:, 4096:8224].rearrange("p (m f) -> p m f", f=516),
                gt_h[:, 28672:32800].rearrange("p (m f) -> p m f", f=516)]
        kkeys = [[("biga", 0)], ["kvB_k"]]
        vkeys = [[("biga", 1), ("biga", 2)], ["kvB_v"]]
        pTall = [gt_h[:, 32800 + i * 4096:32800 + (i + 1) * 4096].rearrange("p (m q) -> p m q", q=512)
                 for i in range(2)]
        mtok = [gt_h[:, 40992 + i * 512:40992 + (i + 1) * 512].rearrange("p (t d) -> p t d", d=128) for i in range(2)]
        ysm = gt_f[:, 21008:21520].rearrange("p (t d) -> p t d", d=128)
        nst = gt_f[:, 21520:21552]
        C.sqf = [sm_h[:, 0:512], sm_h[:, 512:1024]]
        C.sqb = [sm_b[:, 0:512], sm_b[:, 1024:1536]]
        C.rs = [sm_h[:, 1024:1536], sm_h[:, 1536:2048]]
        gl = [sm_h[:, 2048:2560], sm_h[:, 2560:3072]]
        sq = sm_h[:, 3072:3328]
        st4 = [sm_h[:, 3328:3336], sm_h[:, 3336:3344]]
        ao = [sm_h[:, 3344:3600].rearrange("p (h d) -> p h d", d=128),
              sm_h[:, 3600:3856].rearrange("p (h d) -> p h d", d=128)]
        gvt = sm_h[:, 3856:4112]
        gat = sm_h[:, 4112:4368]
        gbt = C.gtab[:, 0:2048]
        gtab_b = C.gtab.bitcast(BF16)
        bm_h = [gtab_b, sm_b]
        bm_off = [4096, 4368 * 2]
        bm_ps = [gtab_b[:, 0:1].ap[0][0], sm_b[:, 0:1].ap[0][0]]
        bo = 5648 * 2
        vnb = [sm_b[:, bo + i * 256:bo + (i + 1) * 256].rearrange("p (h d) -> p h d", d=128) for i in range(2)]
        mtk = [sm_b[:, bo + 512 + i * 256:bo + 512 + (i + 1) * 256].rearrange("p (h d) -> p h d", d=128)
               for i in range(2)]
        wsT = sm_b[:, 12320:14368].rearrange("p (h i) -> p h i", i=128)
        sqy = sm_h[:, 7184:7696]
        xblk = [C.sqf[0], C.sqf[1], gl[0], gl[1]]
        xkey = [("sqf", 0), ("sqf", 1), ("gl", 0), ("gl", 1)]
        hblk = C.rs
        gqs = sm_h[:, 7700:7701]
        B.ts("dve", gqs, pp[:, 0:1], SCALE, 0.0, ALU.mult, ALU.add, ["pp"], ["gqs"])

        B.dma("pool", wsT, wsT_d.ap(), [], ["wsT"], "c2")
        B.dma("sp", gbt, gh_d["g_b"].ap(), [], ["gtab"], "c2")
        mts = cfg.get("mix_mts", [0, 4, 8, 12, 16])
        cnt = {"e": 0, "g": 0}
        for j0 in mts:
            W = min(4, NMIX - j0)
            N = W * 128
            mixT = mixT_f[:, :, 0:N]
            qT = qT_f[:, :, 0:N]
            for i in range(W):
                B.dma("sp", biga[:, i], hnT_s[j0 + 2 + i], [("hnT_s", j0 + 2 + i)], [("biga", i)], "ld2")
            hk = [("biga", i) for i in range(W)]
            pend = []
            tk = {"n": 0}

            def tick():
                tk["n"] += 1
                due = sorted([p for p in pend if p[0] <= tk["n"]], key=lambda p: p[0])
                for p in due:
                    pend.remove(p)
                    p[1]()

            def flush():
                for p in sorted(pend, key=lambda p: p[0]):
                    p[1]()
                pend.clear()

            def gm_E1(hp, t, bk, kb):
                b = t % 2
                G, kG = gl[b], ("gl", b)
                s4, ks4 = st4[b], ("st4", b)
                B.act(G, bk[:, :], AF.Gelu, [kb], [kG])
                B.tt("dve", sq, G[:, 256:512], G[:, 256:512], ALU.mult, [kG], ["sq"])
                B.red("dve", s4[:, 0:2], sq.rearrange("p (h d) -> p h d", d=128), ALU.add, ["sq"], [ks4])
                B.ts("dve", s4[:, 2:4], s4[:, 0:2], 1.0 / HD, EPS, ALU.mult, ALU.add, [ks4], [ks4])
                rstd_from_ms(B, s4[:, 2:4], s4[:, 4:6], s4[:, 6:8], [ks4])
                for hh in range(2):
                    B.stt("dve", vnb[b][:, hh, :], G[:, 256 + hh * 128:384 + hh * 128], s4[:, 6 + hh:7 + hh],
                          gvt[:, hh * 128:(hh + 1) * 128], ALU.mult, ALU.mult, [kG, ks4, "gvt"], [("vnb", b)])

            def gm_E2(hp, t):
                b = t % 2
                G, kG = gl[b], ("gl", b)
                s4, ks4 = st4[b], ("st4", b)
                mb, kmb = C.pf[5], ("pf", 5)
                for hh in range(2):
                    B.mm(mb[:, hh * 128:(hh + 1) * 128], wsT[:, hp * 2 + hh, :], vnb[b][:, hh, :], True, True,
                         ["wsT", ("vnb", b)], [kmb])
                for hh in range(2):
                    B.stt("dve", ao[b][:, hh, :], mb[:, hh * 128:(hh + 1) * 128], pp[:, 2 + hp * 2 + hh:3 + hp * 2 + hh],
                          G[:, hh * 128:(hh + 1) * 128], ALU.add, ALU.mult, [kmb, kG, "pp"], [("ao", b)])
                B.tt("dve", sq.rearrange("p (h d) -> p h d", d=128), ao[b], ao[b], ALU.mult, [("ao", b)], ["sq"])
                B.red("dve", s4[:, 0:2], sq.rearrange("p (h d) -> p h d", d=128), ALU.add, ["sq"], [ks4])
                B.ts("dve", s4[:, 2:4], s4[:, 0:2], 1.0 / HD, EPS, ALU.mult, ALU.add, [ks4], [ks4])
                rstd_from_ms(B, s4[:, 2:4], s4[:, 4:6], s4[:, 6:8], [ks4])
                for hh in range(2):
                    B.stt("dve", mtk[b][:, hh, :], ao[b][:, hh, :], s4[:, 6 + hh:7 + hh],
                          gat[:, hh * 128:(hh + 1) * 128], ALU.mult, ALU.mult, [("ao", b), ks4, "gat"], [("mtk", b)])

            def gm_E3(hp, t):
                b = t % 2
                pi = t % 2
                for hh in range(2):
                    B.tr(C.psb[pi][:, hh * 128:(hh + 1) * 128], mtk[b][:, hh, :], C.ident[:, :],
                         [("mtk", b), "ident"], [("psb", pi)])
                B.copy("act", mixT[:, hp * 2:hp * 2 + 2, t * 128:(t + 1) * 128],
                       C.psb[pi][:, 0:256].rearrange("p (k t) -> p k t", t=128), [("psb", pi)], ["mixT"])

            def ld_gv(hp):
                B.dma("sp", gvt, gh_d["g_v"][:, hp * 256:(hp + 1) * 256], [], ["gvt"], "ld2")

            def ld_ga(hp):
                B.dma("sp", gat, gh_d["g_a"][:, hp * 256:(hp + 1) * 256], [], ["gat"], "ld2")

            nhp = cfg.get("n_hp", 8)
            for hp in range(nhp):
                if hp == 0:
                    ld_gv(0)
                    ld_ga(0)
                bks = [(C.pf[t], ("pf", t)) for t in range(W)]
                for half in range(4):
                    pc, kpc = wpnext()
                    B.dma("pool", pc[:, :, 0:256], wsrc(w_in, half * 1024, 8, hp * 256, 256), [], [kpc], "w2")
                    B.dma("pool", pc[:, :, 256:512], wsrc(w_in, half * 1024, 8, 2048 + hp * 256, 256), [], [(kpc, "b")], "w2")
                    for t in range(W):
                        for kc in range(8):
                            B.mm(bks[t][0][:, :], biga[:, t, half * 8 + kc, :], pc[:, kc, :],
                                 half == 0 and kc == 0, half == 3 and kc == 7, [kpc, (kpc, "b"), hk[t]], [bks[t][1]])
                        if half == 3 and t == 0:
                            T0 = tk["n"] + 1
                            sched = [(0, "E1", 0), (1, "E1", 1), (4, "E2", 0), (5, "E1", 2), (5, "E2", 1), (6, "E1", 3),
                                     (8, "E3", 0), (9, "E3", 1), (9, "E2", 2), (10, "E2", 3), (13, "E3", 2), (14, "E3", 3)]
                            for dt, kind, tt_ in sched:
                                if tt_ >= W:
                                    continue
                                if kind == "E1":
                                    pend.append([T0 + dt, lambda hp=hp, t=tt_, bk=bks[tt_][0], kb=bks[tt_][1]: gm_E1(hp, t, bk, kb)])
                                elif kind == "E2":
                                    pend.append([T0 + dt, lambda hp=hp, t=tt_: gm_E2(hp, t)])
                                else:
                                    pend.append([T0 + dt, lambda hp=hp, t=tt_: gm_E3(hp, t)])
                            if hp + 1 < nhp:
                                pend.append([T0 + 7, lambda hp=hp: ld_gv(hp + 1)])
                                pend.append([T0 + 11, lambda hp=hp: ld_ga(hp + 1)])
                        tick()
            flush()
            def bm_load(h):
                b = h % 2
                for t in range(W):
                    B.dma("pool", bm_h[b][:, bm_off[b] + t * 640:bm_off[b] + (t + 1) * 640].rearrange("p (c q) -> p c q", q=128),
                          btab_d[cls_of(j0 + t), h], [], [("bm", b, t)], "w2")

            bm_load(0)
            bm_load(1)
            alias = [("gl", 0), ("gl", 1), "sq", ("st4", 0), ("st4", 1), ("ao", 0), ("ao", 1), "gvt", "gat"]
            qraw = [sm_h[:, 2048 + i * 512:2048 + (i + 1) * 512] for i in range(4)]
            sqb4 = [sm_b[:, i * 512:(i + 1) * 512] for i in range(4)]
            sqk = [("sqf", 0), ("sqf", 0), ("sqf", 1), ("sqf", 1)]

            def q_epi(q, h):
                ob, kob = C.pf[5], ("pf", 5)
                rsq, krs = C.rs[q % 2], ("rs", q % 2)
                B.mm(ob[:, 0:N], C.ones[:, :], sqb4[q][:, 0:N], True, True, [sqk[q], "ones"], [kob])
                B.ts("dve", rsq[:, 0:N], ob[:, 0:N], 1.0 / HD, EPS, ALU.mult, ALU.add, [kob], [krs])
                B.act(rsq[:, 0:N], rsq[:, 0:N], AF.Sqrt, [krs], [krs])
                B.s.add("dve", lambda e, o=rsq[:, 0:N], i=rsq[:, 0:N]: e.reciprocal(o, i), [krs], [krs])
                B.stt("dve", qT[:, h, :], qraw[q][:, 0:N], gqs, rsq[:, 0:N], ALU.mult, ALU.mult,
                      alias + [krs, "gqs"], ["qT"])

            for cg in range(cfg.get("n_qcg", 4)):
                bks = [bank() for _ in range(4)]
                for half in range(4):
                    pc, kpc = wpnext()
                    B.dma("pool", pc[:, :, :], wsrc(w_in, half * 1024, 8, 4096 + cg * 512, 512), [], [kpc, (kpc, "b")], "w2")
                    for q in range(4):
                        for kc in range(8):
                            B.mm(bks[q][0][:, 0:N].rearrange("p (w t) -> p w t", t=128),
                                 pc[:, kc, q * 128:(q + 1) * 128], biga[:, 0:W, half * 8 + kc, :],
                                 half == 0 and kc == 0, half == 3 and kc == 7, [kpc, (kpc, "b")] + hk, [bks[q][1]])
                        if half == 3:
                            bk, kb = bks[q]
                            B.act(sqb4[q][:, 0:N], bk[:, 0:N], AF.Square, [kb], [sqk[q]])
                            B.copy("act", qraw[q][:, 0:N], bk[:, 0:N], [kb], alias)
                            pend.append([tk["n"] + 3, lambda q=q, h=cg * 4 + q: q_epi(q, h)])
                        tick()
            flush()
            NM = W + 4
            nast = {"s": 0}

            def load_kv(g):
                sl = g % 2
                B.dma("sp", kmac[sl][:, :, 0:NM * 128],
                      kT_s[g * 4:(g + 1) * 4, :, j0 * 128:(j0 + NM) * 128].rearrange("h p t -> p h t"),
                      [("kT_s", hx, j0 + i) for hx in range(g * 4, g * 4 + 4) for i in range(NM)], kkeys[sl], "ld2")
                B.dma("sp", vmac[sl][:, 0:NM, :].rearrange("p m (h d) -> p m h d", d=129),
                      v_s[j0:j0 + NM, :, g * 4:(g + 1) * 4, :].rearrange("m p h d -> p m h d"),
                      [("v_s", j0 + i) for i in range(NM)], vkeys[sl], "ld2")

            def na_S(h):
                g, hh, sl, b = h // 4, h % 4, (h // 4) % 2, h % 2
                for m in range(NM):
                    t_lo, t_hi = max(0, m - 4), min(W - 1, m)
                    nt = t_hi - t_lo + 1
                    sbi = nast["s"] % 3
                    nast["s"] += 1
                    bk, kb = C.pf[sbi], ("pf", sbi)
                    B.mm(bk[:, 0:nt * 128], kmac[sl][:, hh, m * 128:(m + 1) * 128], qT[:, h, t_lo * 128:(t_hi + 1) * 128],
                         True, False, kkeys[sl] + ["qT"], [kb])
                    bap = bass.AP(bm_h[b], bm_off[b] + 128 * m + 512 * t_lo, [[bm_ps[b], 128], [512, nt], [1, 128]])
                    B.mm(bk[:, 0:nt * 128].rearrange("p (t q) -> p t q", q=128), C.ident[:, :], bap, False, True,
                         [("bm", b, tt_) for tt_ in range(W)] + ["ident"], [kb])
                    B.act(pTall[b][:, m, 0:nt * 128], bk[:, 0:nt * 128], AF.Exp, [kb], [("pTall", b)])

            def na_PV(h):
                g, hh, sl, b = h // 4, h % 4, (h // 4) % 2, h % 2
                pvb = []
                for t in range(W):
                    if t % 2 == 0:
                        pvb.append((C.pf[3 + t // 2], ("pf", 3 + t // 2)))
                    bk, kb = pvb[-1]
                    for i in range(5):
                        m = t + i
                        t_lo = max(0, m - 4)
                        B.mm(bk[:, (t % 2) * 129:(t % 2) * 129 + 129], pTall[b][:, m, (t - t_lo) * 128:(t - t_lo + 1) * 128],
                             vmac[sl][:, m, hh * 129:(hh + 1) * 129], i == 0, i == 4, [("pTall", b)] + vkeys[sl], [kb])
                for t in range(W):
                    bk, kb = pvb[t // 2]
                    c0 = (t % 2) * 129
                    B.s.add("dve", lambda e, o=nst[:, t:t + 1], i=bk[:, c0 + 128:c0 + 129]: e.reciprocal(o, i), [kb], ["nst"])
                    B.ts("dve", ysm[:, t, :], bk[:, c0:c0 + 128], nst[:, t:t + 1], 0.0, ALU.mult, ALU.add, [kb, "nst"], ["ysm"])
                yv = ysm[:, 0:W, :]
                B.tt("dve", sqy[:, 0:N].rearrange("p (t d) -> p t d", d=128), yv, yv, ALU.mult, ["ysm"], ["sqy"])
                B.red("dve", nst[:, 8:8 + W], sqy[:, 0:N].rearrange("p (t d) -> p t d", d=128), ALU.add, ["sqy"], ["nst"])
                B.ts("dve", nst[:, 16:16 + W], nst[:, 8:8 + W], 1.0 / HD, EPS, ALU.mult, ALU.add, ["nst"], ["nst"])
                rstd_from_ms(B, nst[:, 16:16 + W], nst[:, 24:24 + W], nst[:, 16:16 + W], ["nst"])
                for t in range(W):
                    B.stt("dve", mtok[b][:, t, :], ysm[:, t, :], nst[:, 16 + t:17 + t], gbt[:, h * 128:(h + 1) * 128],
                          ALU.mult, ALU.mult, ["ysm", "nst", "gtab"], [("mtok", b)])

            def na_T(h):
                b = h % 2
                pi = h % 2
                for t in range(W):
                    B.tr(C.psb[pi][:, t * 128:(t + 1) * 128], mtok[b][:, t, :], C.ident[:, :], [("mtok", b), "ident"],
                         [("psb", pi)])
                B.copy("act", mixT[:, 16 + h, :], C.psb[pi][:, 0:N], [("psb", pi)], ["mixT"])

            nah = cfg.get("n_nah", NH)
            pre = []
            for half in range(4):
                pc, kpc = wpnext()
                B.dma("pool", pc[:, :, :], wsrc(w_out, half * 1024, 8, 0, 512), [], [kpc, (kpc, "b")], "w2")
                pre.append((pc, kpc))
            load_kv(0)
            for k in range(nah + 2):
                if 1 <= k and k + 1 < nah:
                    bm_load(k + 1)
                if k < nah:
                    na_S(k)
                if 1 <= k <= nah:
                    na_PV(k - 1)
                if k >= 2:
                    na_T(k - 2)
                if k % 4 == 0 and k // 4 + 1 < (nah + 3) // 4:
                    load_kv(k // 4 + 1)
            for cb in range(cfg.get("n_cb", 8)):
                bks = [bank() for _ in range(W)]
                for t in range(W):
                    j = j0 + t
                    B.dma("sp", xblk[t], x_ext[(j + 2) * 128:(j + 3) * 128, cb * 512:(cb + 1) * 512], [],
                          [xkey[t]], "ld2")
                for half in range(4):
                    if cb == 0:
                        pc, kpc = pre[half]
                    else:
                        pc, kpc = wpnext()
                        B.dma("pool", pc[:, :, :], wsrc(w_out, half * 1024, 8, cb * 512, 512), [], [kpc, (kpc, "b")], "w2")
                    for t in range(W):
                        for kc in range(8):
                            B.mm(bks[t][0][:, :], mixT[:, half * 8 + kc, t * 128:(t + 1) * 128], pc[:, kc, :],
                                 half == 0 and kc == 0, half == 3 and kc == 7, [kpc, (kpc, "b"), "mixT"], [bks[t][1]])
                for t in range(W):
                    j = j0 + t
                    b = (cb * W + t) % 2
                    B.tt("dve", hblk[b], bks[t][0][:, :], xblk[t], ALU.add, [bks[t][1], xkey[t]], [("rs", b)])
                    B.dma("act", h1_s[j * 128:(j + 1) * 128, cb * 512:(cb + 1) * 512], hblk[b], [("rs", b)],
                          [("h1_s", j)], "st2")
        B.s.barrier()

    if 25 in phases:
        B.dma("sp", C.gtab[:, :], gt_d["g_ffn"].ap(), [], ["gtab"], "c25")

        def after25(t):
            B.dma("act", hn2T_s[t], n_stg[t % 2], [("n_stg", t % 2)], [("hn2T_s", t)], "st25")
        C.after_tile = after25
        phase_norm_T(B, C, "p25", lambda t: h1_s[t * 128:(t + 1) * 128, :], lambda t: ("h1_s", t),
                     cfg.get("nt25", NMIX), lambda t, g: (n_stg[t % 2][:, g * 8:(g + 1) * 8, :], ("n_stg", t % 2)))
        C.after_tile = None
        B.s.barrier()

    if 3 in phases:
        state["nwp"] = 6
        gT = gt_h[:, :].rearrange("p (k t) -> p k t", t=512)
        uext = [sm_h[:, i * 520:i * 520 + 514] for i in range(2)]
        cacc = [sm_h[:, 1040 + i * 512:1040 + (i + 1) * 512] for i in range(2)]
        glb = sm_h[:, 2064:2576]
        h1b = [sm_h[:, 4736 + i * 512:4736 + (i + 1) * 512] for i in range(4)]
        h2b = [sm_h[:, 2576 + i * 512:2576 + (i + 1) * 512] for i in range(4)]
        hcol = sm_b[:, 9400:9464].rearrange("p (k t) -> p k t", t=2)
        hst = [sm_h[:, 6784:6792], sm_h[:, 6792:6800]]
        for w in cfg.get("ffn_wins", range(4)):
            for i in range(4):
                B.dma("sp", biga[:, i], hn2T_s[4 * w + 1 + i], [("hn2T_s", 4 * w + 1 + i)], [("biga", i)], "ld3")
            hk = [("biga", i) for i in range(4)]
            for side, jn, col in ((0, 4 * w, 127), (1, 4 * w + 5, 0)):
                pc, kpc = wpnext()
                nbv = pc[:, :, :].rearrange("p k c -> p (k c)").rearrange("p (k t) -> p k t", t=128)
                B.dma("sp", nbv, hn2T_s[jn], [("hn2T_s", jn)], [kpc, (kpc, "b")], "ld3")
                B.copy("dve", hcol[:, :, side], nbv[:, :, col], [kpc, (kpc, "b")], ["hcol"])
            mcol8 = pp[:, PP_MASK + 8 * w:PP_MASK + 8 * w + 8]
            for fg in range(cfg.get("n_fg", 43)):
                bks = [bank() for _ in range(4)]
                hb, khb = C.pf[5], ("pf", 5)
                for half in range(4):
                    pc, kpc = wpnext()
                    B.dma("pool", pc[:, :, 0:256], wsrc(w_up, half * 1024, 8, fg * 256, 256), [], [kpc], "w3")
                    B.dma("pool", pc[:, :, 256:512], wsrc(w_up, half * 1024, 8, DFF + fg * 256, 256), [], [(kpc, "b")], "w3")
                    for q in range(4):
                        for kc in range(8):
                            st_, sp_ = (half == 0 and kc == 0), (half == 3 and kc == 7)
                            B.mm(bks[q][0][:, :].rearrange("p (w t) -> p w t", t=128), pc[:, kc, q * 128:(q + 1) * 128],
                                 biga[:, :, half * 8 + kc, :], st_, sp_, [kpc, (kpc, "b")] + hk, [bks[q][1]])
                            B.mm(hb[:, 2 * q:2 * q + 2], pc[:, kc, q * 128:(q + 1) * 128], hcol[:, half * 8 + kc, :],
                                 st_ and q == 0, sp_ and q == 3, [kpc, (kpc, "b"), "hcol"], [khb])
                hs = hst[fg % 2]
                B.tt("dve", hs, hb[:, 0:8], mcol8, ALU.mult, [khb, "pp"], [("hst", fg % 2)])
                for c in range(2):
                    fc = 2 * fg + c
                    for part in range(2):
                        q = part * 2 + c
                        bk, kb = bks[q]
                        ch = fc + part * NFC
                        cp = pp[:, PP_CPAR + ch * 4:PP_CPAR + ch * 4 + 4]
                        ue, kue = uext[part], ("uext", part)
                        ca, kca = cacc[part], ("cacc", part)
                        B.copy("act", ue[:, 1:513], bk[:, :], [kb], [kue])
                        B.copy("act", ue[:, 0:514:513], hs[:, 2 * q:2 * q + 2], [("hst", fg % 2)], [kue])
                        B.act(ca, bk[:, :], AF.Identity, [kb, "pp"], [kca], bias=cp[:, 3:4], scale=cp[:, 1:2])
                        B.stt("dve", ca, ue[:, 0:512], cp[:, 0:1], ca, ALU.mult, ALU.add, [kue, kca, "pp"], [kca])
                        B.stt("dve", ca, ue[:, 2:514], cp[:, 2:3], ca, ALU.mult, ALU.add, [kue, kca, "pp"], [kca])
                    B.act(glb, cacc[0], AF.Gelu, [("cacc", 0)], ["glb"])
                    B.tt("dve", gT[:, fc, :], glb, cacc[1], ALU.mult, ["glb", ("cacc", 1)], [("gT", fc)])
            nfc = cfg.get("n_fg", 43) * 2
            gk = [("gT", fc) for fc in range(nfc)]
            for cb in range(cfg.get("n_cb3", 8)):
                bks = [bank() for _ in range(4)]
                for t in range(4):
                    jm = 4 * w + 1 + t
                    B.dma("sp", h1b[t], h1_s[jm * 128:(jm + 1) * 128, cb * 512:(cb + 1) * 512], [("h1_s", jm)],
                          [("h1b", t)], "ld3")
                npc = (nfc + 7) // 8
                for pk in range(npc):
                    nk = min(8, nfc - pk * 8)
                    pc, kpc = wpnext()
                    B.dma("pool", pc[:, 0:nk, :], wsrc(w_down, pk * 1024, nk, cb * 512, 512), [], [kpc, (kpc, "b")], "w3")
                    for t in range(4):
                        for k in range(nk):
                            kg = pk * 8 + k
                            B.mm(bks[t][0][:, :], gT[:, kg, t * 128:(t + 1) * 128], pc[:, k, :], kg == 0, kg == nfc - 1,
                                 [kpc, (kpc, "b"), ("gT", kg)], [bks[t][1]])
                for t in range(4):
                    B.tt("dve", h2b[t], bks[t][0][:, :], h1b[t], ALU.add, [bks[t][1], ("h1b", t)], [("h2b", t)])
                    jo = 4 * w + t
                    B.dma("act", h2_s[jo * 128:(jo + 1) * 128, cb * 512:(cb + 1) * 512], h2b[t], [("h2b", t)],
                          [("h2_s", jo)], "st3")
        B.s.barrier()

    if 4 in phases:
        state["nwp"] = 4
        wps = gt_h[:, 24576:32768].rearrange("p (k c) -> p k c", c=D)
        pin = [gt_f[:, 16384 + i * 256:16384 + (i + 1) * 256] for i in range(2)]
        pbf = gt_h[:, 33792:34048]
        pTt = gt_h[:, 34048:35072].rearrange("p (t k q) -> p t k q", t=4, k=2)
        junk = gt_h[:, 35072:35584]
        ssq = sm_h[:, 0:32].rearrange("p (t c) -> p t c", c=8)
        rse = sm_h[:, 32:48]
        sgb = [sm_h[:, 64 + i * 512:64 + (i + 1) * 512] for i in range(2)] + \
              [sm_h[:, 5184 + i * 512:5184 + (i + 1) * 512] for i in range(2)]
        tmb = [sm_h[:, 1088 + i * 512:1088 + (i + 1) * 512] for i in range(2)] + \
              [sm_h[:, 6208 + i * 512:6208 + (i + 1) * 512] for i in range(2)]
        gpb = [sm_h[:, 2112 + i * 512:2112 + (i + 1) * 512] for i in range(2)]
        h2b = [sm_h[:, 3136 + i * 512:3136 + (i + 1) * 512] for i in range(4)]
        B.dma("sp", C.gtab[:, :], gt_d["g_ple"].ap(), [], ["gtab"], "c4")
        B.dma("pool", wps, w_p.ap().rearrange("(k p) c -> p k c", p=128), [], ["wps"], "w4")
        for w in cfg.get("ple_wins", range(4)):
            phase_norm_T(B, C, "p4", lambda t: h2_s[(4 * w + t) * 128:(4 * w + t + 1) * 128, :],
                         lambda t: ("h2_s", 4 * w + t), 4,
                         lambda t, g: (biga[:, t, g * 8:(g + 1) * 8, :], ("biga", t)))
            for t in range(4):
                jo = 4 * w + t
                b = t % 2
                B.dma("sp", pin[b], p_own[jo * 128:(jo + 1) * 128, :], [], [("pin", b)], "ld4")
                B.copy("dve", pbf, pin[b], [("pin", b)], ["pbf"])
                for k in range(2):
                    B.tr(C.psb[0][:, k * 128:(k + 1) * 128], pbf[:, k * 128:(k + 1) * 128], C.ident[:, :],
                         ["pbf", "ident"], [("psb", 0)])
                B.copy("act", pTt[:, t], C.psb[0][:, 0:256].rearrange("p (k q) -> p k q", q=128), [("psb", 0)],
                       [("pTt", t)])
                for cb in range(8):
                    bk, kb = bank()
                    for k in range(2):
                        B.mm(bk[:, :], pTt[:, t, k, :], wps[:, k, cb * 512:(cb + 1) * 512], k == 0, k == 1,
                             [("pTt", t), "wps"], [kb])
                    B.act(junk, bk[:, :], AF.Square, [kb], ["junk", ("ssq", t)], accum=ssq[:, t, cb:cb + 1])
                r4 = rse[:, t * 4:(t + 1) * 4]
                B.red("dve", r4[:, 0:1], ssq[:, t, :], ALU.add, [("ssq", t)], [("rse", t)])
                B.ts("dve", r4[:, 1:2], r4[:, 0:1], 1.0 / D, EPS, ALU.mult, ALU.add, [("rse", t)], [("rse", t)])
                rstd_from_ms(B, r4[:, 1:2], r4[:, 3:4], r4[:, 2:3], [("rse", t)])
            hk = [("biga", i) for i in range(4)]
            for cb in range(8):
                b = cb % 2
                B.dma("sp", gpb[b], gt_d["g_post"][:, cb * 512:(cb + 1) * 512], [], [("gpb", b)], "ld4")
                bks = [(C.pf[t], ("pf", t)) for t in range(4)]
                for t in range(4):
                    jo = 4 * w + t
                    B.dma("sp", h2b[t], h2_s[jo * 128:(jo + 1) * 128, cb * 512:(cb + 1) * 512], [("h2_s", jo)],
                          [("h2b", t)], "ld4")
                for half in range(4):
                    pc, kpc = wpnext()
                    B.dma("pool", pc[:, :, :], wsrc(w_gate, half * 1024, 8, cb * 512, 512), [], [kpc, (kpc, "b")], "w4")
                    for t in range(4):
                        for kc in range(8):
                            B.mm(bks[t][0][:, :], biga[:, t, half * 8 + kc, :], pc[:, kc, :],
                                 half == 0 and kc == 0, half == 3 and kc == 7, [kpc, (kpc, "b"), hk[t]], [bks[t][1]])
                for t in range(4):
                    jo = 4 * w + t
                    bb = t
                    B.act(sgb[bb], bks[t][0][:, :], AF.Sigmoid, [bks[t][1]], [("sgb", bb)])
                    eb, keb = C.pf[4 + t % 2], ("pf", 4 + t % 2)
                    for k in range(2):
                        B.mm(eb[:, :], pTt[:, t, k, :], wps[:, k, cb * 512:(cb + 1) * 512], k == 0, k == 1,
                             [("pTt", t), "wps"], [keb])
                    r4 = rse[:, t * 4:(t + 1) * 4]
                    B.stt("dve", tmb[bb], eb[:, :], r4[:, 2:3], gpb[b], ALU.mult, ALU.mult,
                          [keb, ("rse", t), ("gpb", b)], [("tmb", bb)])
                    B.tt("dve", tmb[bb], tmb[bb], sgb[bb], ALU.mult, [("tmb", bb), ("sgb", bb)], [("tmb", bb)])
                    B.tt("dve", tmb[bb], tmb[bb], h2b[t], ALU.add, [("tmb", bb), ("h2b", t)], [("tmb", bb)])
                    B.dma("sp", out_d[jo * 128:(jo + 1) * 128, cb * 512:(cb + 1) * 512], tmb[bb], [("tmb", bb)],
                          [("out", jo, cb)], "st4")
    else:
        B.dma("sp", out_d[0:128, 0:512], sm_h[:, 0:512], [], [("out", 0)], "st4")
    B.s.emit(nc, stack)
    return B


def _btab(rpb, c):
    out = np.full((5, NH, 128, 5, 128), MASKVAL, np.float32)
    rep = {0: 1, 1: 2, 2: 15, 3: 16, 4: 8}
    qi = np.arange(128)
    ks = np.arange(576)
    for cls, j in rep.items():
        r0 = 32 * c + 2 * (j - 1)
        if cls == 4:
            r0 = 64
            real_lo = r0 - 4
            slot_rows = real_lo + np.arange(9)
            actual = slot_rows.copy()
            mirrored = np.zeros(9, bool)
        else:
            slot_rows = (r0 - 4) + np.arange(9)
            actual = slot_rows.copy()
            mirrored = np.zeros(9, bool)
            for s in range(9):
                g = slot_rows[s]
                if g < 0:
                    actual[s] = 8 + g if g >= -4 else -1000
                    mirrored[s] = True
                elif g >= 128:
                    actual[s] = g - 8 if g <= 129 else -1000
                    mirrored[s] = True
        real_set = set(int(a) for a, m in zip(actual, mirrored) if not m)
        qrow = np.clip(r0 + qi // 64, 0, 127)
        qcol = qi % 64
        rs = np.clip(qrow - 4, 0, 120)
        cs = np.clip(qcol - 8, 0, 48)
        s_of = ks // 64
        ck = ks % 64
        ar = actual[s_of]
        ok_slot = np.array([(not mirrored[s]) or (int(actual[s]) not in real_set) for s in range(9)])[s_of]
        valid = (ok_slot[:, None] & (ar[:, None] >= rs[None, :]) & (ar[:, None] < rs[None, :] + 8)
                 & (ck[:, None] >= cs[None, :]) & (ck[:, None] < cs[None, :] + 16))
        dr = np.clip(ar[:, None] - qrow[None, :] + 7, 0, 14)
        dc = np.clip(ck[:, None] - qcol[None, :] + 15, 0, 30)
        vals = rpb[:, dr, dc]
        tab = np.where(valid[None], vals, np.float32(MASKVAL)).astype(np.float32)
        full = np.full((NH, 640, 128), MASKVAL, np.float32)
        full[:, :576] = tab
        out[cls] = full.reshape(NH, 5, 128, 128).transpose(0, 2, 1, 3)
    return out


def prep_core(inp, cid):
    b, c = cid // 4, cid % 4
    x = inp["x"][b]
    xe = np.zeros((NEXT * 128, D), np.float32)
    for er in range(44):
        g = 32 * c - 6 + er
        if g < 0:
            g = 8 + g if g >= -4 else None
        elif g >= 128:
            g = g - 8 if g <= 129 else None
        if g is not None:
            xe[er * 64:(er + 1) * 64] = x[g * 64:(g + 1) * 64]
    t0 = 2048 * c
    bc = lambda v, n: np.ascontiguousarray(np.broadcast_to(np.asarray(v, np.float32).reshape(1, n), (128, n)))
    pp = np.zeros((128, NPP), np.float32)
    pp[:, 0] = inp["q_norm_g"][0]
    pp[:, 1] = inp["k_norm_g"][0]
    pp[:, 2:18] = inp["gmlp_bs"][0].T
    cw = inp["conv_w"][0].reshape(3, 172, 128)
    cb = inp["conv_b"][0].reshape(172, 128)
    cp = np.concatenate([cw.transpose(2, 1, 0), cb.T[:, :, None]], axis=2)
    pp[:, PP_CPAR:PP_CPAR + 688] = cp.reshape(128, 688)
    m = np.ones((4, 2), np.float32)
    m[0, 0] = 0.0 if c == 0 else 1.0
    m[3, 1] = 0.0 if c == 3 else 1.0
    pp[:, PP_MASK:PP_MASK + 32] = np.tile(m[:, None, :], (1, 4, 1)).reshape(1, 32)
    return {
        "x_ext": xe,
        "p_own": np.ascontiguousarray(inp["p"][0, b, t0:t0 + NTOK]),
        "g_mix": bc(inp["norm_mix_g"][0], D), "g_ffn": bc(inp["norm_ffn_g"][0], D),
        "g_ple": bc(inp["norm_ple_g"][0], D), "g_post": bc(inp["ple_post_g"][0], D),
        "g_v": bc(inp["gmlp_v_g"][0].reshape(-1), 2048), "g_a": bc(inp["out_norm_a_g"][0].reshape(-1), 2048),
        "g_b": bc(inp["out_norm_b_g"][0].reshape(-1), 2048),
        "pp": pp,
        "wsT": np.ascontiguousarray(inp["gmlp_ws"][0].transpose(2, 0, 1)),
        "btab": _btab(np.asarray(inp["na_rpb"][0], np.float32), c),
        "ident": np.eye(128, dtype=np.float32),
        "w_in": inp["w_in"][0], "w_out": inp["w_out"][0], "w_up": inp["w_up"][0], "w_down": inp["w_down"][0],
        "w_gate": inp["w_ple_gate"][0], "w_p": inp["w_ple_proj"][0],
    }


def kernel(**inputs):
    from contextlib import ExitStack
    inp = {k: np.asarray(v) for k, v in inputs.items()}
    nc = bass.Bass("TRN2", target_bir_lowering=False)
    with ExitStack() as stack:
        build(nc, stack, {})
    in_maps = [prep_core(inp, cid) for cid in range(8)]
    res = run_bass_kernel_spmd(nc, in_maps, core_ids=list(range(8)))
    out = np.zeros((2, 8192, D), np.float32)
    for cid in range(8):
        b, c = cid // 4, cid % 4
        out[b, 2048 * c:2048 * (c + 1)] = np.asarray(res.results[cid]["out"])
    return out
```

```python
import numpy as np
import concourse.bass as bass
import concourse.mybir as mybir
from concourse.bass_utils import run_bass_kernel_spmd

F32 = mybir.dt.float32
BF16 = mybir.dt.bfloat16
AF = mybir.ActivationFunctionType
ALU = mybir.AluOpType
AX = mybir.AxisListType

D = 4096
KC = 32
NH = 16
HD = 128
DIN = 10240
DFF = 11008
NFC = 86
DPLE = 256
EPS = 1e-6
NEXT = 22
NMIX = 18
NTOK = 2048

ENGS = ("pe", "act", "dve", "pool", "sp")
EPOCH = 6000
DMA_RING = {"sp": 16, "pool": 8, "act": 4, "dve": 4, "pe": 4}


class _Op:
    __slots__ = ("eng", "fn", "deps", "dma", "sig", "sem", "cnt", "idx", "chan")


class Sched:
    def __init__(self):
        self.ops = []
        self.streams = {e: [] for e in ENGS}
        self.lastw = {}
        self.readers = {}
        self.chans = {}
        self.dma_last = {}
        self.bar = set()

    def add(self, eng, fn, reads=(), writes=(), dma=False, chan=None):
        op = _Op()
        op.eng = eng
        op.fn = fn
        op.dma = dma
        op.idx = len(self.ops)
        op.sig = False
        op.chan = chan
        deps = set()
        for k in tuple(reads) + tuple(writes):
            w = self.lastw.get(k)
            if w is not None:
                deps.add(w)
        for k in writes:
            r = self.readers.get(k)
            if r:
                deps.update(r.values())
        deps.discard(op.idx)
        deps |= self.bar
        op.deps = deps
        if dma:
            R = DMA_RING[eng]
            st = self.chans.setdefault(eng, [0])
            i = st[0]
            st[0] += 1
            op.sem = (eng, i % R)
            op.cnt = 16 * (i // R + 1)
            prev = self.dma_last.get(op.sem)
            if prev is not None:
                deps.add(prev)
            self.dma_last[op.sem] = op.idx
        for k in reads:
            r = self.readers.setdefault(k, {})
            rk = ("dma", op.idx) if dma else eng
            r[rk] = op.idx
        for k in writes:
            self.lastw[k] = op.idx
            self.readers[k] = {}
        self.ops.append(op)
        self.streams[eng].append(op)
        return op

    def barrier(self):
        b = set(self.dma_last.values())
        for e in ENGS:
            for o in reversed(self.streams[e]):
                if not o.dma:
                    b.add(o.idx)
                    break
        self.bar = b

    def emit(self, nc, stack):
        ops = self.ops
        for op in ops:
            for d in op.deps:
                dop = ops[d]
                if dop.dma:
                    continue
                if dop.eng == "pe" and op.eng == "pe" and not op.dma:
                    continue
                dop.sig = True
        eng_sems = {}
        for e in ENGS:
            n = sum(1 for o in self.streams[e] if o.sig and not o.dma)
            k = n // EPOCH + 1
            eng_sems[e] = [stack.enter_context(nc.semaphore(f"s_{e}_{i}")) for i in range(k)]
            c = 0
            for o in self.streams[e]:
                if o.sig and not o.dma:
                    o.sem = eng_sems[e][c // EPOCH]
                    o.cnt = c % EPOCH + 1
                    c += 1
        dsems = {}
        for o in ops:
            if o.dma:
                if o.sem not in dsems:
                    dsems[o.sem] = stack.enter_context(nc.semaphore(f"d_{o.sem[0]}_{o.sem[1]}"))
                o.sem = dsems[o.sem]
        block = stack.enter_context(nc.Block())

        def run_stream(e, engobj):
            waited = {}
            for o in self.streams[e]:
                for d in sorted(o.deps):
                    dop = ops[d]
                    if not dop.dma and dop.eng == "pe" and e == "pe" and not o.dma:
                        continue
                    key = id(dop.sem)
                    if waited.get(key, 0) >= dop.cnt:
                        continue
                    waited[key] = dop.cnt
                    engobj.wait_ge(dop.sem, dop.cnt)
                ins = o.fn(engobj)
                if o.dma:
                    ins.then_inc(o.sem, 16)
                elif o.sig:
                    ins.then_inc(o.sem, 1)
            last = {}
            for o in self.streams[e]:
                if o.dma:
                    last[id(o.sem)] = (o.sem, max(o.cnt, last.get(id(o.sem), (None, 0))[1]))
            for sem, cnt in last.values():
                engobj.wait_ge(sem, cnt)

        @block.tensor
        def _(eng):
            run_stream("pe", eng)

        @block.scalar
        def _(eng):
            run_stream("act", eng)

        @block.vector
        def _(eng):
            run_stream("dve", eng)

        @block.gpsimd
        def _(eng):
            run_stream("pool", eng)

        @block.sync
        def _(eng):
            run_stream("sp", eng)


class Builder:
    def __init__(self, nc, stack):
        self.nc = nc
        self.stack = stack
        self.s = Sched()
        self.uid = 0
        self.psum_rr = 0

    def sb(self, name, shape, dt):
        return self.stack.enter_context(self.nc.sbuf_tensor(name, list(shape), dt))

    def ps(self, name, shape, dt):
        return self.stack.enter_context(self.nc.psum_tensor(name, list(shape), dt))

    def dram(self, name, shape, dt, kind):
        return self.nc.dram_tensor(name, list(shape), dt, kind=kind)

    def dma(self, q, out, in_, r, w, chan):
        self.s.add(q, lambda e, o=out, i=in_: e.dma_start(out=o, in_=i), r, w, dma=True, chan=chan)

    def mm(self, out, lhsT, rhs, start, stop, r, w):
        self.s.add("pe", lambda e, o=out, l=lhsT, x=rhs, a=start, b=stop: e.matmul(o, l, x, start=a, stop=b), r, w)

    def tr(self, out, in_, ident, r, w):
        self.s.add("pe", lambda e, o=out, i=in_, d=ident: e.transpose(o, i, d), r, w)

    def act(self, out, in_, func, r, w, bias=None, scale=None, accum=None, eng="act"):
        def fn(e, o=out, i=in_, f=func, b=bias, sc=scale, ac=accum):
            kw = {}
            if b is not None:
                kw["bias"] = b
            if sc is not None:
                kw["scale"] = sc
            if ac is not None:
                kw["accum_out"] = ac
            return e.activation(o, i, f, **kw)
        self.s.add(eng, fn, r, w)

    def tt(self, eng, out, in0, in1, op, r, w):
        self.s.add(eng, lambda e, o=out, a=in0, b=in1, p=op: e.tensor_tensor(o, a, b, p), r, w)

    def ts(self, eng, out, in0, s1, s2, op0, op1, r, w, accum=None):
        def fn(e, o=out, a=in0, x=s1, y=s2, p=op0, q=op1, ac=accum):
            if q is None:
                return e.tensor_scalar(o, a, x, None, p)
            if ac is not None:
                return e.tensor_scalar(o, a, x, y, p, q, ac)
            return e.tensor_scalar(o, a, x, y, p, q)
        self.s.add(eng, fn, r, w)

    def stt(self, eng, out, in0, scalar, in1, op0, op1, r, w):
        self.s.add(eng, lambda e, o=out, a=in0, sc=scalar, b=in1, p=op0, q=op1:
                   e.scalar_tensor_tensor(o, a, sc, b, p, q), r, w)

    def red(self, eng, out, in_, op, r, w):
        self.s.add(eng, lambda e, o=out, i=in_, p=op: e.tensor_reduce(o, i, AX.X, p), r, w)

    def copy(self, eng, out, in_, r, w):
        if eng == "act":
            self.s.add(eng, lambda e, o=out, i=in_: e.copy(o, i), r, w)
        else:
            self.s.add(eng, lambda e, o=out, i=in_: e.tensor_copy(o, i), r, w)

    def memset(self, eng, ap, val, w):
        self.s.add(eng, lambda e, a=ap, v=val: e.memset(a, v), (), w)


def wsrc(W, r0, nkc, c0, ncol):
    return W[r0:r0 + 128 * nkc, c0:c0 + ncol].rearrange("(k p) c -> p k c", p=128)


SCALE = float(HD) ** -0.5
NPP = 18 + 172 * 4 + 32
PP_CPAR = 18
PP_MASK = 18 + 172 * 4
MASKVAL = -100.0


class Ctx:
    pass


def cls_of(j):
    return {1: 0, 2: 1, 15: 2, 16: 3}.get(j, 4)


def rstd_from_ms(B, ms_ap, tmp_ap, out_ap, keys):
    B.act(tmp_ap, ms_ap, AF.Sqrt, keys, keys)
    B.s.add("dve", lambda e, o=out_ap, i=tmp_ap: e.reciprocal(o, i), keys, keys)


def phase_norm_T(B, C, name, src_tile_ap, src_key, ntiles, dst_fn):
    xin, ss = C.n_xin, C.n_ss
    B.dma("sp", xin[0], src_tile_ap(0), [src_key(0)], [("n_xin", 0)], "ld" + name)
    for t in range(ntiles):
        b = t % 2
        xbf = C.n_xbf[b]
        kx = ("n_xin", b)
        if t + 1 < ntiles:
            B.dma("sp", xin[1 - b], src_tile_ap(t + 1), [src_key(t + 1)], [("n_xin", 1 - b)], "ld" + name)
        ksq = ("n_xbf", b)
        kss = ("n_ss", b)
        B.act(xbf, xin[b], AF.Square, [kx], [ksq, kss], accum=ss[b][:, 0:1])
        B.ts("dve", ss[b][:, 1:2], ss[b][:, 0:1], 1.0 / D, EPS, ALU.mult, ALU.add, [kss], [kss])
        rstd_from_ms(B, ss[b][:, 1:2], ss[b][:, 3:4], ss[b][:, 2:3], [kss])
        for hlf in range(2):
            sl = slice(hlf * 2048, (hlf + 1) * 2048)
            B.stt("dve", xbf[:, sl], xin[b][:, sl], ss[b][:, 2:3], C.gtab[:, sl], ALU.mult, ALU.mult,
                  [kx, kss, C.gtab_key], [ksq])
        for g in range(4):
            pi = (t * 4 + g) % 2
            pb = C.psb[pi]
            kp = ("psb", pi)
            for i in range(8):
                kc = g * 8 + i
                B.tr(pb[:, i * 128:(i + 1) * 128], xbf[:, kc * 128:(kc + 1) * 128], C.ident[:, :],
                     [ksq, "ident"], [kp])
            dst, kd = dst_fn(t, g)
            eng = "act" if g % 2 == 0 else "dve"
            B.copy(eng, dst, pb[:, :].rearrange("p (k t) -> p k t", t=128), [kp], [kd])
        if C.after_tile is not None:
            C.after_tile(t)


def fm_norm_epilogue(B, C, bank_ap, kbank, N, gcol, out_ap, kout, idx, gkey="pp"):
    b = idx % 2
    sqf, rs = C.sqb[b], C.rs[b]
    ks, kr = ("sqf", b), ("rs", b)
    B.act(sqf[:, 0:N], bank_ap, AF.Square, [kbank], [ks])
    ob = C.pf[5]
    B.mm(ob[:, 0:N], C.ones[:, :], sqf[:, 0:N], True, True, [ks, "ones"], [("pf", 5)])
    B.ts("dve", rs[:, 0:N], ob[:, 0:N], 1.0 / HD, EPS, ALU.mult, ALU.add, [("pf", 5)], [kr])
    B.act(rs[:, 0:N], rs[:, 0:N], AF.Sqrt, [kr], [kr])
    B.s.add("dve", lambda e, o=rs[:, 0:N], i=rs[:, 0:N]: e.reciprocal(o, i), [kr], [kr])
    B.stt("dve", out_ap, bank_ap, gcol, rs[:, 0:N], ALU.mult, ALU.mult, [kbank, kr, gkey], [kout])


def build(nc, stack, cfg):
    B = Builder(nc, stack)
    C = Ctx()
    dbg = cfg.get("debug", False)
    phases = cfg.get("phases", (0, 1, 2, 25, 3, 4))
    skind = "ExternalOutput" if dbg else "Internal"

    x_ext = B.dram("x_ext", [NEXT * 128, D], F32, "ExternalInput")
    p_own = B.dram("p_own", [NTOK, DPLE], F32, "ExternalInput")
    gt_d = {n: B.dram(n, [128, D], F32, "ExternalInput") for n in ("g_mix", "g_ffn", "g_ple", "g_post")}
    gh_d = {n: B.dram(n, [128, 2048], F32, "ExternalInput") for n in ("g_v", "g_a", "g_b")}
    pp_d = B.dram("pp", [128, NPP], F32, "ExternalInput")
    wsT_d = B.dram("wsT", [128, NH, 128], F32, "ExternalInput")
    btab_d = B.dram("btab", [5, NH, 128, 5, 128], F32, "ExternalInput")
    ident_d = B.dram("ident", [128, 128], F32, "ExternalInput")
    w_in = B.dram("w_in", [D, DIN], F32, "ExternalInput")
    w_out = B.dram("w_out", [D, D], F32, "ExternalInput")
    w_up = B.dram("w_up", [D, 2 * DFF], F32, "ExternalInput")
    w_down = B.dram("w_down", [DFF, D], F32, "ExternalInput")
    w_gate = B.dram("w_gate", [D, D], F32, "ExternalInput")
    w_p = B.dram("w_p", [DPLE, D], F32, "ExternalInput")
    hnT_s = B.dram("hnT_s", [NEXT, 128, KC, 128], BF16, skind)
    kT_s = B.dram("kT_s", [NH, 128, NEXT * 128], BF16, skind)
    v_s = B.dram("v_s", [NEXT, 128, NH, 129], BF16, skind)
    h1_s = B.dram("h1_s", [NMIX * 128, D], F32, skind)
    hn2T_s = B.dram("hn2T_s", [NMIX, 128, KC, 128], BF16, skind)
    h2_s = B.dram("h2_s", [NTOK, D], F32, skind)
    out_d = B.dram("out", [NTOK, D], F32, "ExternalOutput")

    biga = B.sb("biga", [128, 4, KC, 128], BF16)
    gt_h = B.sb("gt", [128, NFC * 512], BF16)
    gt_f = gt_h.bitcast(F32)
    wp = [B.sb(f"wp{i}", [128, 8, 512], BF16) for i in range(4)]
    C.gtab = B.sb("gtab", [128, D], F32)
    sm_h = B.sb("small", [128, 8192], F32)
    sm_b = sm_h.bitcast(BF16)
    C.ident = B.sb("ident_sb", [128, 128], BF16)
    C.ones = B.sb("ones_sb", [128, 128], BF16)
    pp = B.sb("pp_sb", [128, NPP], F32)
    C.pf = [B.ps(f"pf{i}", [128, 512], F32) for i in range(6)]
    C.psb = [B.ps(f"psb{i}", [128, 1024], BF16) for i in range(2)]
    C.after_tile = None
    C.gtab_key = "gtab"

    C.n_xin = [gt_f[:, 0:4096], gt_f[:, 4096:8192]]
    C.n_xbf = [gt_h[:, 16384:20480], gt_h[:, 20480:24576]]
    n_stg = [gt_h[:, 32768:36864].rearrange("p (k t) -> p k t", t=128),
             gt_h[:, 36864:40960].rearrange("p (k t) -> p k t", t=128)]
    C.n_ss = [sm_h[:, 8000:8004], sm_h[:, 8004:8008]]

    state = {"bank": 0, "wp": 0, "nwp": 4}
    _gb = C.gtab.bitcast(BF16)
    wpx = [_gb[:, 0:4096].rearrange("p (k c) -> p k c", c=512), _gb[:, 4096:8192].rearrange("p (k c) -> p k c", c=512)]

    def bank():
        i = state["bank"] % 5
        state["bank"] += 1
        return C.pf[i], ("pf", i)

    def wpnext():
        i = state["wp"] % state["nwp"]
        state["wp"] += 1
        if i >= 4:
            return wpx[i - 4], ("gtabq", i - 4)
        return wp[i], ("wp", i)

    B.dma("pool", C.ident[:, :], ident_d.ap(), [], ["ident"], "c")
    B.dma("sp", pp[:, :], pp_d.ap(), [], ["pp"], "c")
    B.memset("dve", C.ones[:, :], 1.0, ["ones"])

    if 0 in phases:
        gtab_keep = C.gtab
        C.gtab = gt_f[:, 12288:16384]
        C.gtab_key = "gtab0"
        B.dma("sp", C.gtab, gt_d["g_mix"].ap(), [], ["gtab0"], "c")
        nt0 = cfg.get("nt0", NEXT)

        def after0(t):
            B.dma("act", hnT_s[t], n_stg[t % 2], [("n_stg", t % 2)], [("hnT_s", t)], "st0")
        C.after_tile = after0
        phase_norm_T(B, C, "p0", lambda t: x_ext[t * 128:(t + 1) * 128, :], lambda t: ("x_ext", t), nt0,
                     lambda t, g: (n_stg[t % 2][:, g * 8:(g + 1) * 8, :], ("n_stg", t % 2)))
        C.after_tile = None
        C.gtab = gtab_keep
        C.gtab_key = "gtab"

    if 1 in phases:
        state["nwp"] = 6
        sqb4 = [sm_b[:, i * 512:(i + 1) * 512] for i in range(4)]
        rs4 = [sm_h[:, 1024 + i * 512:1024 + (i + 1) * 512] for i in range(4)]
        kraw = [sm_h[:, 3072 + i * 512:3072 + (i + 1) * 512] for i in range(4)]
        kst = [sm_b[:, 10240 + i * 512:10240 + (i + 1) * 512] for i in range(4)]
        vst = [sm_b[:, 12288 + i * 516:12288 + (i + 1) * 516].rearrange("p (h d) -> p h d", d=129) for i in range(2)]
        for i in range(2):
            B.memset("dve", vst[i][:, :, 128:129], 1.0, [("vst", i)])
        wins = cfg.get("kv_wins", [(0, 4), (4, 4), (8, 4), (12, 4), (16, 4), (20, 2)])
        ecnt = 0
        pend1 = []
        tk1 = {"n": 0}

        def tick1():
            tk1["n"] += 1
            due = sorted([p for p in pend1 if p[0] <= tk1["n"]], key=lambda p: p[0])
            for p in due:
                pend1.remove(p)
                p[1]()

        def k_epi(q, h, e0, W, N):
            ob, kob = C.pf[5], ("pf", 5)
            B.mm(ob[:, 0:N], C.ones[:, :], sqb4[q][:, 0:N], True, True, [("sqb4", q), "ones"], [kob])
            B.ts("dve", rs4[q][:, 0:N], ob[:, 0:N], 1.0 / HD, EPS, ALU.mult, ALU.add, [kob], [("rs4", q)])
            B.act(rs4[q][:, 0:N], rs4[q][:, 0:N], AF.Sqrt, [("rs4", q)], [("rs4", q)])
            B.s.add("dve", lambda e, o=rs4[q][:, 0:N], i=rs4[q][:, 0:N]: e.reciprocal(o, i), [("rs4", q)], [("rs4", q)])
            B.stt("dve", kst[q][:, 0:N], kraw[q][:, 0:N], pp[:, 1:2], rs4[q][:, 0:N], ALU.mult, ALU.mult,
                  [("kraw", q), ("rs4", q), "pp"], [("kst", q)])
            B.dma("sp", kT_s[h, :, e0 * 128:e0 * 128 + N], kst[q][:, 0:N], [("kst", q)],
                  [("kT_s", h, e0 + i) for i in range(W)], "st1")

        for (e0, W) in wins:
            N = W * 128
            for i in range(W):
                B.dma("sp", biga[:, i], hnT_s[e0 + i], [("hnT_s", e0 + i)], [("biga", i)], "ld1")
            hk = [("biga", i) for i in range(W)]
            for part in ("k", "v"):
                for cg in range(cfg.get("kv_cgs", 4)):
                    c0 = (6144 if part == "k" else 8192) + cg * 512
                    nb = 4 if part == "k" else W
                    bks = [bank() for _ in range(nb)]
                    for half in range(4):
                        pc, kpc = wpnext()
                        B.dma("pool", pc[:, :, :], wsrc(w_in, half * 1024, 8, c0, 512), [], [kpc, (kpc, "b")], "w1")
                        for q in range(nb):
                            for kc in range(8):
                                st_, sp_ = (half == 0 and kc == 0), (half == 3 and kc == 7)
                                if part == "k":
                                    B.mm(bks[q][0][:, 0:N].rearrange("p (w t) -> p w t", t=128),
                                         pc[:, kc, q * 128:(q + 1) * 128], biga[:, 0:W, half * 8 + kc, :],
                                         st_, sp_, [kpc, (kpc, "b")] + hk, [bks[q][1]])
                                else:
                                    B.mm(bks[q][0][:, :], biga[:, q, half * 8 + kc, :], pc[:, kc, :],
                                         st_, sp_, [kpc, (kpc, "b"), hk[q]], [bks[q][1]])
                            if half == 3:
                                bk, kb = bks[q]
                                if part == "k":
                                    B.act(sqb4[q][:, 0:N], bk[:, 0:N], AF.Square, [kb], [("sqb4", q)])
                                    B.copy("act", kraw[q][:, 0:N], bk[:, 0:N], [kb], [("kraw", q)])
                                    pend1.append([tk1["n"] + 3, lambda q=q, h=cg * 4 + q, e0=e0, W=W, N=N: k_epi(q, h, e0, W, N)])
                                else:
                                    b = ecnt % 2
                                    ecnt += 1
                                    B.copy("act", vst[b][:, :, 0:128], bk[:, :].rearrange("p (h d) -> p h d", d=128),
                                           [kb], [("vst", b)])
                                    B.dma("act", v_s[e0 + q, :, cg * 4:(cg + 1) * 4, :], vst[b], [("vst", b)],
                                          [("v_s", e0 + q)], "st1")
                            tick1()
            for p in sorted(pend1, key=lambda p: p[0]):
                p[1]()
            pend1.clear()
        B.s.barrier()

    if 2 in phases:
        state["nwp"] = 4
        mixT_f = gt_h[:, 0:16384].rearrange("p (k t) -> p k t", t=512)
        qT_f = gt_h[:, 16384:24576].rearrange("p (h t) -> p h t", t=512)
        biga_flat = biga[:, :, :, :].rearrange("p a k t -> p (a k t)")
        kmac = [biga_flat[:, 0:4096].rearrange("p (h t) -> p h t", t=1024),
                gt_h[:, 24576:28672].rearrange("p (h t) -> p h t", t=1024)]
        vmac = [biga_flat[:, 4096:8224].rearrange("p (m f) -> p m f", f=516),
                gt_h[:, 28672:32800].rearrange("p (m f) -> p m f", f=516)]
        kkeys = [[("biga", 0)], ["kvB_k"]]
        vkeys = [[("biga", 1), ("biga", 2)], ["kvB_v"]]
        pTall = [gt_h[:, 32800 + i * 4096:32800 + (i + 1) * 4096].rearrange("p (m q) -> p m q", q=512)
                 for i in range(2)]
        mtok = [gt_h[:, 40992 + i * 512:40992 + (i + 1) * 512].rearrange("p (t d) -> p t d", d=128) for i in range(2)]
        ysm = gt_f[:, 21008:21520].rearrange("p (t d) -> p t d", d=128)
        nst = gt_f[:, 21520:21552]
        C.sqf = [sm_h[:, 0:512], sm_h[:, 512:1024]]
        C.sqb = [sm_b[:, 0:512], sm_b[:, 1024:1536]]
        C.rs = [sm_h[:, 1024:1536], sm_h[:, 1536:2048]]
        gl = [sm_h[:, 2048:2560], sm_h[:, 2560:3072]]
        sq = sm_h[:, 3072:3328]
        st4 = [sm_h[:, 3328:3336], sm_h[:, 3336:3344]]
        ao = [sm_h[:, 3344:3600].rearrange("p (h d) -> p h d", d=128),
              sm_h[:, 3600:3856].rearrange("p (h d) -> p h d", d=128)]
        gvt = sm_h[:, 3856:4112]
        gat = sm_h[:, 4112:4368]
        gbt = C.gtab[:, 0:2048]
        gtab_b = C.gtab.bitcast(BF16)
        bm_h = [gtab_b, sm_b]
        bm_off = [4096, 4368 * 2]
        bm_ps = [gtab_b[:, 0:1].ap[0][0], sm_b[:, 0:1].ap[0][0]]
        bo = 5648 * 2
        vnb = [sm_b[:, bo + i * 256:bo + (i + 1) * 256].rearrange("p (h d) -> p h d", d=128) for i in range(2)]
        mtk = [sm_b[:, bo + 512 + i * 256:bo + 512 + (i + 1) * 256].rearrange("p (h d) -> p h d", d=128)
               for i in range(2)]
        wsT = sm_b[:, 12320:14368].rearrange("p (h i) -> p h i", i=128)
        sqy = sm_h[:, 7184:7696]
        xblk = [C.sqf[0], C.sqf[1], gl[0], gl[1]]
        xkey = [("sqf", 0), ("sqf", 1), ("gl", 0), ("gl", 1)]
        hblk = C.rs
        gqs = sm_h[:, 7700:7701]
        B.ts("dve", gqs, pp[:, 0:1], SCALE, 0.0, ALU.mult, ALU.add, ["pp"], ["gqs"])

        B.dma("pool", wsT, wsT_d.ap(), [], ["wsT"], "c2")
        B.dma("sp", gbt, gh_d["g_b"].ap(), [], ["gtab"], "c2")
        mts = cfg.get("mix_mts", [0, 4, 8, 12, 16])
        cnt = {"e": 0, "g": 0}
        for j0 in mts:
            W = min(4, NMIX - j0)
            N = W * 128
            mixT = mixT_f[:, :, 0:N]
            qT = qT_f[:, :, 0:N]
            for i in range(W):
                B.dma("sp", biga[:, i], hnT_s[j0 + 2 + i], [("hnT_s", j0 + 2 + i)], [("biga", i)], "ld2")
            hk = [("biga", i) for i in range(W)]
            pend = []
            tk = {"n": 0}

            def tick():
                tk["n"] += 1
                due = sorted([p for p in pend if p[0] <= tk["n"]], key=lambda p: p[0])
                for p in due:
                    pend.remove(p)
                    p[1]()

            def flush():
                for p in sorted(pend, key=lambda p: p[0]):
                    p[1]()
                pend.clear()

            def gm_E1(hp, t, bk, kb):
                b = t % 2
                G, kG = gl[b], ("gl", b)
                s4, ks4 = st4[b], ("st4", b)
                B.act(G, bk[:, :], AF.Gelu, [kb], [kG])
                B.tt("dve", sq, G[:, 256:512], G[:, 256:512], ALU.mult, [kG], ["sq"])
                B.red("dve", s4[:, 0:2], sq.rearrange("p (h d) -> p h d", d=128), ALU.add, ["sq"], [ks4])
                B.ts("dve", s4[:, 2:4], s4[:, 0:2], 1.0 / HD, EPS, ALU.mult, ALU.add, [ks4], [ks4])
                rstd_from_ms(B, s4[:, 2:4], s4[:, 4:6], s4[:, 6:8], [ks4])
                for hh in range(2):
                    B.stt("dve", vnb[b][:, hh, :], G[:, 256 + hh * 128:384 + hh * 128], s4[:, 6 + hh:7 + hh],
                          gvt[:, hh * 128:(hh + 1) * 128], ALU.mult, ALU.mult, [kG, ks4, "gvt"], [("vnb", b)])

            def gm_E2(hp, t):
                b = t % 2
                G, kG = gl[b], ("gl", b)
                s4, ks4 = st4[b], ("st4", b)
                mb, kmb = C.pf[5], ("pf", 5)
                for hh in range(2):
                    B.mm(mb[:, hh * 128:(hh + 1) * 128], wsT[:, hp * 2 + hh, :], vnb[b][:, hh, :], True, True,
                         ["wsT", ("vnb", b)], [kmb])
                for hh in range(2):
                    B.stt("dve", ao[b][:, hh, :], mb[:, hh * 128:(hh + 1) * 128], pp[:, 2 + hp * 2 + hh:3 + hp * 2 + hh],
                          G[:, hh * 128:(hh + 1) * 128], ALU.add, ALU.mult, [kmb, kG, "pp"], [("ao", b)])
                B.tt("dve", sq.rearrange("p (h d) -> p h d", d=128), ao[b], ao[b], ALU.mult, [("ao", b)], ["sq"])
                B.red("dve", s4[:, 0:2], sq.rearrange("p (h d) -> p h d", d=128), ALU.add, ["sq"], [ks4])
                B.ts("dve", s4[:, 2:4], s4[:, 0:2], 1.0 / HD, EPS, ALU.mult, ALU.add, [ks4], [ks4])
                rstd_from_ms(B, s4[:, 2:4], s4[:, 4:6], s4[:, 6:8], [ks4])
                for hh in range(2):
                    B.stt("dve", mtk[b][:, hh, :], ao[b][:, hh, :], s4[:, 6 + hh:7 + hh],
                          gat[:, hh * 128:(hh + 1) * 128], ALU.mult, ALU.mult, [("ao", b), ks4, "gat"], [("mtk", b)])

            def gm_E3(hp, t):
                b = t % 2
                pi = t % 2
                for hh in range(2):
                    B.tr(C.psb[pi][:, hh * 128:(hh + 1) * 128], mtk[b][:, hh, :], C.ident[:, :],
                         [("mtk", b), "ident"], [("psb", pi)])
                B.copy("act", mixT[:, hp * 2:hp * 2 + 2, t * 128:(t + 1) * 128],
                       C.psb[pi][:, 0:256].rearrange("p (k t) -> p k t", t=128), [("psb", pi)], ["mixT"])

            def ld_gv(hp):
                B.dma("sp", gvt, gh_d["g_v"][:, hp * 256:(hp + 1) * 256], [], ["gvt"], "ld2")

            def ld_ga(hp):
                B.dma("sp", gat, gh_d["g_a"][:, hp * 256:(hp + 1) * 256], [], ["gat"], "ld2")

            nhp = cfg.get("n_hp", 8)
            for hp in range(nhp):
                if hp == 0:
                    ld_gv(0)
                    ld_ga(0)
                bks = [(C.pf[t], ("pf", t)) for t in range(W)]
                for half in range(4):
                    pc, kpc = wpnext()
                    B.dma("pool", pc[:, :, 0:256], wsrc(w_in, half * 1024, 8, hp * 256, 256), [], [kpc], "w2")
                    B.dma("pool", pc[:, :, 256:512], wsrc(w_in, half * 1024, 8, 2048 + hp * 256, 256), [], [(kpc, "b")], "w2")
                    for t in range(W):
                        for kc in range(8):
                            B.mm(bks[t][0][:, :], biga[:, t, half * 8 + kc, :], pc[:, kc, :],
                                 half == 0 and kc == 0, half == 3 and kc == 7, [kpc, (kpc, "b"), hk[t]], [bks[t][1]])
                        if half == 3 and t == 0:
                            T0 = tk["n"] + 1
                            sched = [(0, "E1", 0), (1, "E1", 1), (2, "E2", 0), (3, "E1", 2), (3, "E2", 1), (4, "E1", 3),
                                     (6, "E3", 0), (7, "E3", 1), (7, "E2", 2), (8, "E2", 3), (11, "E3", 2), (12, "E3", 3)]
                            for dt, kind, tt_ in sched:
                                if tt_ >= W:
                                    continue
                                if kind == "E1":
                                    pend.append([T0 + dt, lambda hp=hp, t=tt_, bk=bks[tt_][0], kb=bks[tt_][1]: gm_E1(hp, t, bk, kb)])
                                elif kind == "E2":
                                    pend.append([T0 + dt, lambda hp=hp, t=tt_: gm_E2(hp, t)])
                                else:
                                    pend.append([T0 + dt, lambda hp=hp, t=tt_: gm_E3(hp, t)])
                            if hp + 1 < nhp:
                                pend.append([T0 + 5, lambda hp=hp: ld_gv(hp + 1)])
                                pend.append([T0 + 9, lambda hp=hp: ld_ga(hp + 1)])
                        tick()
            flush()
            def bm_load(h):
                b = h % 2
                for t in range(W):
                    B.dma("pool", bm_h[b][:, bm_off[b] + t * 640:bm_off[b] + (t + 1) * 640].rearrange("p (c q) -> p c q", q=128),
                          btab_d[cls_of(j0 + t), h], [], [("bm", b, t)], "w2")

            bm_load(0)
            bm_load(1)
            alias = [("gl", 0), ("gl", 1), "sq", ("st4", 0), ("st4", 1), ("ao", 0), ("ao", 1), "gvt", "gat"]
            qraw = [sm_h[:, 2048 + i * 512:2048 + (i + 1) * 512] for i in range(4)]
            sqb4 = [sm_b[:, i * 512:(i + 1) * 512] for i in range(4)]
            sqk = [("sqf", 0), ("sqf", 0), ("sqf", 1), ("sqf", 1)]

            def q_epi(q, h):
                ob, kob = C.pf[5], ("pf", 5)
                rsq, krs = C.rs[q % 2], ("rs", q % 2)
                B.mm(ob[:, 0:N], C.ones[:, :], sqb4[q][:, 0:N], True, True, [sqk[q], "ones"], [kob])
                B.ts("dve", rsq[:, 0:N], ob[:, 0:N], 1.0 / HD, EPS, ALU.mult, ALU.add, [kob], [krs])
                B.act(rsq[:, 0:N], rsq[:, 0:N], AF.Sqrt, [krs], [krs])
                B.s.add("dve", lambda e, o=rsq[:, 0:N], i=rsq[:, 0:N]: e.reciprocal(o, i), [krs], [krs])
                B.stt("dve", qT[:, h, :], qraw[q][:, 0:N], gqs, rsq[:, 0:N], ALU.mult, ALU.mult,
                      alias + [krs, "gqs"], ["qT"])

            for cg in range(cfg.get("n_qcg", 4)):
                bks = [bank() for _ in range(4)]
                for half in range(4):
                    pc, kpc = wpnext()
                    B.dma("pool", pc[:, :, :], wsrc(w_in, half * 1024, 8, 4096 + cg * 512, 512), [], [kpc, (kpc, "b")], "w2")
                    for q in range(4):
                        for kc in range(8):
                            B.mm(bks[q][0][:, 0:N].rearrange("p (w t) -> p w t", t=128),
                                 pc[:, kc, q * 128:(q + 1) * 128], biga[:, 0:W, half * 8 + kc, :],
                                 half == 0 and kc == 0, half == 3 and kc == 7, [kpc, (kpc, "b")] + hk, [bks[q][1]])
                        if half == 3:
                            bk, kb = bks[q]
                            B.act(sqb4[q][:, 0:N], bk[:, 0:N], AF.Square, [kb], [sqk[q]])
                            B.copy("act", qraw[q][:, 0:N], bk[:, 0:N], [kb], alias)
                            pend.append([tk["n"] + 3, lambda q=q, h=cg * 4 + q: q_epi(q, h)])
                        tick()
            flush()
            NM = W + 4
            nast = {"s": 0}

            def load_kv(g):
                sl = g % 2
                B.dma("sp", kmac[sl][:, :, 0:NM * 128],
                      kT_s[g * 4:(g + 1) * 4, :, j0 * 128:(j0 + NM) * 128].rearrange("h p t -> p h t"),
                      [("kT_s", hx, j0 + i) for hx in range(g * 4, g * 4 + 4) for i in range(NM)], kkeys[sl], "ld2")
                B.dma("sp", vmac[sl][:, 0:NM, :].rearrange("p m (h d) -> p m h d", d=129),
                      v_s[j0:j0 + NM, :, g * 4:(g + 1) * 4, :].rearrange("m p h d -> p m h d"),
                      [("v_s", j0 + i) for i in range(NM)], vkeys[sl], "ld2")

            def na_S(h):
                g, hh, sl, b = h // 4, h % 4, (h // 4) % 2, h % 2
                for m in range(NM):
                    t_lo, t_hi = max(0, m - 4), min(W - 1, m)
                    nt = t_hi - t_lo + 1
                    sbi = nast["s"] % 3
                    nast["s"] += 1
                    bk, kb = C.pf[sbi], ("pf", sbi)
                    B.mm(bk[:, 0:nt * 128], kmac[sl][:, hh, m * 128:(m + 1) * 128], qT[:, h, t_lo * 128:(t_hi + 1) * 128],
                         True, False, kkeys[sl] + ["qT"], [kb])
                    bap = bass.AP(bm_h[b], bm_off[b] + 128 * m + 512 * t_lo, [[bm_ps[b], 128], [512, nt], [1, 128]])
                    B.mm(bk[:, 0:nt * 128].rearrange("p (t q) -> p t q", q=128), C.ident[:, :], bap, False, True,
                         [("bm", b, tt_) for tt_ in range(W)] + ["ident"], [kb])
                    B.act(pTall[b][:, m, 0:nt * 128], bk[:, 0:nt * 128], AF.Exp, [kb], [("pTall", b)])

            def na_PV(h):
                g, hh, sl, b = h // 4, h % 4, (h // 4) % 2, h % 2
                pvb = []
                for t in range(W):
                    if t % 2 == 0:
                        pvb.append((C.pf[3 + t // 2], ("pf", 3 + t // 2)))
                    bk, kb = pvb[-1]
                    for i in range(5):
                        m = t + i
                        t_lo = max(0, m - 4)
                        B.mm(bk[:, (t % 2) * 129:(t % 2) * 129 + 129], pTall[b][:, m, (t - t_lo) * 128:(t - t_lo + 1) * 128],
                             vmac[sl][:, m, hh * 129:(hh + 1) * 129], i == 0, i == 4, [("pTall", b)] + vkeys[sl], [kb])
                for t in range(W):
                    bk, kb = pvb[t // 2]
                    c0 = (t % 2) * 129
                    B.s.add("dve", lambda e, o=nst[:, t:t + 1], i=bk[:, c0 + 128:c0 + 129]: e.reciprocal(o, i), [kb], ["nst"])
                    B.ts("dve", ysm[:, t, :], bk[:, c0:c0 + 128], nst[:, t:t + 1], 0.0, ALU.mult, ALU.add, [kb, "nst"], ["ysm"])
                yv = ysm[:, 0:W, :]
                B.tt("dve", sqy[:, 0:N].rearrange("p (t d) -> p t d", d=128), yv, yv, ALU.mult, ["ysm"], ["sqy"])
                B.red("dve", nst[:, 8:8 + W], sqy[:, 0:N].rearrange("p (t d) -> p t d", d=128), ALU.add, ["sqy"], ["nst"])
                B.ts("dve", nst[:, 16:16 + W], nst[:, 8:8 + W], 1.0 / HD, EPS, ALU.mult, ALU.add, ["nst"], ["nst"])
                rstd_from_ms(B, nst[:, 16:16 + W], nst[:, 24:24 + W], nst[:, 16:16 + W], ["nst"])
                for t in range(W):
                    B.stt("dve", mtok[b][:, t, :], ysm[:, t, :], nst[:, 16 + t:17 + t], gbt[:, h * 128:(h + 1) * 128],
                          ALU.mult, ALU.mult, ["ysm", "nst", "gtab"], [("mtok", b)])

            def na_T(h):
                b = h % 2
                pi = h % 2
                for t in range(W):
                    B.tr(C.psb[pi][:, t * 128:(t + 1) * 128], mtok[b][:, t, :], C.ident[:, :], [("mtok", b), "ident"],
                         [("psb", pi)])
                B.copy("act", mixT[:, 16 + h, :], C.psb[pi][:, 0:N], [("psb", pi)], ["mixT"])

            nah = cfg.get("n_nah", NH)
            pre = []
            for half in range(4):
                pc, kpc = wpnext()
                B.dma("pool", pc[:, :, :], wsrc(w_out, half * 1024, 8, 0, 512), [], [kpc, (kpc, "b")], "w2")
                pre.append((pc, kpc))
            load_kv(0)
            for k in range(nah + 2):
                if 1 <= k and k + 1 < nah:
                    bm_load(k + 1)
                if k < nah:
                    na_S(k)
                if 1 <= k <= nah:
                    na_PV(k - 1)
                if k >= 2:
                    na_T(k - 2)
                if k % 4 == 0 and k // 4 + 1 < (nah + 3) // 4:
                    load_kv(k // 4 + 1)
            for cb in range(cfg.get("n_cb", 8)):
                bks = [bank() for _ in range(W)]
                for t in range(W):
                    j = j0 + t
                    B.dma("sp", xblk[t], x_ext[(j + 2) * 128:(j + 3) * 128, cb * 512:(cb + 1) * 512], [],
                          [xkey[t]], "ld2")
                for half in range(4):
                    if cb == 0:
                        pc, kpc = pre[half]
                    else:
                        pc, kpc = wpnext()
                        B.dma("pool", pc[:, :, :], wsrc(w_out, half * 1024, 8, cb * 512, 512), [], [kpc, (kpc, "b")], "w2")
                    for t in range(W):
                        for kc in range(8):
                            B.mm(bks[t][0][:, :], mixT[:, half * 8 + kc, t * 128:(t + 1) * 128], pc[:, kc, :],
                                 half == 0 and kc == 0, half == 3 and kc == 7, [kpc, (kpc, "b"), "mixT"], [bks[t][1]])
                for t in range(W):
                    j = j0 + t
                    b = (cb * W + t) % 2
                    B.tt("dve", hblk[b], bks[t][0][:, :], xblk[t], ALU.add, [bks[t][1], xkey[t]], [("rs", b)])
                    B.dma("act", h1_s[j * 128:(j + 1) * 128, cb * 512:(cb + 1) * 512], hblk[b], [("rs", b)],
                          [("h1_s", j)], "st2")
        B.s.barrier()

    if 25 in phases:
        B.dma("sp", C.gtab[:, :], gt_d["g_ffn"].ap(), [], ["gtab"], "c25")

        def after25(t):
            B.dma("act", hn2T_s[t], n_stg[t % 2], [("n_stg", t % 2)], [("hn2T_s", t)], "st25")
        C.after_tile = after25
        phase_norm_T(B, C, "p25", lambda t: h1_s[t * 128:(t + 1) * 128, :], lambda t: ("h1_s", t),
                     cfg.get("nt25", NMIX), lambda t, g: (n_stg[t % 2][:, g * 8:(g + 1) * 8, :], ("n_stg", t % 2)))
        C.after_tile = None
        B.s.barrier()

    if 3 in phases:
        state["nwp"] = 6
        gT = gt_h[:, :].rearrange("p (k t) -> p k t", t=512)
        uext = [sm_h[:, i * 520:i * 520 + 514] for i in range(2)]
        cacc = [sm_h[:, 1040 + i * 512:1040 + (i + 1) * 512] for i in range(2)]
        glb = sm_h[:, 2064:2576]
        h1b = [sm_h[:, 4736 + i * 512:4736 + (i + 1) * 512] for i in range(4)]
        h2b = [sm_h[:, 2576 + i * 512:2576 + (i + 1) * 512] for i in range(4)]
        hcol = sm_b[:, 9400:9464].rearrange("p (k t) -> p k t", t=2)
        hst = [sm_h[:, 6784:6792], sm_h[:, 6792:6800]]
        for w in cfg.get("ffn_wins", range(4)):
            for i in range(4):
                B.dma("sp", biga[:, i], hn2T_s[4 * w + 1 + i], [("hn2T_s", 4 * w + 1 + i)], [("biga", i)], "ld3")
            hk = [("biga", i) for i in range(4)]
            for side, jn, col in ((0, 4 * w, 127), (1, 4 * w + 5, 0)):
                pc, kpc = wpnext()
                nbv = pc[:, :, :].rearrange("p k c -> p (k c)").rearrange("p (k t) -> p k t", t=128)
                B.dma("sp", nbv, hn2T_s[jn], [("hn2T_s", jn)], [kpc, (kpc, "b")], "ld3")
                B.copy("dve", hcol[:, :, side], nbv[:, :, col], [kpc, (kpc, "b")], ["hcol"])
            mcol8 = pp[:, PP_MASK + 8 * w:PP_MASK + 8 * w + 8]
            for fg in range(cfg.get("n_fg", 43)):
                bks = [bank() for _ in range(4)]
                hb, khb = C.pf[5], ("pf", 5)
                for half in range(4):
                    pc, kpc = wpnext()
                    B.dma("pool", pc[:, :, 0:256], wsrc(w_up, half * 1024, 8, fg * 256, 256), [], [kpc], "w3")
                    B.dma("pool", pc[:, :, 256:512], wsrc(w_up, half * 1024, 8, DFF + fg * 256, 256), [], [(kpc, "b")], "w3")
                    for q in range(4):
                        for kc in range(8):
                            st_, sp_ = (half == 0 and kc == 0), (half == 3 and kc == 7)
                            B.mm(bks[q][0][:, :].rearrange("p (w t) -> p w t", t=128), pc[:, kc, q * 128:(q + 1) * 128],
                                 biga[:, :, half * 8 + kc, :], st_, sp_, [kpc, (kpc, "b")] + hk, [bks[q][1]])
                            B.mm(hb[:, 2 * q:2 * q + 2], pc[:, kc, q * 128:(q + 1) * 128], hcol[:, half * 8 + kc, :],
                                 st_ and q == 0, sp_ and q == 3, [kpc, (kpc, "b"), "hcol"], [khb])
                hs = hst[fg % 2]
                B.tt("dve", hs, hb[:, 0:8], mcol8, ALU.mult, [khb, "pp"], [("hst", fg % 2)])
                for c in range(2):
                    fc = 2 * fg + c
                    for part in range(2):
                        q = part * 2 + c
                        bk, kb = bks[q]
                        ch = fc + part * NFC
                        cp = pp[:, PP_CPAR + ch * 4:PP_CPAR + ch * 4 + 4]
                        ue, kue = uext[part], ("uext", part)
                        ca, kca = cacc[part], ("cacc", part)
                        B.copy("act", ue[:, 1:513], bk[:, :], [kb], [kue])
                        B.copy("act", ue[:, 0:514:513], hs[:, 2 * q:2 * q + 2], [("hst", fg % 2)], [kue])
                        B.act(ca, bk[:, :], AF.Identity, [kb, "pp"], [kca], bias=cp[:, 3:4], scale=cp[:, 1:2])
                        B.stt("dve", ca, ue[:, 0:512], cp[:, 0:1], ca, ALU.mult, ALU.add, [kue, kca, "pp"], [kca])
                        B.stt("dve", ca, ue[:, 2:514], cp[:, 2:3], ca, ALU.mult, ALU.add, [kue, kca, "pp"], [kca])
                    B.act(glb, cacc[0], AF.Gelu, [("cacc", 0)], ["glb"])
                    B.tt("dve", gT[:, fc, :], glb, cacc[1], ALU.mult, ["glb", ("cacc", 1)], [("gT", fc)])
            nfc = cfg.get("n_fg", 43) * 2
            gk = [("gT", fc) for fc in range(nfc)]
            for cb in range(cfg.get("n_cb3", 8)):
                bks = [bank() for _ in range(4)]
                for t in range(4):
                    jm = 4 * w + 1 + t
                    B.dma("sp", h1b[t], h1_s[jm * 128:(jm + 1) * 128, cb * 512:(cb + 1) * 512], [("h1_s", jm)],
                          [("h1b", t)], "ld3")
                npc = (nfc + 7) // 8
                for pk in range(npc):
                    nk = min(8, nfc - pk * 8)
                    pc, kpc = wpnext()
                    B.dma("pool", pc[:, 0:nk, :], wsrc(w_down, pk * 1024, nk, cb * 512, 512), [], [kpc, (kpc, "b")], "w3")
                    for t in range(4):
                        for k in range(nk):
                            kg = pk * 8 + k
                            B.mm(bks[t][0][:, :], gT[:, kg, t * 128:(t + 1) * 128], pc[:, k, :], kg == 0, kg == nfc - 1,
                                 [kpc, (kpc, "b"), ("gT", kg)], [bks[t][1]])
                for t in range(4):
                    B.tt("dve", h2b[t], bks[t][0][:, :], h1b[t], ALU.add, [bks[t][1], ("h1b", t)], [("h2b", t)])
                    jo = 4 * w + t
                    B.dma("act", h2_s[jo * 128:(jo + 1) * 128, cb * 512:(cb + 1) * 512], h2b[t], [("h2b", t)],
                          [("h2_s", jo)], "st3")
        B.s.barrier()

    if 4 in phases:
        state["nwp"] = 4
        wps = gt_h[:, 24576:32768].rearrange("p (k c) -> p k c", c=D)
        pin = [gt_f[:, 16384 + i * 256:16384 + (i + 1) * 256] for i in range(2)]
        pbf = gt_h[:, 33792:34048]
        pTt = gt_h[:, 34048:35072].rearrange("p (t k q) -> p t k q", t=4, k=2)
        junk = gt_h[:, 35072:35584]
        ssq = sm_h[:, 0:32].rearrange("p (t c) -> p t c", c=8)
        rse = sm_h[:, 32:48]
        sgb = [sm_h[:, 64 + i * 512:64 + (i + 1) * 512] for i in range(2)] + \
              [sm_h[:, 5184 + i * 512:5184 + (i + 1) * 512] for i in range(2)]
        tmb = [sm_h[:, 1088 + i * 512:1088 + (i + 1) * 512] for i in range(2)] + \
              [sm_h[:, 6208 + i * 512:6208 + (i + 1) * 512] for i in range(2)]
        gpb = [sm_h[:, 2112 + i * 512:2112 + (i + 1) * 512] for i in range(2)]
        h2b = [sm_h[:, 3136 + i * 512:3136 + (i + 1) * 512] for i in range(4)]
        B.dma("sp", C.gtab[:, :], gt_d["g_ple"].ap(), [], ["gtab"], "c4")
        B.dma("pool", wps, w_p.ap().rearrange("(k p) c -> p k c", p=128), [], ["wps"], "w4")
        for w in cfg.get("ple_wins", range(4)):
            phase_norm_T(B, C, "p4", lambda t: h2_s[(4 * w + t) * 128:(4 * w + t + 1) * 128, :],
                         lambda t: ("h2_s", 4 * w + t), 4,
                         lambda t, g: (biga[:, t, g * 8:(g + 1) * 8, :], ("biga", t)))
            for t in range(4):
                jo = 4 * w + t
                b = t % 2
                B.dma("sp", pin[b], p_own[jo * 128:(jo + 1) * 128, :], [], [("pin", b)], "ld4")
                B.copy("dve", pbf, pin[b], [("pin", b)], ["pbf"])
                for k in range(2):
                    B.tr(C.psb[0][:, k * 128:(k + 1) * 128], pbf[:, k * 128:(k + 1) * 128], C.ident[:, :],
                         ["pbf", "ident"], [("psb", 0)])
                B.copy("act", pTt[:, t], C.psb[0][:, 0:256].rearrange("p (k q) -> p k q", q=128), [("psb", 0)],
                       [("pTt", t)])
                for cb in range(8):
                    bk, kb = bank()
                    for k in range(2):
                        B.mm(bk[:, :], pTt[:, t, k, :], wps[:, k, cb * 512:(cb + 1) * 512], k == 0, k == 1,
                             [("pTt", t), "wps"], [kb])
                    B.act(junk, bk[:, :], AF.Square, [kb], ["junk", ("ssq", t)], accum=ssq[:, t, cb:cb + 1])
                r4 = rse[:, t * 4:(t + 1) * 4]
                B.red("dve", r4[:, 0:1], ssq[:, t, :], ALU.add, [("ssq", t)], [("rse", t)])
                B.ts("dve", r4[:, 1:2], r4[:, 0:1], 1.0 / D, EPS, ALU.mult, ALU.add, [("rse", t)], [("rse", t)])
                rstd_from_ms(B, r4[:, 1:2], r4[:, 3:4], r4[:, 2:3], [("rse", t)])
            hk = [("biga", i) for i in range(4)]
            for cb in range(8):
                b = cb % 2
                B.dma("sp", gpb[b], gt_d["g_post"][:, cb * 512:(cb + 1) * 512], [], [("gpb", b)], "ld4")
                bks = [(C.pf[t], ("pf", t)) for t in range(4)]
                for t in range(4):
                    jo = 4 * w + t
                    B.dma("sp", h2b[t], h2_s[jo * 128:(jo + 1) * 128, cb * 512:(cb + 1) * 512], [("h2_s", jo)],
                          [("h2b", t)], "ld4")
                for half in range(4):
                    pc, kpc = wpnext()
                    B.dma("pool", pc[:, :, :], wsrc(w_gate, half * 1024, 8, cb * 512, 512), [], [kpc, (kpc, "b")], "w4")
                    for t in range(4):
                        for kc in range(8):
                            B.mm(bks[t][0][:, :], biga[:, t, half * 8 + kc, :], pc[:, kc, :],
                                 half == 0 and kc == 0, half == 3 and kc == 7, [kpc, (kpc, "b"), hk[t]], [bks[t][1]])
                for t in range(4):
                    jo = 4 * w + t
                    bb = t
                    B.act(sgb[bb], bks[t][0][:, :], AF.Sigmoid, [bks[t][1]], [("sgb", bb)])
                    eb, keb = C.pf[4 + t % 2], ("pf", 4 + t % 2)
                    for k in range(2):
                        B.mm(eb[:, :], pTt[:, t, k, :], wps[:, k, cb * 512:(cb + 1) * 512], k == 0, k == 1,
                             [("pTt", t), "wps"], [keb])
                    r4 = rse[:, t * 4:(t + 1) * 4]
                    B.stt("dve", tmb[bb], eb[:, :], r4[:, 2:3], gpb[b], ALU.mult, ALU.mult,
                          [keb, ("rse", t), ("gpb", b)], [("tmb", bb)])
                    B.tt("dve", tmb[bb], tmb[bb], sgb[bb], ALU.mult, [("tmb", bb), ("sgb", bb)], [("tmb", bb)])
                    B.tt("dve", tmb[bb], tmb[bb], h2b[t], ALU.add, [("tmb", bb), ("h2b", t)], [("tmb", bb)])
                    B.dma("sp", out_d[jo * 128:(jo + 1) * 128, cb * 512:(cb + 1) * 512], tmb[bb], [("tmb", bb)],
                          [("out", jo, cb)], "st4")
    else:
        B.dma("sp", out_d[0:128, 0:512], sm_h[:, 0:512], [], [("out", 0)], "st4")
    B.s.emit(nc, stack)
    return B


def _btab(rpb, c):
    out = np.full((5, NH, 128, 5, 128), MASKVAL, np.float32)
    rep = {0: 1, 1: 2, 2: 15, 3: 16, 4: 8}
    qi = np.arange(128)
    ks = np.arange(576)
    for cls, j in rep.items():
        r0 = 32 * c + 2 * (j - 1)
        if cls == 4:
            r0 = 64
            real_lo = r0 - 4
            slot_rows = real_lo + np.arange(9)
            actual = slot_rows.copy()
            mirrored = np.zeros(9, bool)
        else:
            slot_rows = (r0 - 4) + np.arange(9)
            actual = slot_rows.copy()
            mirrored = np.zeros(9, bool)
            for s in range(9):
                g = slot_rows[s]
                if g < 0:
                    actual[s] = 8 + g if g >= -4 else -1000
                    mirrored[s] = True
                elif g >= 128:
                    actual[s] = g - 8 if g <= 129 else -1000
                    mirrored[s] = True
        real_set = set(int(a) for a, m in zip(actual, mirrored) if not m)
        qrow = np.clip(r0 + qi // 64, 0, 127)
        qcol = qi % 64
        rs = np.clip(qrow - 4, 0, 120)
        cs = np.clip(qcol - 8, 0, 48)
        s_of = ks // 64
        ck = ks % 64
        ar = actual[s_of]
        ok_slot = np.array([(not mirrored[s]) or (int(actual[s]) not in real_set) for s in range(9)])[s_of]
        valid = (ok_slot[:, None] & (ar[:, None] >= rs[None, :]) & (ar[:, None] < rs[None, :] + 8)
                 & (ck[:, None] >= cs[None, :]) & (ck[:, None] < cs[None, :] + 16))
        dr = np.clip(ar[:, None] - qrow[None, :] + 7, 0, 14)
        dc = np.clip(ck[:, None] - qcol[None, :] + 15, 0, 30)
        vals = rpb[:, dr, dc]
        tab = np.where(valid[None], vals, np.float32(MASKVAL)).astype(np.float32)
        full = np.full((NH, 640, 128), MASKVAL, np.float32)
        full[:, :576] = tab
        out[cls] = full.reshape(NH, 5, 128, 128).transpose(0, 2, 1, 3)
    return out


def prep_core(inp, cid):
    b, c = cid // 4, cid % 4
    x = inp["x"][b]
    xe = np.zeros((NEXT * 128, D), np.float32)
    for er in range(44):
        g = 32 * c - 6 + er
        if g < 0:
            g = 8 + g if g >= -4 else None
        elif g >= 128:
            g = g - 8 if g <= 129 else None
        if g is not None:
            xe[er * 64:(er + 1) * 64] = x[g * 64:(g + 1) * 64]
    t0 = 2048 * c
    bc = lambda v, n: np.ascontiguousarray(np.broadcast_to(np.asarray(v, np.float32).reshape(1, n), (128, n)))
    pp = np.zeros((128, NPP), np.float32)
    pp[:, 0] = inp["q_norm_g"][0]
    pp[:, 1] = inp["k_norm_g"][0]
    pp[:, 2:18] = inp["gmlp_bs"][0].T
    cw = inp["conv_w"][0].reshape(3, 172, 128)
    cb = inp["conv_b"][0].reshape(172, 128)
    cp = np.concatenate([cw.transpose(2, 1, 0), cb.T[:, :, None]], axis=2)
    pp[:, PP_CPAR:PP_CPAR + 688] = cp.reshape(128, 688)
    m = np.ones((4, 2), np.float32)
    m[0, 0] = 0.0 if c == 0 else 1.0
    m[3, 1] = 0.0 if c == 3 else 1.0
    pp[:, PP_MASK:PP_MASK + 32] = np.tile(m[:, None, :], (1, 4, 1)).reshape(1, 32)
    return {
        "x_ext": xe,
        "p_own": np.ascontiguousarray(inp["p"][0, b, t0:t0 + NTOK]),
        "g_mix": bc(inp["norm_mix_g"][0], D), "g_ffn": bc(inp["norm_ffn_g"][0], D),
        "g_ple": bc(inp["norm_ple_g"][0], D), "g_post": bc(inp["ple_post_g"][0], D),
        "g_v": bc(inp["gmlp_v_g"][0].reshape(-1), 2048), "g_a": bc(inp["out_norm_a_g"][0].reshape(-1), 2048),
        "g_b": bc(inp["out_norm_b_g"][0].reshape(-1), 2048),
        "pp": pp,
        "wsT": np.ascontiguousarray(inp["gmlp_ws"][0].transpose(2, 0, 1)),
        "btab": _btab(np.asarray(inp["na_rpb"][0], np.float32), c),
        "ident": np.eye(128, dtype=np.float32),
        "w_in": inp["w_in"][0], "w_out": inp["w_out"][0], "w_up": inp["w_up"][0], "w_down": inp["w_down"][0],
        "w_gate": inp["w_ple_gate"][0], "w_p": inp["w_ple_proj"][0],
    }


def kernel(**inputs):
    from contextlib import ExitStack
    inp = {k: np.asarray(v) for k, v in inputs.items()}
    nc = bass.Bass("TRN2", target_bir_lowering=False)
    with ExitStack() as stack:
        build(nc, stack, {})
    in_maps = [prep_core(inp, cid) for cid in range(8)]
    res = run_bass_kernel_spmd(nc, in_maps, core_ids=list(range(8)))
    out = np.zeros((2, 8192, D), np.float32)
    for cid in range(8):
        b, c = cid // 4, cid % 4
        out[b, 2048 * c:2048 * (c + 1)] = np.asarray(res.results[cid]["out"])
    return out
```
